# Optimizing a Trainium2 kernel written in Bass

```python
import jax, jax.numpy as jnp
from jax import lax
import numpy as np

D_MODEL = 1024
BATCH = 4
SEQ = 4096
DEPTH = 4

GRID_W = 64
CTX_LEN = 256
CHUNK = 128
A_GROUPS = 4
A_W = D_MODEL // 2
A_GROUP_DIM = A_W // A_GROUPS
B_HEAD_DIM = 64
B_W = D_MODEL // 2
B_HEADS = B_W // B_HEAD_DIM
WIN_H = 8
WIN_W = 16
MIX_IN = 2 * A_W + 3 * B_W
MIX_OUT = A_W + B_W
CONV_K = 31
CONV_W = D_MODEL
FFN = 4 * D_MODEL
N_EVEN = (DEPTH + 1) // 2
N_ODD = DEPTH // 2
EPS = 1e-6
NEG = -1e30

kernel_name = "hybrid_gmlp_natten_conformer_dit"


def rmsnorm(x, g):
    x32 = x.astype(jnp.float32)
    y = x32 * lax.rsqrt(jnp.mean(x32 * x32, axis=-1, keepdims=True) + EPS)
    return (y * g.astype(jnp.float32)).astype(x.dtype)


def layernorm(x, g, b=None):
    x32 = x.astype(jnp.float32)
    mu = jnp.mean(x32, axis=-1, keepdims=True)
    var = jnp.mean(jnp.square(x32 - mu), axis=-1, keepdims=True)
    y = (x32 - mu) * lax.rsqrt(var + EPS) * g.astype(jnp.float32)
    if b is not None:
        y = y + b.astype(jnp.float32)
    return y.astype(x.dtype)


def modulate(h, shift, scale):
    return h * (1 + scale) + shift


def split_heads(t):
    return t.reshape(*t.shape[:-1], B_HEADS, B_HEAD_DIM)


def spatial_gating(z, ln_v_g, w_sp, b_sp):
    u, g = jnp.split(jax.nn.gelu(z), 2, axis=-1)
    bsz, length, _ = g.shape
    g = g.reshape(bsz, length // CHUNK, CHUNK, A_GROUPS, A_GROUP_DIM)
    g = layernorm(g, ln_v_g.reshape(A_GROUPS, A_GROUP_DIM))
    s = jnp.einsum('gpq,bnqgc->bnpgc', w_sp, g) + b_sp.T[:, :, None]
    return u * s.reshape(bsz, length, A_W)


def neighborhood_attention(q, k, v, k_ctx, v_ctx, rpb):
    bsz, seq, nh, dh = q.shape
    rows_n = seq // GRID_W
    kh = min(WIN_H, rows_n)
    kw = WIN_W
    qg = q.reshape(bsz, rows_n, GRID_W, nh, dh)
    kg = k.reshape(bsz, rows_n, GRID_W, nh, dh)
    vg = v.reshape(bsz, rows_n, GRID_W, nh, dh)
    rows = jnp.arange(rows_n)
    row_idx = jnp.clip(rows - kh // 2, 0, rows_n - kh)[:, None] + jnp.arange(kh)
    k_band = kg[:, row_idx]
    v_band = vg[:, row_idx]
    cols = jnp.arange(GRID_W)
    col_start = jnp.clip(cols - kw // 2, 0, GRID_W - kw)
    col_mask = (cols[None, :] >= col_start[:, None]) & (cols[None, :] < col_start[:, None] + kw)
    dr = row_idx - rows[:, None] + WIN_H - 1
    dc = jnp.clip(cols[None, :] - cols[:, None] + WIN_W - 1, 0, 2 * WIN_W - 2)
    bias = rpb[:, dr[:, None, :, None], dc[None, :, None, :]].astype(jnp.float32)
    scale = dh ** -0.5
    s_loc = jnp.einsum('brqhd,brkwhd->bhrqkw', qg, k_band).astype(jnp.float32) * scale + bias[None]
    s_loc = jnp.where(col_mask[:, None, :], s_loc, NEG)
    s_ctx = jnp.einsum('brqhd,bchd->bhrqc', qg, k_ctx).astype(jnp.float32) * scale
    n_loc = kh * GRID_W
    s = jnp.concatenate([s_loc.reshape(bsz, nh, rows_n, GRID_W, n_loc), s_ctx], axis=-1)
    p = jax.nn.softmax(s, axis=-1).astype(v.dtype)
    p_loc = p[..., :n_loc].reshape(bsz, nh, rows_n, GRID_W, kh, GRID_W)
    p_ctx = p[..., n_loc:]
    out = (jnp.einsum('bhrqkw,brkwhd->brqhd', p_loc, v_band)
           + jnp.einsum('bhrqc,bchd->brqhd', p_ctx, v_ctx))
    return out.reshape(bsz, seq, nh * dh)


def context_attention(q, k, v):
    s = jnp.einsum('bqhd,bkhd->bhqk', q, k).astype(jnp.float32) * (q.shape[-1] ** -0.5)
    p = jax.nn.softmax(s, axis=-1).astype(v.dtype)
    out = jnp.einsum('bhqk,bkhd->bqhd', p, v)
    return out.reshape(*out.shape[:2], -1)


def even_mixer(h, hc, w_in, w_out, ln_v_g, w_sp, b_sp, rpb, ctx_out):
    cuts = [2 * A_W, 2 * A_W + B_W, 2 * A_W + 2 * B_W]
    if ctx_out:
        zc, qc, kc, vc = jnp.split(hc @ w_in, cuts, axis=-1)
    else:
        kc, vc = jnp.split(hc @ w_in[:, cuts[1]:], 2, axis=-1)
    kc, vc = split_heads(kc), split_heads(vc)
    z, q, k, v = jnp.split(h @ w_in, cuts, axis=-1)
    a = spatial_gating(z, ln_v_g, w_sp, b_sp)
    b = neighborhood_attention(split_heads(q), split_heads(k), split_heads(v), kc, vc, rpb)
    y = jnp.concatenate([a, b], axis=-1) @ w_out
    yc = None
    if ctx_out:
        ac = spatial_gating(zc, ln_v_g, w_sp, b_sp)
        bc = context_attention(split_heads(qc), kc, vc)
        yc = jnp.concatenate([ac, bc], axis=-1) @ w_out
    return y, yc


def conformer_conv(h, w_pw1, b_pw1, w_dw, b_dw, ln_g, ln_b, w_pw2, b_pw2):
    a, gate = jnp.split(h @ w_pw1 + b_pw1, 2, axis=-1)
    y = a * jax.nn.sigmoid(gate)
    y = lax.conv_general_dilated(y, w_dw[:, None, :], window_strides=(1,),
                                 padding=[(CONV_K // 2, CONV_K // 2)],
                                 dimension_numbers=('NWC', 'WIO', 'NWC'),
                                 feature_group_count=CONV_W) + b_dw
    y = jax.nn.silu(layernorm(y, ln_g, ln_b))
    return y @ w_pw2 + b_pw2


def sq_relu_mlp(h, w1, w2):
    return jnp.square(jax.nn.relu(h @ w1)) @ w2


def setup_inputs(seed: int = 0) -> dict:
    key = jax.random.key(seed)
    ks = jax.random.split(key, 24)
    f32 = jnp.float32
    nrm = lambda k, shape, s: jax.random.normal(k, shape, f32) * s
    D = D_MODEL
    return {
        "x": nrm(ks[0], (BATCH, SEQ, D), 1.0),
        "c": nrm(ks[1], (BATCH, D), 1.0),
        "ctx": nrm(ks[2], (BATCH, CTX_LEN, D), 1.0),
        "c_ctx": nrm(ks[3], (D,), 1.0),
        "w_mod": nrm(ks[4], (DEPTH, D, 6 * D), 0.5 * D ** -0.5),
        "b_mod": nrm(ks[5], (DEPTH, 6 * D), 0.02),
        "norm_g": 1.0 + nrm(ks[6], (DEPTH, 2, D), 0.05),
        "w_in": nrm(ks[7], (N_EVEN, D, MIX_IN), D ** -0.5),
        "w_out": nrm(ks[8], (N_EVEN, MIX_OUT, D), MIX_OUT ** -0.5),
        "ln_v_g": 1.0 + nrm(ks[9], (N_EVEN, A_W), 0.05),
        "w_sp": nrm(ks[10], (N_EVEN, A_GROUPS, CHUNK, CHUNK), CHUNK ** -0.5),
        "b_sp": 1.0 + nrm(ks[11], (N_EVEN, A_GROUPS, CHUNK), 0.1),
        "rpb": nrm(ks[12], (N_EVEN, B_HEADS, 2 * WIN_H - 1, 2 * WIN_W - 1), 0.1),
        "w_pw1": nrm(ks[13], (N_ODD, D, 2 * CONV_W), D ** -0.5),
        "b_pw1": nrm(ks[14], (N_ODD, 2 * CONV_W), 0.02),
        "w_dw": nrm(ks[15], (N_ODD, CONV_K, CONV_W), CONV_K ** -0.5),
        "b_dw": nrm(ks[16], (N_ODD, CONV_W), 0.02),
        "ln_c_g": 1.0 + nrm(ks[17], (N_ODD, CONV_W), 0.05),
        "ln_c_b": nrm(ks[18], (N_ODD, CONV_W), 0.02),
        "w_pw2": nrm(ks[19], (N_ODD, CONV_W, D), CONV_W ** -0.5),
        "b_pw2": nrm(ks[20], (N_ODD, D), 0.02),
        "w_ff1": nrm(ks[21], (DEPTH, D, FFN), D ** -0.5),
        "w_ff2": nrm(ks[22], (DEPTH, FFN, D), FFN ** -0.5),
        "final_g": 1.0 + nrm(ks[23], (D,), 0.05),
    }


def reference(x, c, ctx, c_ctx, w_mod, b_mod, norm_g, w_in, w_out, ln_v_g, w_sp, b_sp, rpb,
              w_pw1, b_pw1, w_dw, b_dw, ln_c_g, ln_c_b, w_pw2, b_pw2, w_ff1, w_ff2, final_g):
    last_reader = ((DEPTH - 1) // 2) * 2
    sc_c = jax.nn.silu(c)
    sc_ctx = jax.nn.silu(c_ctx)
    for l in range(DEPTH):
        m = (sc_c @ w_mod[l] + b_mod[l])[:, None, :]
        sh1, s1, g1, sh2, s2, g2 = jnp.split(m, 6, axis=-1)
        mc = sc_ctx @ w_mod[l] + b_mod[l]
        csh1, cs1, cg1, csh2, cs2, cg2 = jnp.split(mc, 6, axis=-1)
        ctx_out = l < last_reader
        h = modulate(rmsnorm(x, norm_g[l, 0]), sh1, s1)
        if l % 2 == 0:
            i = l // 2
            hc = modulate(rmsnorm(ctx, norm_g[l, 0]), csh1, cs1)
            y, yc = even_mixer(h, hc, w_in[i], w_out[i], ln_v_g[i], w_sp[i], b_sp[i], rpb[i], ctx_out)
        else:
            i = l // 2
            conv = lambda t: conformer_conv(t, w_pw1[i], b_pw1[i], w_dw[i], b_dw[i],
                                            ln_c_g[i], ln_c_b[i], w_pw2[i], b_pw2[i])
            y = conv(h)
            yc = conv(modulate(rmsnorm(ctx, norm_g[l, 0]), csh1, cs1)) if ctx_out else None
        x = x + g1 * y
        x = x + g2 * sq_relu_mlp(modulate(rmsnorm(x, norm_g[l, 1]), sh2, s2), w_ff1[l], w_ff2[l])
        if ctx_out:
            ctx = ctx + cg1 * yc
            ctx = ctx + cg2 * sq_relu_mlp(modulate(rmsnorm(ctx, norm_g[l, 1]), csh2, cs2), w_ff1[l], w_ff2[l])
    return rmsnorm(x, final_g)
```

```python
import numpy as np
import concourse.bass as bass
import concourse.mybir as mybir
from concourse.bass_utils import run_bass_kernel_spmd
from contextlib import ExitStack

F32 = mybir.dt.float32
BF16 = mybir.dt.bfloat16
AF = mybir.ActivationFunctionType
ALU = mybir.AluOpType
AX = mybir.AxisListType

D = 1024
KC = 8
NMAIN = 2432
CTXO = 2432
NXR = 2688
NWIN = 2688
EPS = 1e-6
SELF_SYNC = True
DMA_RING = 8


class Inst:
    __slots__ = ("eng", "fn", "deps", "idx", "sig", "key", "val", "dma", "waits")


class T:
    __slots__ = ("name", "w", "rs", "init")

    def __init__(self, name, init=()):
        self.name = name
        self.w = None
        self.rs = []
        self.init = list(init)


class Prog:
    ENGS = ("pe", "act", "dve", "pool", "sp")

    def __init__(self):
        self.q = {e: [] for e in self.ENGS}
        self.ndma = {e: 0 for e in self.ENGS}
        self.dhist = {e: [] for e in self.ENGS}

    def _add(self, eng, fn, r, w, dma):
        ins = Inst()
        ins.eng = eng
        ins.fn = fn
        ins.dma = dma
        ins.sig = False
        ins.key = None
        ins.val = 0
        deps = []
        for t in r:
            if t.w is not None:
                deps.append(t.w)
            elif t.init:
                deps.extend(t.init)
        for t in w:
            if t.w is not None:
                deps.append(t.w)
            if t.init:
                deps.extend(t.init)
                t.init = []
            deps.extend(t.rs)
        for t in w:
            t.w = ins
            t.rs = []
        for t in r:
            if t.w is not ins:
                t.rs.append(ins)
        if dma:
            m = self.ndma[eng]
            self.ndma[eng] = m + 1
            ins.key = ("d", eng, m % DMA_RING)
            ins.val = 16 * (m // DMA_RING + 1)
            if m >= DMA_RING:
                deps.append(self.dhist[eng][m - DMA_RING])
            self.dhist[eng].append(ins)
        ins.deps = deps
        self.q[eng].append(ins)
        return ins

    def op(self, eng, fn, r=(), w=()):
        return self._add(eng, fn, r, w, False)

    def dma(self, eng, fn, r=(), w=()):
        return self._add(eng, fn, r, w, True)

    def finalize(self):
        for e, lst in self.q.items():
            for i, ins in enumerate(lst):
                ins.idx = i
        for e, lst in self.q.items():
            for ins in lst:
                best = {}
                dm = {}
                for d in ins.deps:
                    if d is ins:
                        continue
                    if d.dma:
                        dm[id(d)] = d
                    else:
                        if d.eng == e and (e == "pe" or not SELF_SYNC):
                            continue
                        b = best.get(d.eng)
                        if b is None or b.idx < d.idx:
                            best[d.eng] = d
                ins.deps = list(best.values()) + list(dm.values())
                for d in best.values():
                    d.sig = True
        for e in self.ENGS:
            cnt = 0
            for ins in self.q[e]:
                if ins.dma:
                    continue
                if ins.sig:
                    cnt += 1
                    ins.key = ("c", e)
                    ins.val = cnt
        for e, lst in self.q.items():
            known = {}
            for ins in lst:
                waits = []
                for d in ins.deps:
                    if known.get(d.key, 0) >= d.val:
                        continue
                    known[d.key] = d.val
                    waits.append((d.key, d.val))
                ins.waits = waits

    def emit_engine(self, eng, e, sems):
        for ins in self.q[eng]:
            for key, val in ins.waits:
                e.wait_ge(sems[key], val)
            r = ins.fn(e)
            if ins.dma:
                r.then_inc(sems[ins.key], 16)
            elif ins.sig:
                r.then_inc(sems[ins.key], 1)


class Buf:
    def __init__(self, ap, ts):
        self.ap = ap
        self.ts = ts

    def t(self, i=0):
        return self.ts[i]


class Arena:
    def __init__(self, tensor, nbytes):
        self.tensor = tensor
        self.nbytes = nbytes
        self.live = []

    def _track(self, off, nbytes, name):
        t = T(name)
        init = []
        keep = []
        end = off + nbytes
        for (o, e, old) in self.live:
            if o < end and off < e:
                if old.w is not None:
                    init.append(old.w)
                init.extend(old.rs)
                init.extend(old.init)
                if not (off <= o and e <= end):
                    keep.append((o, e, old))
            else:
                keep.append((o, e, old))
        t.init = init
        keep.append((off, end, t))
        self.live = keep
        return t

    def split(self, parent, names):
        out = []
        keep = []
        rng = None
        for (o, e, t) in self.live:
            if t is parent:
                rng = (o, e)
            else:
                keep.append((o, e, t))
        assert rng is not None
        for nm in names:
            t = T(nm, init=parent.init)
            keep.append((rng[0], rng[1], t))
            out.append(t)
        self.live = keep
        return out

    def buf(self, off, shape, dtype, name, nt=1):
        esz = 4 if dtype == F32 else 2
        n = int(np.prod(shape))
        nbytes = n * esz
        assert off % 4 == 0 and off + nbytes <= self.nbytes, (name, off, nbytes, self.nbytes)
        ap = self.tensor[:, off // 2: off // 2 + nbytes // 2]
        if dtype == F32:
            ap = ap.bitcast(F32)
        if len(shape) == 2:
            ap = ap.rearrange("p (a b) -> p a b", a=shape[0])
        elif len(shape) == 3:
            ap = ap.rearrange("p (a b c) -> p a b c", a=shape[0], b=shape[1])
        if nt == 1:
            ts = [self._track(off, nbytes, name)]
        else:
            assert shape[0] == nt
            sub = nbytes // nt
            ts = [self._track(off + i * sub, sub, f"{name}{i}") for i in range(nt)]
        return Buf(ap, ts)


class Phase:
    def __init__(self, arena, base, limit):
        self.arena = arena
        self.off = base
        self.limit = limit

    def buf(self, shape, dtype, name, nt=1):
        esz = 4 if dtype == F32 else 2
        nbytes = int(np.prod(shape)) * esz
        nbytes_al = (nbytes + 31) // 32 * 32
        off = self.off
        self.off += nbytes_al
        assert self.off <= self.limit, ("phase overflow", name, self.off, self.limit)
        return self.arena.buf(off, shape, dtype, name, nt)

    def ring(self, n, shape, dtype, name, nt=1):
        return [self.buf(shape, dtype, f"{name}_{i}", nt) for i in range(n)]


LAYER_CFG = {
    0: dict(kind="even", nq=19, nkv=21, ctx="full", ffn=[(0, 2432, 0), (2432, 2688, 1)]),
    1: dict(kind="odd", ny=2432, nout=2432, ctx=True, ffn=[(0, 2432, 0), (2432, 2688, 1)]),
    2: dict(kind="even", nq=17, nkv=19, ctx="kv", ffn=[(0, 2064, 0)]),
    3: dict(kind="odd", ny=2064, nout=2048, ctx=False, ffn=[(0, 2048, 0)]),
}


def split_blocks(t0, t1, step):
    out = []
    t = t0
    while t < t1:
        n = min(step, t1 - t)
        out.append((t, n))
        t += n
    return out


def build_program(layers=(0, 1, 2, 3), load_state=False, store_state=False, do_final=True):
    nc = bass.Bass("TRN2", target_bir_lowering=False)
    pg = Prog()
    dr = {}

    def din(name, shape):
        dr[name] = nc.dram_tensor(name, list(shape), F32, kind="ExternalInput").ap()
        return dr[name]

    if load_state:
        din("xst", [D, NXR])
    else:
        din("xin", [D, NWIN + 256])
    din("cvec", [128, 16])
    din("wmod", [4, D, 6 * D])
    din("bmod", [4, 128, 48])
    din("normg", [4, 128, 16])
    din("finalg", [128, 8])
    din("win", [2, D, 2560])
    din("wout", [2, D, D])
    din("lnvg", [2, 128, 4])
    din("wspT", [2, 128, 512])
    din("bspb", [2, 128, 512])
    din("tint", [2, 128, 8 * 640])
    din("tbnd", [2, 2, 128, 8 * 512])
    din("wpw1", [2, D, 2048])
    din("bpw1", [2, 128, 16])
    din("wdw", [2, 128, 248])
    din("bdw", [2, 128, 8])
    din("lncg", [2, 128, 8])
    din("lncb", [2, 128, 8])
    din("wpw2", [2, D, D])
    din("bpw2", [2, 1, D])
    din("wff1", [4, D, 4096])
    din("wff2", [4, 4096, D])
    din("ident", [128, 128])
    if do_final:
        yout = nc.dram_tensor("yout", [D, 2048], F32, kind="ExternalOutput").ap()
    if store_state:
        xst_out = nc.dram_tensor("xst_out", [D, NXR], F32, kind="ExternalOutput").ap()

    es = ExitStack()
    TOTAL = 212800
    arena_t = es.enter_context(nc.sbuf_tensor("arena", [128, TOTAL // 2], BF16))
    psum_t = es.enter_context(nc.psum_tensor("ps", [128, 8, 512], F32))
    AR = Arena(arena_t, TOTAL)

    pers = Phase(AR, 0, TOTAL)
    XR = pers.buf([8, NXR], F32, "XR")
    xr_t = [[T(f"xr{c}_{j}") for j in range(NXR // 128)] for c in range(8)]
    ONES = pers.buf([128], BF16, "ONES")
    IDENT = pers.buf([128], BF16, "IDENT")
    ONESROW = pers.buf([512], BF16, "ONESROW")
    SCT = pers.buf([8, 2], BF16, "SCT")
    CVEC = pers.buf([16], F32, "CVEC")
    FINALG = pers.buf([8], F32, "FINALG")
    NORMG = pers.buf([4, 16], F32, "NORMG")
    MODR = [pers.buf([48, 2], F32, f"MODR{i}") for i in range(2)]
    MODA = [pers.buf([2, 8, 2], F32, f"MODA{i}") for i in range(2)]
    BMOD = [pers.buf([48], F32, f"BMOD{i}") for i in range(2)]
    MODRT = [[T(f"modr{i}_{p}") for p in range(2)] for i in range(2)]
    MODAT = [[T(f"moda{i}_{p}") for p in range(2)] for i in range(2)]
    LNVG = pers.buf([4], F32, "LNVG")
    BSPB = pers.buf([4, 128], F32, "BSPB")
    WSP = pers.buf([4, 128], BF16, "WSP")
    BPW1 = pers.buf([16], F32, "BPW1")
    BDW = pers.buf([8], F32, "BDW")
    LNCG = pers.buf([8], F32, "LNCG")
    LNCB = pers.buf([8], F32, "LNCB")
    WDW = pers.buf([8, 31], F32, "WDW")
    BPW2 = pers.buf([1024], BF16, "BPW2")
    PH0 = (pers.off + 63) // 64 * 64
    PHLIM = TOTAL

    ps_t = [T(f"psb{b}") for b in range(8)]
    bank_ctr = [0]

    reserved_banks = set()

    def bank():
        while True:
            b = bank_ctr[0] % 8
            bank_ctr[0] += 1
            if b not in reserved_banks:
                return b

    def xts(c, t0, n):
        return [xr_t[c][j] for j in range(t0 // 128, (t0 + n + 127) // 128)]

    def mm(out_ap, out_t, lhsT, rhs, reads, start, stop):
        pg.op("pe", lambda e, o=out_ap, l=lhsT, r=rhs, s=start, p=stop: e.matmul(o, l, r, start=s, stop=p),
              r=reads, w=[out_t])

    def act(out_ap, in_ap, func, r, w, bias=None, scale=None):
        def fn(e, o=out_ap, i=in_ap, f=func, b=bias, s=scale):
            kw = {}
            if b is not None:
                kw["bias"] = b
            if s is not None:
                kw["scale"] = s
            return e.activation(out=o, in_=i, func=f, **kw)
        pg.op("act", fn, r=r, w=w)

    def dma_w(out_ap, in_ap, w, eng="pool", r=()):
        pg.dma(eng, lambda e, o=out_ap, i=in_ap: e.dma_start(out=o, in_=i), r=r, w=w)

    dma_w(IDENT.ap, dr["ident"], [IDENT.t()])
    pg.op("dve", lambda e: e.memset(ONES.ap, 1.0), w=[ONES.t()])
    pg.op("dve", lambda e: e.memset(ONESROW.ap, 1.0), w=[ONESROW.t()])
    dma_w(CVEC.ap, dr["cvec"], [CVEC.t()], eng="sp")
    dma_w(FINALG.ap, dr["finalg"], [FINALG.t()], eng="sp")
    dma_w(NORMG.ap, dr["normg"].rearrange("l p f -> p l f"), [NORMG.t()], eng="sp")
    act(SCT.ap.rearrange("p a b -> p (a b)"), CVEC.ap, AF.Silu, r=[CVEC.t()], w=[SCT.t()])

    if load_state:
        src = dr["xst"].rearrange("(c p) t -> p c t", p=128)
        for (t0, n) in split_blocks(0, NXR, 512):
            dma_w(XR.ap[:, :, t0:t0 + n], src[:, :, t0:t0 + n],
                  [xr_t[c][j] for c in range(8) for j in range(t0 // 128, (t0 + n) // 128)], eng="sp")
    else:
        src = dr["xin"].rearrange("(c p) t -> p c t", p=128)
        dma_w(XR.ap[:, :, CTXO:CTXO + 256], src[:, :, NWIN:NWIN + 256],
              [xr_t[c][j] for c in range(8) for j in (19, 20)], eng="sp")
        late_x = []
        for (t0, n) in [(0, 256), (256, 256)] + split_blocks(512, NMAIN, 512):
            if t0 >= 512:
                late_x.append((t0, n))
                continue
            dma_w(XR.ap[:, :, t0:t0 + n], src[:, :, t0:t0 + n],
                  [xr_t[c][j] for c in range(8) for j in range(t0 // 128, (t0 + n) // 128)], eng="sp")

    def mod_part_gen(l, part, bufs, pcols):
        slot = l % 2
        wv = dr["wmod"][l].rearrange("(kc p) f -> p kc f", p=128)
        c0 = part * 24
        npc = pcols // 128
        npieces = 24 // npc
        b = bank()
        reserved_banks.add(b)
        R = len(bufs)

        def issue(k):
            wm = bufs[k % R]
            col = (c0 + k * npc) * 128
            dma_w(wm.ap, wv[:, :, col:col + pcols], [wm.t()])

        def matmuls(k):
            wm = bufs[k % R]
            for fl in range(npc):
                fc = k * npc + fl
                for kc in range(8):
                    mm(psum_t[:, b, fc * 2:fc * 2 + 2], ps_t[b], wm.ap[:, kc, fl * 128:(fl + 1) * 128],
                       SCT.ap[:, kc, :], [wm.t(), SCT.t()], kc == 0, kc == 7)

        if part == 0:
            dma_w(BMOD[slot].ap, dr["bmod"][l], [BMOD[slot].t()], eng="sp")
        for k in range(min(R - 1, npieces)):
            issue(k)
        if R > 1:
            yield
        for k in range(npieces):
            if R == 1:
                issue(k) if k == 0 else None
            elif k + R - 1 < npieces:
                issue(k + R - 1)
            matmuls(k)
            if R == 1 and k + 1 < npieces:
                issue(k + 1)
            yield
        pv = psum_t[:, b, 0:48].rearrange("p (a b) -> p a b", b=2)
        mt = MODRT[slot][part]
        for sgi in range(2):
            pg.op("dve", lambda e, sgi=sgi, pv=pv, slot=slot: e.tensor_tensor(
                out=MODR[slot].ap[:, c0:c0 + 24, sgi], in0=pv[:, :, sgi], in1=BMOD[slot].ap[:, c0:c0 + 24], op=ALU.add),
                r=[ps_t[b], BMOD[slot].t()], w=[mt])
        reserved_banks.discard(b)
        which = part
        sb = 8 if which == 0 else 32
        for sgi in range(2):
            pg.op("dve", lambda e, which=which, sgi=sgi, sb=sb, slot=slot, l=l: e.scalar_tensor_tensor(
                out=MODA[slot].ap[:, which, :, sgi], in0=MODR[slot].ap[:, sb:sb + 8, sgi], scalar=1.0,
                in1=NORMG.ap[:, l, which * 8:which * 8 + 8], op0=ALU.add, op1=ALU.mult),
                r=[mt, NORMG.t()], w=[MODAT[slot][which]])

    def run_gen(g, n=None):
        k = 0
        while n is None or k < n:
            try:
                next(g)
            except StopIteration:
                return True
            k += 1
        return False

    def m_sh(l, which, c, seg):
        return MODR[l % 2].ap[:, (0 if which == 0 else 24) + c, seg:seg + 1]

    def m_g(l, which, c, seg):
        return MODR[l % 2].ap[:, (16 if which == 0 else 40) + c, seg:seg + 1]

    def m_a(l, which, c, seg):
        return MODA[l % 2].ap[:, which, c, seg:seg + 1]

    def mod_ts(l, which=None):
        if which is None:
            return MODRT[l % 2] + MODAT[l % 2]
        return [MODRT[l % 2][which], MODAT[l % 2][which]]

    class NormBufs:
        def __init__(self, ph, nmax, nsq=2, sq_eng="act"):
            self.nsq = nsq
            self.sq_eng = sq_eng
            self.SQ = ph.ring(nsq, [nmax], BF16, "SQ")
            self.RT = ph.buf([nmax], F32, "RT")
            self.NT = ph.ring(2, [nmax], F32, "NT")
            self.ctr = 0
            self.sqctr = 0

    def rstd_psum(nb, src_fn, src_ts_fn, n):
        b = bank()
        pap = psum_t[:, b, 0:n]
        sqs = []
        for c in range(8):
            sq = nb.SQ[nb.sqctr % nb.nsq]
            nb.sqctr += 1
            sqs.append(sq)
            if nb.sq_eng == "pool":
                pg.op("pool", lambda e, sq=sq, c=c: e.tensor_tensor(out=sq.ap[:, 0:n], in0=src_fn(c), in1=src_fn(c),
                                                                 op=ALU.mult), r=src_ts_fn(c), w=[sq.t()])
            else:
                act(sq.ap[:, 0:n], src_fn(c), AF.Square, r=src_ts_fn(c), w=[sq.t()])
            if nb.nsq <= 2:
                mm(pap, ps_t[b], ONES.ap, sq.ap[:, 0:n], [ONES.t(), sq.t()], c == 0, c == 7)
            elif c >= nb.nsq - 1:
                cc = c - (nb.nsq - 1)
                mm(pap, ps_t[b], ONES.ap, sqs[cc].ap[:, 0:n], [ONES.t(), sqs[cc].t()], cc == 0, cc == 7)
        if nb.nsq > 2:
            for cc in range(8 - (nb.nsq - 1), 8):
                mm(pap, ps_t[b], ONES.ap, sqs[cc].ap[:, 0:n], [ONES.t(), sqs[cc].t()], cc == 0, cc == 7)
        act(nb.RT.ap[:, 0:n], pap, AF.Ln, r=[ps_t[b]], w=[nb.RT.t()], bias=EPS, scale=1.0 / D)
        act(pap, nb.RT.ap[:, 0:n], AF.Exp, r=[nb.RT.t()], w=[ps_t[b]], scale=-0.5)
        return b, pap

    def norm_mod(nb, l, which, seg, src_fn, src_ts_fn, n, dst_fn, dst_ts_fn):
        b, pap = rstd_psum(nb, src_fn, src_ts_fn, n)
        for c in range(8):
            nt = nb.NT[nb.ctr % 2]
            nb.ctr += 1
            pg.op("dve", lambda e, c=c, nt=nt, pap=pap: e.scalar_tensor_tensor(
                out=nt.ap[:, 0:n], in0=src_fn(c), scalar=m_a(l, which, c, seg), in1=pap,
                op0=ALU.mult, op1=ALU.mult),
                r=src_ts_fn(c) + [ps_t[b]] + mod_ts(l, which), w=[nt.t()])
            act(dst_fn(c), nt.ap[:, 0:n], AF.Identity, r=[nt.t()] + mod_ts(l, which), w=dst_ts_fn(c),
                bias=m_sh(l, which, c, seg))

    def xr_update(l, which, seg, dc, t0, n, pap, pb):
        pg.op("dve", lambda e, dc=dc, t0=t0, n=n, pap=pap: e.scalar_tensor_tensor(
            out=XR.ap[:, dc, t0:t0 + n], in0=pap, scalar=m_g(l, which, dc, seg), in1=XR.ap[:, dc, t0:t0 + n],
            op0=ALU.mult, op1=ALU.add),
            r=[ps_t[pb]] + xts(dc, t0, n) + mod_ts(l, which), w=xts(dc, t0, n))

    pending_gens = []
    first_layer_part1 = [None]

    def emit_ffn(l, next_mod, fuse_final=False):
        cfg = LAYER_CFG[l]
        ph = Phase(AR, PH0, PHLIM)
        HTA = ph.buf([8, NXR], BF16, "HTA")
        _fl = AR.split(HTA.t(), [f"hta{c}_{j}" for c in range(8) for j in range(NXR // 128)])
        hta_t = [[_fl[c * (NXR // 128) + j] for j in range(NXR // 128)] for c in range(8)]
        W1G = ph.ring(2, [8, 512], BF16, "W1G")
        W2G = ph.ring(2, [4, 1024], BF16, "W2G")
        H1R = ph.ring(2, [512], BF16, "H1R")
        H1 = ph.ring(2, [4, 512], BF16, "H1", nt=4)
        nb = NormBufs(ph, 512)
        blocks = []
        for (a, bnd, seg) in cfg["ffn"]:
            for (t0, n) in split_blocks(a, bnd, 512):
                blocks.append((t0, n, seg))
        def do_norm(bi):
            t0, n, seg = blocks[bi]
            norm_mod(nb, l, 1, seg,
                     lambda c, t0=t0, n=n: XR.ap[:, c, t0:t0 + n],
                     lambda c, t0=t0, n=n: xts(c, t0, n), n,
                     lambda c, t0=t0, n=n: HTA.ap[:, c, t0:t0 + n],
                     lambda c, t0=t0, n=n: [hta_t[c][j] for j in range(t0 // 128, (t0 + n + 127) // 128)])

        gens = []
        OBF = None
        if fuse_final:
            OBF = ph.buf([8, 512], F32, "OBF", nt=8)
            yv = yout.rearrange("(c p) t -> p c t", p=128)

        def final_block(bi):
            t0, n, seg = blocks[bi]
            b, pap = rstd_psum(nb, lambda c, t0=t0, n=n: XR.ap[:, c, t0:t0 + n],
                               lambda c, t0=t0, n=n: xts(c, t0, n), n)
            for c in range(8):
                pg.op("dve", lambda e, c=c, pap=pap, t0=t0, n=n: e.scalar_tensor_tensor(
                    out=OBF.ap[:, c, 0:n], in0=XR.ap[:, c, t0:t0 + n], scalar=FINALG.ap[:, c:c + 1], in1=pap,
                    op0=ALU.mult, op1=ALU.mult),
                    r=xts(c, t0, n) + [ps_t[b], FINALG.t()], w=[OBF.t(c)])
            pg.dma("sp", lambda e, t0=t0, n=n: e.dma_start(out=yv[:, :, t0:t0 + n], in_=OBF.ap[:, :, 0:n]),
                   r=OBF.ts, w=[])

        if next_mod is not None:
            WM = ph.ring(2, [8, 512], BF16, "WM")
            gens = [mod_part_gen(next_mod, 0, WM, 512), mod_part_gen(next_mod, 1, WM, 512)]
        for extra_gen in pending_gens:
            gens.insert(0, extra_gen)
        del pending_gens[:]

        def step_gens():
            while gens:
                if run_gen(gens[0], 1):
                    gens.pop(0)
                    continue
                return
        w1v = dr["wff1"][l].rearrange("(kc p) f -> p kc f", p=128)
        w2v = dr["wff2"][l].rearrange("(fc p) d -> p fc d", p=128)
        h1ctr = [0]

        groups = [(4 * k, 4) for k in range(8)]
        NG = len(groups)

        def load_group(g):
            f0, nfc = groups[g]
            dma_w(W1G[g % 2].ap[:, :, 0:nfc * 128], w1v[:, :, f0 * 128:(f0 + nfc) * 128], [W1G[g % 2].t()])
            dma_w(W2G[g % 2].ap[:, 0:nfc, :], w2v[:, f0:f0 + nfc, :], [W2G[g % 2].t()])

        load_group(0)
        for g in range(NG):
            w1 = W1G[g % 2]
            w2 = W2G[g % 2]
            nfc = groups[g][1]
            if g + 1 < NG:
                load_group(g + 1)

            def stage_a(bi):
                t0, n, seg = blocks[bi]
                hs = H1[h1ctr[0] % 2]
                for fc in range(nfc):
                    b = bank()
                    pap = psum_t[:, b, 0:n]
                    for kc in range(8):
                        mm(pap, ps_t[b], w1.ap[:, kc, fc * 128:(fc + 1) * 128], HTA.ap[:, kc, t0:t0 + n],
                           [w1.t()] + [hta_t[kc][j] for j in range(t0 // 128, (t0 + n + 127) // 128)],
                           kc == 0, kc == 7)
                    hr = H1R[(h1ctr[0] * 4 + fc) % 2]
                    act(hr.ap[:, 0:n], pap, AF.Relu, r=[ps_t[b]], w=[hr.t()])
                    act(hs.ap[:, fc, 0:n], hr.ap[:, 0:n], AF.Square, r=[hr.t()], w=[hs.t(fc)])
                h1ctr[0] += 1
                return hs

            def stage_b(bi, hs):
                t0, n, seg = blocks[bi]
                for dc in range(8):
                    b = bank()
                    pap = psum_t[:, b, 0:n]
                    for fc in range(nfc):
                        mm(pap, ps_t[b], w2.ap[:, fc, dc * 128:(dc + 1) * 128], hs.ap[:, fc, 0:n],
                           [w2.t(), hs.t(fc)], fc == 0, fc == nfc - 1)
                    xr_update(l, 1, seg, dc, t0, n, pap, b)

            prev = None
            if g == 0:
                do_norm(0)
                if len(blocks) > 1:
                    do_norm(1)
            for bi in range(len(blocks)):
                if g == 0 and bi + 2 < len(blocks):
                    do_norm(bi + 2)
                hs = stage_a(bi)
                if prev is not None:
                    stage_b(prev[0], prev[1])
                    if fuse_final and g == NG - 1:
                        final_block(prev[0])
                    if g >= 1:
                        step_gens()
                prev = (bi, hs)
            stage_b(prev[0], prev[1])
            if fuse_final and g == NG - 1:
                final_block(prev[0])
        while gens:
            step_gens()

    def emit_even(l):
        cfg = LAYER_CFG[l]
        i = l // 2
        NQ, NKV = cfg["nq"], cfg["nkv"]
        ph = Phase(AR, PH0, PHLIM)
        WIN = ph.buf([8, 2560], BF16, "WIN")
        win_t = AR.split(WIN.t(), [f"win{b}" for b in range(5)])
        WOUT = ph.buf([8, 1024], BF16, "WOUT")
        TINT = ph.buf([8, 640], BF16, "TINT", nt=8)
        tbs_off = ph.off
        TBS = ph.buf([512], F32, "TBS")
        TBE = ph.ring(2, [512], BF16, "TBE")
        KR = ph.ring(6, [4, 128], BF16, "KR")
        VR = ph.ring(6, [512], BF16, "VR")
        KCX = ph.ring(2, [4, 128], BF16, "KCX")
        VCX = ph.ring(2, [512], BF16, "VCX")
        HT = ph.ring(2, [8, 128], BF16, "HT", nt=8)
        nb = NormBufs(ph, 128, nsq=4, sq_eng="pool")
        QR = ph.ring(4, [4, 128], BF16, "QR")
        ATR = ph.ring(4, [4, 128], BF16, "ATR")
        UTR = ph.ring(2, [4, 128], BF16, "UT")
        GGB = ph.buf([512], F32, "GGB")
        G2B = ph.buf([512], BF16, "G2B")
        GN = ph.buf([512], BF16, "GN")
        ST = ph.buf([8, 4], F32, "ST", nt=8)
        EE = ph.ring(2, [7, 128], BF16, "EE")
        RC = ph.ring(2, [128], F32, "RC")
        BT = ph.buf([4, 128], BF16, "BT", nt=4)
        even_gens = []
        if first_layer_part1[0] == l:
            WMS = ph.buf([8, 128], BF16, "WMS")
            even_gens.append(mod_part_gen(l, 1, [WMS], 128))
            first_layer_part1[0] = None

        wv = dr["win"][i].rearrange("(kc p) f -> p kc f", p=128)
        for blk in (3, 4, 2, 0, 1):
            dma_w(WIN.ap[:, :, blk * 512:(blk + 1) * 512], wv[:, :, blk * 512:(blk + 1) * 512], [win_t[blk]])
        dma_w(WOUT.ap, dr["wout"][i].rearrange("(kc p) f -> p kc f", p=128), [WOUT.t()])
        dma_w(LNVG.ap, dr["lnvg"][i], [LNVG.t()], eng="sp")
        dma_w(BSPB.ap.rearrange("p a b -> p (a b)"), dr["bspb"][i], [BSPB.t()], eng="sp")
        dma_w(WSP.ap.rearrange("p a b -> p (a b)"), dr["wspT"][i], [WSP.t()])
        for h in range(8):
            for hf in range(2):
                dma_w(TBS.ap[:, 0:320], dr["tint"][i][:, h * 640 + hf * 320:h * 640 + (hf + 1) * 320],
                      [TBS.t()], eng="sp")
                act(TINT.ap[:, h, hf * 320:(hf + 1) * 320], TBS.ap[:, 0:320], AF.Exp, r=[TBS.t()], w=[TINT.t(h)])

        GG = GGB.ap
        G2 = G2B.ap

        class _GGT:
            @staticmethod
            def t(k):
                return GGB.t() if k == 0 else G2B.t()
        GGT = _GGT
        s1, s2, mean, msq, var, sd, rstd, nmr = [ST.ap[:, k, :] for k in range(8)]

        class WT:
            pass

        def table_int(h):
            return TINT.ap[:, h, :], [TINT.t(h)]

        def make_table_bnd(j):
            def fn(h):
                tb = TBE[h % 2]
                dma_w(TBS.ap[:, 0:512], dr["tbnd"][i][j][:, h * 512:(h + 1) * 512], [TBS.t()], eng="sp")
                act(tb.ap, TBS.ap[:, 0:512], AF.Exp, r=[TBS.t()], w=[tb.t()])
                return tb.ap, [tb.t()]
            return fn

        ckeys = [(KCX[0], VCX[0]), (KCX[1], VCX[1])]
        W = []
        if cfg["ctx"] in ("full", "kv"):
            for ct in range(2):
                w = WT()
                w.kind, w.idx, w.seg = "ctx", ct, 1
                w.t0 = CTXO + ct * 128
                w.xe = False
                w.kdst, w.vdst = KCX[ct], VCX[ct]
                w.has_q = cfg["ctx"] == "full"
                w.keys, w.nloc, w.table_fn = ckeys, 0, None
                W.append(w)
        for t in range(NKV):
            w = WT()
            w.kind, w.idx, w.seg = "main", t, 0
            w.t0 = t * 128
            w.xe = t >= 19
            w.kdst, w.vdst = KR[t % 6], VR[t % 6]
            w.has_q = t < NQ
            if w.has_q:
                if t < 2:
                    kl = [0, 1, 2, 3]
                    w.table_fn = make_table_bnd(t)
                else:
                    kl = [t - 2, t - 1, t, t + 1, t + 2]
                    w.table_fn = table_int
                w.keys = [(KR[k % 6], VR[k % 6]) for k in kl] + ckeys
                w.nloc = len(kl)
            W.append(w)
        for wi, w in enumerate(W):
            w.wi = wi
            w.ht = HT[wi % 2]
            w.qdst = QR[wi % 4] if w.has_q else None
            w.adst = ATR[wi % 4] if w.has_q else None
            w.ut = UTR[wi % 2]
        XEb = [None]

        def stage_A(w):
            if w.xe:
                if XEb[0] is None:
                    XEb[0] = AR.buf(tbs_off, [8, 128], F32, "XE", nt=8)
                XE = XEb[0]
                xsrc = dr["xin"].rearrange("(c p) t -> p c t", p=128)
                dma_w(XE.ap, xsrc[:, :, w.t0:w.t0 + 128], XE.ts, eng="sp")
                src_fn = lambda c: XE.ap[:, c, :]
                src_ts = lambda c: [XE.t(c)]
            else:
                src_fn = lambda c, t0=w.t0: XR.ap[:, c, t0:t0 + 128]
                src_ts = lambda c, t0=w.t0: xts(c, t0, 128)
            norm_mod(nb, l, 0, w.seg, src_fn, src_ts, 128,
                     lambda c: w.ht.ap[:, c, :], lambda c: [w.ht.t(c)])

        def proj_fm(ht, blk, dst, func, scale=None):
            b = bank()
            for hc in range(4):
                for kc in range(8):
                    mm(psum_t[:, b, hc * 128:(hc + 1) * 128], ps_t[b],
                       WIN.ap[:, kc, blk * 512 + hc * 128: blk * 512 + (hc + 1) * 128], ht.ap[:, kc, :],
                       [win_t[blk], ht.t(kc)], kc == 0, kc == 7)
            act(dst.ap.rearrange("p a b -> p (a b)"), psum_t[:, b, :], func, r=[ps_t[b]], w=[dst.t()], scale=scale)

        def proj_tm(ht, blk):
            b = bank()
            for kc in range(8):
                mm(psum_t[:, b, :], ps_t[b], ht.ap[:, kc, :], WIN.ap[:, kc, blk * 512:(blk + 1) * 512],
                   [win_t[blk], ht.t(kc)], kc == 0, kc == 7)
            return b

        def B_k(w):
            proj_fm(w.ht, 3, w.kdst, AF.Identity)

        def B_v(w):
            b = proj_tm(w.ht, 4)
            pg.op("dve", lambda e, b=b, vdst=w.vdst: e.tensor_copy(out=vdst.ap, in_=psum_t[:, b, :]),
                  r=[ps_t[b]], w=[w.vdst.t()])

        def B_q(w):
            if w.has_q:
                proj_fm(w.ht, 2, w.qdst, AF.Identity, scale=0.125)

        def B_u(w):
            if w.has_q:
                proj_fm(w.ht, 0, w.ut, AF.Gelu_apprx_tanh)

        def B_g(w):
            if w.has_q:
                b = proj_tm(w.ht, 1)
                act(GG, psum_t[:, b, :], AF.Gelu_apprx_tanh, r=[ps_t[b]], w=[GGT.t(0)])

        def C1(w):
            if not w.has_q:
                return
            pg.op("pool", lambda e: e.tensor_tensor(out=G2, in0=GG, in1=GG, op=ALU.mult),
                  r=[GGT.t(0)], w=[GGT.t(1)])
            pg.op("dve", lambda e: e.tensor_reduce(out=s1, in_=GG.rearrange("p (a b) -> p a b", a=4), axis=AX.X, op=ALU.add),
                  r=[GGT.t(0)], w=[ST.t(0)])
            pg.op("dve", lambda e: e.tensor_reduce(out=s2, in_=G2.rearrange("p (a b) -> p a b", a=4), axis=AX.X, op=ALU.add),
                  r=[GGT.t(1)], w=[ST.t(1)])
            pg.op("dve", lambda e: e.tensor_scalar(out=mean, in0=s1, scalar1=1.0 / 128, scalar2=None, op0=ALU.mult),
                  r=[ST.t(0)], w=[ST.t(2)])
            pg.op("dve", lambda e: e.tensor_tensor(out=msq, in0=mean, in1=mean, op=ALU.mult),
                  r=[ST.t(2)], w=[ST.t(3)])
            pg.op("dve", lambda e: e.scalar_tensor_tensor(out=var, in0=s2, scalar=1.0 / 128, in1=msq,
                                                         op0=ALU.mult, op1=ALU.subtract),
                  r=[ST.t(1), ST.t(3)], w=[ST.t(4)])
            act(sd, var, AF.Ln, r=[ST.t(4)], w=[ST.t(5)], bias=EPS)
            act(rstd, sd, AF.Exp, r=[ST.t(5)], w=[ST.t(6)], scale=-0.5)
            pg.op("dve", lambda e: e.scalar_tensor_tensor(out=nmr, in0=mean, scalar=-1.0, in1=rstd,
                                                         op0=ALU.mult, op1=ALU.mult),
                  r=[ST.t(2), ST.t(6)], w=[ST.t(7)])
            for g in range(4):
                pg.op("pool", lambda e, g=g: e.tensor_scalar(
                    out=GN.ap[:, g * 128:(g + 1) * 128], in0=GG[:, g * 128:(g + 1) * 128],
                    scalar1=rstd[:, g:g + 1], scalar2=nmr[:, g:g + 1], op0=ALU.mult, op1=ALU.add),
                    r=[GGT.t(0), ST.t(6), ST.t(7)], w=[GN.t()])

        def C2(w):
            if not w.has_q:
                return
            b = bank()
            for g in range(4):
                mm(psum_t[:, b, g * 128:(g + 1) * 128], ps_t[b], GN.ap[:, g * 128:(g + 1) * 128], WSP.ap[:, g, :],
                   [GN.t(), WSP.t()], True, True)
            for g in range(4):
                pg.op("dve", lambda e, g=g, b=b: e.scalar_tensor_tensor(
                    out=G2[:, g * 128:(g + 1) * 128], in0=psum_t[:, b, g * 128:(g + 1) * 128],
                    scalar=LNVG.ap[:, g:g + 1], in1=BSPB.ap[:, g, :], op0=ALU.mult, op1=ALU.add),
                    r=[ps_t[b], LNVG.t(), BSPB.t()], w=[GGT.t(1)])
            pg.op("dve", lambda e, w=w: e.tensor_tensor(out=w.adst.ap.rearrange("p a b -> p (a b)"), in0=G2,
                                                       in1=w.ut.ap.rearrange("p a b -> p (a b)"), op=ALU.mult),
                  r=[GGT.t(1), w.ut.t()], w=[w.adst.t()])

        ectr = [0]

        def D_scores(w, h):
            hp, pbase = h // 2, (h % 2) * 64
            nk = len(w.keys)
            bs = [bank(), bank()]
            for ki, (kb, vb) in enumerate(w.keys):
                b = bs[ki // 4]
                mm(psum_t[:, b, (ki % 4) * 128:(ki % 4 + 1) * 128], ps_t[b],
                   kb.ap[pbase:pbase + 64, hp, :], w.qdst.ap[pbase:pbase + 64, hp, :],
                   [kb.t(), w.qdst.t()], True, True)
            ee = EE[ectr[0] % 2]
            ectr[0] += 1
            n0 = min(nk, 4)
            act(ee.ap[:, 0:n0, :].rearrange("p a b -> p (a b)"), psum_t[:, bs[0], 0:n0 * 128], AF.Exp,
                r=[ps_t[bs[0]]], w=[ee.t()])
            if nk > 4:
                act(ee.ap[:, 4:nk, :].rearrange("p a b -> p (a b)"), psum_t[:, bs[1], 0:(nk - 4) * 128], AF.Exp,
                    r=[ps_t[bs[1]]], w=[ee.t()])
            if w.nloc > 0:
                tap, tts = w.table_fn(h)
                nloc = w.nloc
                pg.op("dve", lambda e, ee=ee, tap=tap, nloc=nloc: e.tensor_tensor(
                    out=ee.ap[:, 0:nloc, :].rearrange("p a b -> p (a b)"),
                    in0=ee.ap[:, 0:nloc, :].rearrange("p a b -> p (a b)"), in1=tap, op=ALU.mult),
                    r=[ee.t()] + tts, w=[ee.t()])
            w.ee[h] = ee

        def D_pv(w, h):
            hp, pbase = h // 2, (h % 2) * 64
            nk = len(w.keys)
            if h % 2 == 0:
                w.pvb = bank()
            b = w.pvb
            ee = w.ee[h]
            for ki, (kb, vb) in enumerate(w.keys):
                mm(psum_t[pbase:pbase + 64, b, 0:128], ps_t[b], vb.ap[:, h * 64:(h + 1) * 64], ee.ap[:, ki, :],
                   [vb.t(), ee.t()], ki == 0, ki == nk - 1)
            for ki, (kb, vb) in enumerate(w.keys):
                mm(psum_t[pbase:pbase + 64, b, 128:256], ps_t[b], ONES.ap[:, 0:64], ee.ap[:, ki, :],
                   [ONES.t(), ee.t()], ki == 0, ki == nk - 1)
            if h % 2 == 1:
                rc = RC[hp % 2]
                act(rc.ap, psum_t[:, b, 128:256], AF.Ln, r=[ps_t[b]], w=[rc.t()])
                act(rc.ap, rc.ap, AF.Exp, r=[rc.t()], w=[rc.t()], scale=-1.0)
                pg.op("dve", lambda e, b=b, rc=rc, hp=hp: e.tensor_tensor(
                    out=BT.ap[:, hp, :], in0=psum_t[:, b, 0:128], in1=rc.ap, op=ALU.mult),
                    r=[ps_t[b], rc.t()], w=[BT.t(hp)])

        def D_out(w):
            for half in range(2):
                b = bank()
                for dl in range(4):
                    dc = half * 4 + dl
                    pap = psum_t[:, b, dl * 128:(dl + 1) * 128]
                    for kc in range(8):
                        if kc < 4:
                            rhs, rt = w.adst.ap[:, kc, :], w.adst.t()
                        else:
                            rhs, rt = BT.ap[:, kc - 4, :], BT.t(kc - 4)
                        mm(pap, ps_t[b], WOUT.ap[:, kc, dc * 128:(dc + 1) * 128], rhs, [WOUT.t(), rt], kc == 0, kc == 7)
                for dl in range(4):
                    dc = half * 4 + dl
                    xr_update(l, 0, w.seg, dc, w.t0, 128, psum_t[:, b, dl * 128:(dl + 1) * 128], b)

        NW = len(W)
        LAG = 3
        pend_out = []
        stage_A(W[0])
        for s in range(NW + LAG):
            wB = W[s] if s < NW else None
            wA = W[s + 1] if s + 1 < NW else None
            wC = W[s - 1] if 1 <= s <= NW else None
            wD = W[s - LAG] if 0 <= s - LAG < NW and W[s - LAG].has_q else None
            if wD is not None:
                wD.ee = {}
            serial = False
            if wD is not None and wB is not None and wB.kind == "main" and wD.kind == "main":
                need = 3 if wD.idx < 2 else wD.idx + 2
                if need >= wB.idx:
                    serial = True
            if wA is not None:
                stage_A(wA)
            if even_gens and s >= 1:
                if run_gen(even_gens[0], 1):
                    even_gens.pop(0)
            if wC is not None:
                C1(wC)
            bgroups = []
            if wB is not None:
                bgroups = [lambda: B_k(wB), lambda: B_v(wB), lambda: B_q(wB), lambda: B_u(wB), lambda: B_g(wB)]
            if serial:
                for f in bgroups:
                    f()
                bgroups = []
            extra = list(bgroups)
            if wC is not None:
                extra.append(lambda: C2(wC))
            if wD is None:
                for f in extra:
                    f()
                if pend_out:
                    D_out(pend_out.pop())
                continue
            D_scores(wD, 0)
            if pend_out:
                D_out(pend_out.pop())
            for h in range(8):
                if extra:
                    extra.pop(0)()
                if h + 1 < 8:
                    D_scores(wD, h + 1)
                D_pv(wD, h)
            for f in extra:
                f()
            pend_out.append(wD)
        if pend_out:
            D_out(pend_out.pop())
        while even_gens:
            if run_gen(even_gens[0], None):
                even_gens.pop(0)

    def emit_odd(l):
        cfg = LAYER_CFG[l]
        i = l // 2
        ny, nout = cfg["ny"], cfg["nout"]
        ph = Phase(AR, PH0, PHLIM)
        YW = 16 + NMAIN + 16
        YM = ph.buf([8, YW], BF16, "YM")
        YC = ph.buf([8, 288], BF16, "YC")
        _fl = AR.split(YM.t(), [f"ym{c}_{j}" for c in range(8) for j in range(YW // 128 + 1)])
        ym_t = [[_fl[c * (YW // 128 + 1) + j] for j in range(YW // 128 + 1)] for c in range(8)]
        yc_t = AR.split(YC.t(), [f"yc{c}" for c in range(8)])
        base2 = ph.off
        WPW1 = ph.buf([8, 2048], BF16, "WPW1")
        w1_t = AR.split(WPW1.t(), [f"wpw1_{b}" for b in range(4)])
        HT = ph.ring(2, [8, 512], BF16, "HT", nt=8)
        nb = NormBufs(ph, 512)
        SG = ph.ring(2, [512], F32, "SG")

        wv = dr["wpw1"][i].rearrange("(kc p) f -> p kc f", p=128)
        for blk in (0, 2, 1, 3):
            dma_w(WPW1.ap[:, :, blk * 512:(blk + 1) * 512], wv[:, :, blk * 512:(blk + 1) * 512], [w1_t[blk]])
        dma_w(BPW1.ap, dr["bpw1"][i], [BPW1.t()], eng="sp")
        dma_w(BDW.ap, dr["bdw"][i], [BDW.t()], eng="sp")
        dma_w(LNCG.ap, dr["lncg"][i], [LNCG.t()], eng="sp")
        dma_w(LNCB.ap, dr["lncb"][i], [LNCB.t()], eng="sp")
        dma_w(WDW.ap.rearrange("p a b -> p (a b)"), dr["wdw"][i], [WDW.t()], eng="sp")
        dma_w(BPW2.ap[0:1, :], dr["bpw2"][i], [BPW2.t()])
        pg.op("pool", lambda e: e.memset(YM.ap, 0.0), w=[ym_t[c][j] for c in range(8) for j in range(YW // 128 + 1)])
        pg.op("pool", lambda e: e.memset(YC.ap, 0.0), w=yc_t)

        def ymts(c, col0, n):
            return [ym_t[c][j] for j in range(col0 // 128, (col0 + n + 127) // 128)]

        segs = [(0, ny, 0, YM, ymts)]
        if cfg["ctx"]:
            segs.append((CTXO, CTXO + 256, 1, YC, lambda c, col0, n: [yc_t[c]]))
        p1blocks = []
        for (a, bnd, seg, Y, ytf) in segs:
            for (t0, n) in split_blocks(a, bnd, 512):
                p1blocks.append((a, seg, Y, ytf, t0, n))

        def p1_norm(bi):
            a, seg, Y, ytf, t0, n = p1blocks[bi]
            ht = HT[bi % 2]
            norm_mod(nb, l, 0, seg,
                     lambda c, t0=t0, n=n: XR.ap[:, c, t0:t0 + n],
                     lambda c, t0=t0, n=n: xts(c, t0, n), n,
                     lambda c, ht=ht, n=n: ht.ap[:, c, 0:n], lambda c, ht=ht: [ht.t(c)])

        p1_norm(0)
        for bi in range(len(p1blocks)):
            if True:
                a, seg, Y, ytf, t0, n = p1blocks[bi]
                ht = HT[bi % 2]
                if bi + 1 < len(p1blocks):
                    p1_norm(bi + 1)
                ycol = 16 + (t0 - a)
                for c in range(8):
                    ba = bank()
                    bg = bank()
                    for kc in range(8):
                        mm(psum_t[:, ba, 0:n], ps_t[ba], WPW1.ap[:, kc, c * 128:(c + 1) * 128], ht.ap[:, kc, 0:n],
                           [w1_t[c // 4], ht.t(kc)], kc == 0, kc == 7)
                    for kc in range(8):
                        mm(psum_t[:, bg, 0:n], ps_t[bg], WPW1.ap[:, kc, 1024 + c * 128:1024 + (c + 1) * 128],
                           ht.ap[:, kc, 0:n], [w1_t[2 + c // 4], ht.t(kc)], kc == 0, kc == 7)
                    sg = SG[c % 2]
                    act(sg.ap[:, 0:n], psum_t[:, bg, 0:n], AF.Sigmoid, r=[ps_t[bg], BPW1.t()], w=[sg.t()],
                        bias=BPW1.ap[:, 8 + c:9 + c])
                    pg.op("dve", lambda e, c=c, ba=ba, sg=sg, n=n, Y=Y, ycol=ycol: e.scalar_tensor_tensor(
                        out=Y.ap[:, c, ycol:ycol + n], in0=psum_t[:, ba, 0:n], scalar=BPW1.ap[:, c:c + 1],
                        in1=sg.ap[:, 0:n], op0=ALU.add, op1=ALU.mult),
                        r=[ps_t[ba], sg.t(), BPW1.t()], w=ytf(c, ycol, n))

        ph2 = Phase(AR, base2, PHLIM)
        WPW2 = ph2.buf([8, 1024], BF16, "WPW2")
        DG = ph2.ring(2, [31, 128], BF16, "DG", nt=31)
        CVB = ph2.buf([8, 512], BF16, "CVB", nt=8)
        SQB = ph2.buf([8, 512], BF16, "SQB", nt=8)
        MS = ph2.buf([512], F32, "MS")
        DT = ph2.ring(2, [512], F32, "DT")
        HN = ph2.ring(1, [8, 512], BF16, "HN", nt=8)
        TMPF = ph2.ring(2, [512], F32, "TMPF")
        NDVE = 8
        dma_w(WPW2.ap, dr["wpw2"][i].rearrange("(kc p) f -> p kc f", p=128), [WPW2.t()])
        segs2 = [(0, nout, 0, YM, ymts)]
        if cfg["ctx"]:
            segs2.append((CTXO, CTXO + 256, 1, YC, lambda c, col0, n: [yc_t[c]]))
        dctr = 0
        hnc = 0
        for (a, bnd, seg, Y, ytf) in segs2:
            for (t0, n) in split_blocks(a, bnd, 512):
                ycol = 16 + (t0 - a)
                for cp in range(4):
                    cs = (2 * cp, 2 * cp + 1)
                    dgs = {}
                    bds = {}
                    for c in cs:
                        dg = DG[dctr % 2]
                        dctr += 1
                        dgs[c] = dg
                        for k in range(NDVE, 31):
                            if k % 2 == 0:
                                pg.op("pool", lambda e, dg=dg, k=k, c=c: e.tensor_scalar(
                                    out=dg.ap[:, k, :], in0=IDENT.ap, scalar1=WDW.ap[:, c, k:k + 1], scalar2=0.0,
                                    op0=ALU.mult, op1=ALU.add), r=[IDENT.t(), WDW.t()], w=[dg.t(k)])
                            else:
                                act(dg.ap[:, k, :], IDENT.ap, AF.Identity, r=[IDENT.t(), WDW.t()], w=[dg.t(k)],
                                    scale=WDW.ap[:, c, k:k + 1])
                        bds[c] = bank()
                    for k in range(NDVE):
                        for c in cs:
                            col = ycol + k - 15
                            accD = psum_t[:, bds[c], 0:n]
                            if k == 0:
                                pg.op("dve", lambda e, c=c, col=col, accD=accD, Y=Y, n=n: e.tensor_scalar(
                                    out=accD, in0=Y.ap[:, c, col:col + n], scalar1=WDW.ap[:, c, 0:1], scalar2=None,
                                    op0=ALU.mult), r=ytf(c, col, n) + [WDW.t()], w=[ps_t[bds[c]]])
                            else:
                                pg.op("dve", lambda e, c=c, col=col, accD=accD, Y=Y, n=n, k=k: e.scalar_tensor_tensor(
                                    out=accD, in0=Y.ap[:, c, col:col + n], scalar=WDW.ap[:, c, k:k + 1], in1=accD,
                                    op0=ALU.mult, op1=ALU.add),
                                    r=ytf(c, col, n) + [WDW.t(), ps_t[bds[c]]], w=[ps_t[bds[c]]])
                    for c in cs:
                        dg = dgs[c]
                        b = bank()
                        for k in range(NDVE, 31):
                            col = ycol + k - 15
                            mm(psum_t[:, b, 0:n], ps_t[b], dg.ap[:, k, :], Y.ap[:, c, col:col + n],
                               [dg.t(k)] + ytf(c, col, n), k == NDVE, k == 30)
                        tmp = TMPF[c % 2]
                        act(tmp.ap[:, 0:n], psum_t[:, b, 0:n], AF.Identity, r=[ps_t[b], BDW.t()], w=[tmp.t()],
                            bias=BDW.ap[:, c:c + 1])
                        pg.op("dve", lambda e, c=c, tmp=tmp, n=n, bd=bds[c]: e.tensor_tensor(
                            out=CVB.ap[:, c, 0:n], in0=tmp.ap[:, 0:n], in1=psum_t[:, bd, 0:n], op=ALU.add),
                            r=[tmp.t(), ps_t[bds[c]]], w=[CVB.t(c)])
                        act(SQB.ap[:, c, 0:n], CVB.ap[:, c, 0:n], AF.Square, r=[CVB.t(c)], w=[SQB.t(c)])
                b1 = bank()
                b2 = bank()
                for c in range(8):
                    mm(psum_t[:, b1, 0:n], ps_t[b1], ONES.ap, CVB.ap[:, c, 0:n], [ONES.t(), CVB.t(c)], c == 0, c == 7)
                for c in range(8):
                    mm(psum_t[:, b2, 0:n], ps_t[b2], ONES.ap, SQB.ap[:, c, 0:n], [ONES.t(), SQB.t(c)], c == 0, c == 7)
                mean_p = psum_t[:, b1, 0:n]
                rstd_p = psum_t[:, b2, 0:n]
                ms = MS.ap[:, 0:n]
                pg.op("dve", lambda e, mean_p=mean_p: e.tensor_scalar(out=mean_p, in0=mean_p, scalar1=1.0 / D,
                                                                    scalar2=None, op0=ALU.mult),
                      r=[ps_t[b1]], w=[ps_t[b1]])
                act(ms, mean_p, AF.Square, r=[ps_t[b1]], w=[MS.t()])
                pg.op("dve", lambda e, rstd_p=rstd_p, ms=ms: e.scalar_tensor_tensor(
                    out=ms, in0=rstd_p, scalar=1.0 / D, in1=ms, op0=ALU.mult, op1=ALU.subtract),
                    r=[ps_t[b2], MS.t()], w=[MS.t()])
                act(ms, ms, AF.Ln, r=[MS.t()], w=[MS.t()], bias=EPS)
                act(rstd_p, ms, AF.Exp, r=[MS.t()], w=[ps_t[b2]], scale=-0.5)
                hn = HN[0]
                hnc += 1
                for c in range(8):
                    dt_ = DT[c % 2]
                    pg.op("dve", lambda e, c=c, dt_=dt_, mean_p=mean_p, n=n: e.tensor_tensor(
                        out=dt_.ap[:, 0:n], in0=CVB.ap[:, c, 0:n], in1=mean_p, op=ALU.subtract),
                        r=[CVB.t(c), ps_t[b1]], w=[dt_.t()])
                    pg.op("dve", lambda e, dt_=dt_, rstd_p=rstd_p, n=n: e.tensor_tensor(
                        out=dt_.ap[:, 0:n], in0=dt_.ap[:, 0:n], in1=rstd_p, op=ALU.mult),
                        r=[dt_.t(), ps_t[b2]], w=[dt_.t()])
                    act(hn.ap[:, c, 0:n], dt_.ap[:, 0:n], AF.Silu, r=[dt_.t(), LNCG.t(), LNCB.t()], w=[hn.t(c)],
                        bias=LNCB.ap[:, c:c + 1], scale=LNCG.ap[:, c:c + 1])
                for dc in range(8):
                    b = bank()
                    pap = psum_t[:, b, 0:n]
                    for kc in range(8):
                        mm(pap, ps_t[b], WPW2.ap[:, kc, dc * 128:(dc + 1) * 128], hn.ap[:, kc, 0:n],
                           [WPW2.t(), hn.t(kc)], kc == 0, False)
                    mm(pap, ps_t[b], BPW2.ap[0:1, dc * 128:(dc + 1) * 128], ONESROW.ap[0:1, 0:n],
                       [BPW2.t(), ONESROW.t()], False, True)
                    xr_update(l, 0, seg, dc, t0, n, pap, b)

    def emit_final():
        ph = Phase(AR, PH0, PHLIM)
        nb = NormBufs(ph, 512)
        OB = ph.ring(2, [8, 512], F32, "OB", nt=8)
        yv = yout.rearrange("(c p) t -> p c t", p=128)
        for bi, (t0, n) in enumerate(split_blocks(0, 2048, 512)):
            b, pap = rstd_psum(nb, lambda c, t0=t0, n=n: XR.ap[:, c, t0:t0 + n],
                               lambda c, t0=t0, n=n: xts(c, t0, n), n)
            ob = OB[bi % 2]
            for c in range(8):
                pg.op("dve", lambda e, c=c, ob=ob, pap=pap, t0=t0, n=n: e.scalar_tensor_tensor(
                    out=ob.ap[:, c, 0:n], in0=XR.ap[:, c, t0:t0 + n], scalar=FINALG.ap[:, c:c + 1], in1=pap,
                    op0=ALU.mult, op1=ALU.mult),
                    r=xts(c, t0, n) + [ps_t[b], FINALG.t()], w=[ob.t(c)])
            pg.dma("sp", lambda e, ob=ob, t0=t0, n=n: e.dma_start(out=yv[:, :, t0:t0 + n], in_=ob.ap[:, :, 0:n]),
                   r=ob.ts, w=[])

    layers = list(layers)
    ph_mod = Phase(AR, PH0, PHLIM)
    l0 = layers[0]
    WM0 = ph_mod.ring(6, [8, 512], BF16, "WM0")
    run_gen(mod_part_gen(l0, 0, WM0, 512), None)
    if not load_state:
        src = dr["xin"].rearrange("(c p) t -> p c t", p=128)
        for (t0, n) in late_x:
            dma_w(XR.ap[:, :, t0:t0 + n], src[:, :, t0:t0 + n],
                  [xr_t[c][j] for c in range(8) for j in range(t0 // 128, (t0 + n) // 128)], eng="sp",
                  r=[MODRT[l0 % 2][0]])
    if LAYER_CFG[l0]["kind"] == "even":
        first_layer_part1[0] = l0
    else:
        run_gen(mod_part_gen(l0, 1, WM0, 512), None)
    for li, l in enumerate(layers):
        nxt = layers[li + 1] if li + 1 < len(layers) else None
        if LAYER_CFG[l]["kind"] == "even":
            emit_even(l)
        else:
            emit_odd(l)
        fuse = do_final and l == 3 and nxt is None
        emit_ffn(l, nxt, fuse_final=fuse)
    if do_final and not (layers[-1] == 3):
        emit_final()
    if store_state:
        dst = xst_out.rearrange("(c p) t -> p c t", p=128)
        for (t0, n) in split_blocks(0, NXR, 512):
            pg.dma("sp", lambda e, t0=t0, n=n: e.dma_start(out=dst[:, :, t0:t0 + n], in_=XR.ap[:, :, t0:t0 + n]),
                   r=[xr_t[c][j] for c in range(8) for j in range(t0 // 128, (t0 + n) // 128)], w=[])
    out_dmas = [ins for ins in pg.q["sp"] if ins.dma]
    tail = out_dmas[-DMA_RING:]
    fin = Inst()
    fin.eng = "sp"
    fin.fn = None
    fin.dma = False
    fin.sig = False
    fin.key = None
    fin.val = 0
    fin.deps = list(tail)
    pg.q["sp"].append(fin)

    pg.finalize()

    keys = set()
    for e, lst in pg.q.items():
        for ins in lst:
            if ins.key is not None and (ins.dma or ins.sig):
                keys.add(ins.key)
            for k, v in ins.waits:
                keys.add(k)
    sems = {}
    for k in sorted(keys, key=str):
        sems[k] = es.enter_context(nc.semaphore("s_" + "_".join(str(x) for x in k)))

    def emit(engname, e):
        for ins in pg.q[engname]:
            for key, val in ins.waits:
                e.wait_ge(sems[key], val)
            if ins.fn is None:
                continue
            r = ins.fn(e)
            if ins.dma:
                r.then_inc(sems[ins.key], 16)
            elif ins.sig:
                r.then_inc(sems[ins.key], 1)

    with nc.Block() as block:
        @block.tensor
        def _(e):
            emit("pe", e)

        @block.scalar
        def _(e):
            emit("act", e)

        @block.vector
        def _(e):
            emit("dve", e)

        @block.gpsimd
        def _(e):
            emit("pool", e)

        @block.sync
        def _(e):
            emit("sp", e)
    es.close()
    return nc


NEG_FILL = -30000.0


def _bias_tables(rpb_l, flipped):
    def table(jq, ktiles):
        out = np.full((128, 8, len(ktiles), 128), NEG_FILL, np.float32)
        kk = np.arange(128)
        qq = np.arange(128)
        for ki, kt in enumerate(ktiles):
            kr_l = 2 * kt + kk // 64
            kc_l = kk % 64
            qr_l = 2 * jq + qq // 64
            qc_l = qq % 64
            if flipped:
                kr, kc_, qr, qc = 63 - kr_l, 63 - kc_l, 63 - qr_l, 63 - qc_l
            else:
                kr, kc_, qr, qc = kr_l, kc_l, qr_l, qc_l
            rs = np.clip(qr - 4, 0, 56)
            cs = np.clip(qc - 8, 0, 48)
            KR_, QR_ = np.meshgrid(kr, qr, indexing="ij")
            KC_, QC_ = np.meshgrid(kc_, qc, indexing="ij")
            RS_ = np.broadcast_to(rs[None, :], KR_.shape)
            CS_ = np.broadcast_to(cs[None, :], KR_.shape)
            valid = (KR_ >= RS_) & (KR_ < RS_ + 8) & (KC_ >= CS_) & (KC_ < CS_ + 16) & (KR_ >= 0) & (KR_ < 64)
            drr = np.clip(KR_ - QR_ + 7, 0, 14)
            dcc = np.clip(KC_ - QC_ + 15, 0, 30)
            vals = rpb_l[:, drr, dcc]
            blk = np.where(valid[None], vals, np.float32(NEG_FILL)).astype(np.float32)
            out[:, :, ki, :] = np.transpose(blk, (1, 0, 2))
        return out

    tint = table(10, [8, 9, 10, 11, 12]).reshape(128, 8 * 5 * 128)
    tb = np.stack([table(0, [0, 1, 2, 3]).reshape(128, 8 * 4 * 128),
                   table(1, [0, 1, 2, 3]).reshape(128, 8 * 4 * 128)], axis=0)
    return tint, tb


def _pc(v, nchunk):
    return np.ascontiguousarray(v.reshape(nchunk, 128).T)


def prep_inputs(inp):
    f = lambda a: np.ascontiguousarray(np.asarray(a, dtype=np.float32))
    x = f(inp["x"])
    c = f(inp["c"])
    ctx = f(inp["ctx"])
    c_ctx = f(inp["c_ctx"])
    shared = {
        "wmod": f(inp["w_mod"]),
        "bmod": np.stack([_pc(f(inp["b_mod"])[l], 48) for l in range(4)]),
        "normg": np.stack([np.concatenate([_pc(f(inp["norm_g"])[l, 0], 8), _pc(f(inp["norm_g"])[l, 1], 8)], axis=1)
                           for l in range(4)]),
        "finalg": _pc(f(inp["final_g"]), 8),
        "win": f(inp["w_in"]),
        "wout": f(inp["w_out"]),
        "lnvg": np.stack([_pc(f(inp["ln_v_g"])[i], 4) for i in range(2)]),
        "wpw1": f(inp["w_pw1"]),
        "bpw1": np.stack([_pc(f(inp["b_pw1"])[i], 16) for i in range(2)]),
        "bdw": np.stack([_pc(f(inp["b_dw"])[i], 8) for i in range(2)]),
        "lncg": np.stack([_pc(f(inp["ln_c_g"])[i], 8) for i in range(2)]),
        "lncb": np.stack([_pc(f(inp["ln_c_b"])[i], 8) for i in range(2)]),
        "wpw2": f(inp["w_pw2"]),
        "bpw2": f(inp["b_pw2"]).reshape(2, 1, D),
        "wff1": f(inp["w_ff1"]),
        "wff2": f(inp["w_ff2"]),
        "ident": np.eye(128, dtype=np.float32),
    }
    w_sp = f(inp["w_sp"])
    b_sp = f(inp["b_sp"])
    w_dw = f(inp["w_dw"])
    rpb = f(inp["rpb"])
    per_half = []
    for flipped in (False, True):
        wsp = w_sp[:, :, ::-1, ::-1] if flipped else w_sp
        bsp = b_sp[:, :, ::-1] if flipped else b_sp
        wdw = w_dw[:, ::-1, :] if flipped else w_dw
        wspT = np.ascontiguousarray(np.transpose(wsp, (0, 3, 1, 2))).reshape(2, 128, 512)
        bspb = np.ascontiguousarray(np.broadcast_to(bsp.reshape(2, 1, 512), (2, 128, 512)))
        wdwl = np.ascontiguousarray(np.transpose(wdw.reshape(2, 31, 8, 128), (0, 3, 2, 1))).reshape(2, 128, 248)
        tints, tbnds = [], []
        for i in range(2):
            ti, tb = _bias_tables(rpb[i], flipped)
            tints.append(ti)
            tbnds.append(tb)
        per_half.append({"wspT": wspT, "bspb": bspb, "wdw": wdwl,
                         "tint": np.stack(tints), "tbnd": np.stack(tbnds)})
    cores = []
    for core in range(8):
        b, half = core // 2, core % 2
        if half == 0:
            xs = x[b, 0:NWIN]
            cs = ctx[b]
        else:
            xs = x[b, ::-1][0:NWIN]
            cs = ctx[b, ::-1]
        xin = np.ascontiguousarray(np.concatenate([xs, cs], axis=0).T)
        cvec = np.ascontiguousarray(np.stack([_pc(c[b], 8), _pc(c_ctx, 8)], axis=2).reshape(128, 16))
        m = dict(shared)
        m.update(per_half[half])
        m["xin"] = xin
        m["cvec"] = cvec
        cores.append(m)
    return cores


_NC_CACHE = {}


def _get_nc(key, **kw):
    if key not in _NC_CACHE:
        _NC_CACHE[key] = build_program(**kw)
    return _NC_CACHE[key]


def assemble_output(youts):
    out = np.empty((4, 4096, D), np.float32)
    for core in range(8):
        b, half = core // 2, core % 2
        y = np.asarray(youts[core]).T
        if half == 0:
            out[b, 0:2048] = y
        else:
            out[b, 2048:4096] = y[::-1]
    return out


def kernel(**inputs):
    cores = prep_inputs(inputs)
    nc = _get_nc("full", layers=(0, 1, 2, 3), load_state=False, store_state=False, do_final=True)
    res = run_bass_kernel_spmd(nc, cores, core_ids=list(range(8)))
    return assemble_output([r["yout"] for r in res.results])
```

```python
import numpy as np
import concourse.bass as bass
import concourse.mybir as mybir
from concourse.bass_utils import run_bass_kernel_spmd
from contextlib import ExitStack

F32 = mybir.dt.float32
BF16 = mybir.dt.bfloat16
AF = mybir.ActivationFunctionType
ALU = mybir.AluOpType
AX = mybir.AxisListType

D = 1024
KC = 8
NMAIN = 2432
CTXO = 2432
NXR = 2688
NWIN = 2688
EPS = 1e-6
SELF_SYNC = True
DMA_RING = 8


class Inst:
    __slots__ = ("eng", "fn", "deps", "idx", "sig", "key", "val", "dma", "waits")


class T:
    __slots__ = ("name", "w", "rs", "init")

    def __init__(self, name, init=()):
        self.name = name
        self.w = None
        self.rs = []
        self.init = list(init)


class Prog:
    ENGS = ("pe", "act", "dve", "pool", "sp")

    def __init__(self):
        self.q = {e: [] for e in self.ENGS}
        self.ndma = {e: 0 for e in self.ENGS}
        self.dhist = {e: [] for e in self.ENGS}

    def _add(self, eng, fn, r, w, dma):
        ins = Inst()
        ins.eng = eng
        ins.fn = fn
        ins.dma = dma
        ins.sig = False
        ins.key = None
        ins.val = 0
        deps = []
        for t in r:
            if t.w is not None:
                deps.append(t.w)
            elif t.init:
                deps.extend(t.init)
        for t in w:
            if t.w is not None:
                deps.append(t.w)
            if t.init:
                deps.extend(t.init)
                t.init = []
            deps.extend(t.rs)
        for t in w:
            t.w = ins
            t.rs = []
        for t in r:
            if t.w is not ins:
                t.rs.append(ins)
        if dma:
            m = self.ndma[eng]
            self.ndma[eng] = m + 1
            ins.key = ("d", eng, m % DMA_RING)
            ins.val = 16 * (m // DMA_RING + 1)
            if m >= DMA_RING:
                deps.append(self.dhist[eng][m - DMA_RING])
            self.dhist[eng].append(ins)
        ins.deps = deps
        self.q[eng].append(ins)
        return ins

    def op(self, eng, fn, r=(), w=()):
        return self._add(eng, fn, r, w, False)

    def dma(self, eng, fn, r=(), w=()):
        return self._add(eng, fn, r, w, True)

    def finalize(self):
        for e, lst in self.q.items():
            for i, ins in enumerate(lst):
                ins.idx = i
        for e, lst in self.q.items():
            for ins in lst:
                best = {}
                dm = {}
                for d in ins.deps:
                    if d is ins:
                        continue
                    if d.dma:
                        dm[id(d)] = d
                    else:
                        if d.eng == e and (e == "pe" or not SELF_SYNC):
                            continue
                        b = best.get(d.eng)
                        if b is None or b.idx < d.idx:
                            best[d.eng] = d
                ins.deps = list(best.values()) + list(dm.values())
                for d in best.values():
                    d.sig = True
        for e in self.ENGS:
            cnt = 0
            for ins in self.q[e]:
                if ins.dma:
                    continue
                if ins.sig:
                    cnt += 1
                    ins.key = ("c", e)
                    ins.val = cnt
        for e, lst in self.q.items():
            known = {}
            for ins in lst:
                waits = []
                for d in ins.deps:
                    if known.get(d.key, 0) >= d.val:
                        continue
                    known[d.key] = d.val
                    waits.append((d.key, d.val))
                ins.waits = waits

    def emit_engine(self, eng, e, sems):
        for ins in self.q[eng]:
            for key, val in ins.waits:
                e.wait_ge(sems[key], val)
            r = ins.fn(e)
            if ins.dma:
                r.then_inc(sems[ins.key], 16)
            elif ins.sig:
                r.then_inc(sems[ins.key], 1)


class Buf:
    def __init__(self, ap, ts):
        self.ap = ap
        self.ts = ts

    def t(self, i=0):
        return self.ts[i]


class Arena:
    def __init__(self, tensor, nbytes):
        self.tensor = tensor
        self.nbytes = nbytes
        self.live = []

    def _track(self, off, nbytes, name):
        t = T(name)
        init = []
        keep = []
        end = off + nbytes
        for (o, e, old) in self.live:
            if o < end and off < e:
                if old.w is not None:
                    init.append(old.w)
                init.extend(old.rs)
                init.extend(old.init)
                if not (off <= o and e <= end):
                    keep.append((o, e, old))
            else:
                keep.append((o, e, old))
        t.init = init
        keep.append((off, end, t))
        self.live = keep
        return t

    def split(self, parent, names):
        out = []
        keep = []
        rng = None
        for (o, e, t) in self.live:
            if t is parent:
                rng = (o, e)
            else:
                keep.append((o, e, t))
        assert rng is not None
        for nm in names:
            t = T(nm, init=parent.init)
            keep.append((rng[0], rng[1], t))
            out.append(t)
        self.live = keep
        return out

    def buf(self, off, shape, dtype, name, nt=1):
        esz = 4 if dtype == F32 else 2
        n = int(np.prod(shape))
        nbytes = n * esz
        assert off % 4 == 0 and off + nbytes <= self.nbytes, (name, off, nbytes, self.nbytes)
        ap = self.tensor[:, off // 2: off // 2 + nbytes // 2]
        if dtype == F32:
            ap = ap.bitcast(F32)
        if len(shape) == 2:
            ap = ap.rearrange("p (a b) -> p a b", a=shape[0])
        elif len(shape) == 3:
            ap = ap.rearrange("p (a b c) -> p a b c", a=shape[0], b=shape[1])
        if nt == 1:
            ts = [self._track(off, nbytes, name)]
        else:
            assert shape[0] == nt
            sub = nbytes // nt
            ts = [self._track(off + i * sub, sub, f"{name}{i}") for i in range(nt)]
        return Buf(ap, ts)


class Phase:
    def __init__(self, arena, base, limit):
        self.arena = arena
        self.off = base
        self.limit = limit

    def buf(self, shape, dtype, name, nt=1):
        esz = 4 if dtype == F32 else 2
        nbytes = int(np.prod(shape)) * esz
        nbytes_al = (nbytes + 31) // 32 * 32
        off = self.off
        self.off += nbytes_al
        assert self.off <= self.limit, ("phase overflow", name, self.off, self.limit)
        return self.arena.buf(off, shape, dtype, name, nt)

    def ring(self, n, shape, dtype, name, nt=1):
        return [self.buf(shape, dtype, f"{name}_{i}", nt) for i in range(n)]


LAYER_CFG = {
    0: dict(kind="even", nq=19, nkv=21, ctx="full", ffn=[(0, 2432, 0), (2432, 2688, 1)]),
    1: dict(kind="odd", ny=2432, nout=2432, ctx=True, ffn=[(0, 2432, 0), (2432, 2688, 1)]),
    2: dict(kind="even", nq=17, nkv=19, ctx="kv", ffn=[(0, 2064, 0)]),
    3: dict(kind="odd", ny=2064, nout=2048, ctx=False, ffn=[(0, 2048, 0)]),
}


def split_blocks(t0, t1, step):
    out = []
    t = t0
    while t < t1:
        n = min(step, t1 - t)
        out.append((t, n))
        t += n
    return out


def build_program(layers=(0, 1, 2, 3), load_state=False, store_state=False, do_final=True):
    nc = bass.Bass("TRN2", target_bir_lowering=False)
    pg = Prog()
    dr = {}

    def din(name, shape):
        dr[name] = nc.dram_tensor(name, list(shape), F32, kind="ExternalInput").ap()
        return dr[name]

    if load_state:
        din("xst", [D, NXR])
    else:
        din("xin", [D, NWIN + 256])
    din("cvec", [128, 16])
    din("wmod", [4, D, 6 * D])
    din("bmod", [4, 128, 48])
    din("normg", [4, 128, 16])
    din("finalg", [128, 8])
    din("win", [2, D, 2560])
    din("wout", [2, D, D])
    din("lnvg", [2, 128, 4])
    din("wspT", [2, 128, 512])
    din("bspb", [2, 128, 512])
    din("tint", [2, 128, 8 * 640])
    din("tbnd", [2, 2, 128, 8 * 512])
    din("wpw1", [2, D, 2048])
    din("bpw1", [2, 128, 16])
    din("wdw", [2, 128, 248])
    din("bdw", [2, 128, 8])
    din("lncg", [2, 128, 8])
    din("lncb", [2, 128, 8])
    din("wpw2", [2, D, D])
    din("bpw2", [2, 1, D])
    din("wff1", [4, D, 4096])
    din("wff2", [4, 4096, D])
    din("ident", [128, 128])
    if do_final:
        yout = nc.dram_tensor("yout", [D, 2048], F32, kind="ExternalOutput").ap()
    if store_state:
        xst_out = nc.dram_tensor("xst_out", [D, NXR], F32, kind="ExternalOutput").ap()

    es = ExitStack()
    TOTAL = 212800
    arena_t = es.enter_context(nc.sbuf_tensor("arena", [128, TOTAL // 2], BF16))
    psum_t = es.enter_context(nc.psum_tensor("ps", [128, 8, 512], F32))
    AR = Arena(arena_t, TOTAL)

    pers = Phase(AR, 0, TOTAL)
    XR = pers.buf([8, NXR], F32, "XR")
    xr_t = [[T(f"xr{c}_{j}") for j in range(NXR // 128)] for c in range(8)]
    ONES = pers.buf([128], BF16, "ONES")
    IDENT = pers.buf([128], BF16, "IDENT")
    ONESROW = pers.buf([512], BF16, "ONESROW")
    SCT = pers.buf([8, 2], BF16, "SCT")
    CVEC = pers.buf([16], F32, "CVEC")
    FINALG = pers.buf([8], F32, "FINALG")
    NORMG = pers.buf([4, 16], F32, "NORMG")
    MODR = [pers.buf([48, 2], F32, f"MODR{i}") for i in range(2)]
    MODA = [pers.buf([2, 8, 2], F32, f"MODA{i}") for i in range(2)]
    BMOD = [pers.buf([48], F32, f"BMOD{i}") for i in range(2)]
    MODRT = [[T(f"modr{i}_{p}") for p in range(2)] for i in range(2)]
    MODAT = [[T(f"moda{i}_{p}") for p in range(2)] for i in range(2)]
    LNVG = pers.buf([4], F32, "LNVG")
    BSPB = pers.buf([4, 128], F32, "BSPB")
    WSP = pers.buf([4, 128], BF16, "WSP")
    BPW1 = pers.buf([16], F32, "BPW1")
    BDW = pers.buf([8], F32, "BDW")
    LNCG = pers.buf([8], F32, "LNCG")
    LNCB = pers.buf([8], F32, "LNCB")
    WDW = pers.buf([8, 31], F32, "WDW")
    BPW2 = pers.buf([1024], BF16, "BPW2")
    PH0 = (pers.off + 63) // 64 * 64
    PHLIM = TOTAL

    ps_t = [T(f"psb{b}") for b in range(8)]
    bank_ctr = [0]

    reserved_banks = set()

    def bank():
        while True:
            b = bank_ctr[0] % 8
            bank_ctr[0] += 1
            if b not in reserved_banks:
                return b

    def xts(c, t0, n):
        return [xr_t[c][j] for j in range(t0 // 128, (t0 + n + 127) // 128)]

    def mm(out_ap, out_t, lhsT, rhs, reads, start, stop):
        pg.op("pe", lambda e, o=out_ap, l=lhsT, r=rhs, s=start, p=stop: e.matmul(o, l, r, start=s, stop=p),
              r=reads, w=[out_t])

    def act(out_ap, in_ap, func, r, w, bias=None, scale=None):
        def fn(e, o=out_ap, i=in_ap, f=func, b=bias, s=scale):
            kw = {}
            if b is not None:
                kw["bias"] = b
            if s is not None:
                kw["scale"] = s
            return e.activation(out=o, in_=i, func=f, **kw)
        pg.op("act", fn, r=r, w=w)

    def dma_w(out_ap, in_ap, w, eng="pool", r=()):
        pg.dma(eng, lambda e, o=out_ap, i=in_ap: e.dma_start(out=o, in_=i), r=r, w=w)

    dma_w(IDENT.ap, dr["ident"], [IDENT.t()])
    pg.op("dve", lambda e: e.memset(ONES.ap, 1.0), w=[ONES.t()])
    pg.op("dve", lambda e: e.memset(ONESROW.ap, 1.0), w=[ONESROW.t()])
    dma_w(CVEC.ap, dr["cvec"], [CVEC.t()], eng="sp")
    dma_w(FINALG.ap, dr["finalg"], [FINALG.t()], eng="sp")
    dma_w(NORMG.ap, dr["normg"].rearrange("l p f -> p l f"), [NORMG.t()], eng="sp")
    act(SCT.ap.rearrange("p a b -> p (a b)"), CVEC.ap, AF.Silu, r=[CVEC.t()], w=[SCT.t()])

    if load_state:
        src = dr["xst"].rearrange("(c p) t -> p c t", p=128)
        for (t0, n) in split_blocks(0, NXR, 512):
            dma_w(XR.ap[:, :, t0:t0 + n], src[:, :, t0:t0 + n],
                  [xr_t[c][j] for c in range(8) for j in range(t0 // 128, (t0 + n) // 128)], eng="sp")
    else:
        src = dr["xin"].rearrange("(c p) t -> p c t", p=128)
        dma_w(XR.ap[:, :, CTXO:CTXO + 256], src[:, :, NWIN:NWIN + 256],
              [xr_t[c][j] for c in range(8) for j in (19, 20)], eng="sp")
        late_x = []
        for (t0, n) in [(0, 256), (256, 256)] + split_blocks(512, NMAIN, 512):
            if t0 >= 512:
                late_x.append((t0, n))
                continue
            dma_w(XR.ap[:, :, t0:t0 + n], src[:, :, t0:t0 + n],
                  [xr_t[c][j] for c in range(8) for j in range(t0 // 128, (t0 + n) // 128)], eng="sp")

    def mod_part_gen(l, part, bufs, pcols):
        slot = l % 2
        wv = dr["wmod"][l].rearrange("(kc p) f -> p kc f", p=128)
        c0 = part * 24
        npc = pcols // 128
        npieces = 24 // npc
        b = bank()
        reserved_banks.add(b)
        R = len(bufs)

        def issue(k):
            wm = bufs[k % R]
            col = (c0 + k * npc) * 128
            dma_w(wm.ap, wv[:, :, col:col + pcols], [wm.t()])

        def matmuls(k):
            wm = bufs[k % R]
            for fl in range(npc):
                fc = k * npc + fl
                for kc in range(8):
                    mm(psum_t[:, b, fc * 2:fc * 2 + 2], ps_t[b], wm.ap[:, kc, fl * 128:(fl + 1) * 128],
                       SCT.ap[:, kc, :], [wm.t(), SCT.t()], kc == 0, kc == 7)

        if part == 0:
            dma_w(BMOD[slot].ap, dr["bmod"][l], [BMOD[slot].t()], eng="sp")
        for k in range(min(R - 1, npieces)):
            issue(k)
        if R > 1:
            yield
        for k in range(npieces):
            if R == 1:
                issue(k) if k == 0 else None
            elif k + R - 1 < npieces:
                issue(k + R - 1)
            matmuls(k)
            if R == 1 and k + 1 < npieces:
                issue(k + 1)
            yield
        pv = psum_t[:, b, 0:48].rearrange("p (a b) -> p a b", b=2)
        mt = MODRT[slot][part]
        for sgi in range(2):
            pg.op("dve", lambda e, sgi=sgi, pv=pv, slot=slot: e.tensor_tensor(
                out=MODR[slot].ap[:, c0:c0 + 24, sgi], in0=pv[:, :, sgi], in1=BMOD[slot].ap[:, c0:c0 + 24], op=ALU.add),
                r=[ps_t[b], BMOD[slot].t()], w=[mt])
        reserved_banks.discard(b)
        which = part
        sb = 8 if which == 0 else 32
        for sgi in range(2):
            pg.op("dve", lambda e, which=which, sgi=sgi, sb=sb, slot=slot, l=l: e.scalar_tensor_tensor(
                out=MODA[slot].ap[:, which, :, sgi], in0=MODR[slot].ap[:, sb:sb + 8, sgi], scalar=1.0,
                in1=NORMG.ap[:, l, which * 8:which * 8 + 8], op0=ALU.add, op1=ALU.mult),
                r=[mt, NORMG.t()], w=[MODAT[slot][which]])

    def run_gen(g, n=None):
        k = 0
        while n is None or k < n:
            try:
                next(g)
            except StopIteration:
                return True
            k += 1
        return False

    def m_sh(l, which, c, seg):
        return MODR[l % 2].ap[:, (0 if which == 0 else 24) + c, seg:seg + 1]

    def m_g(l, which, c, seg):
        return MODR[l % 2].ap[:, (16 if which == 0 else 40) + c, seg:seg + 1]

    def m_a(l, which, c, seg):
        return MODA[l % 2].ap[:, which, c, seg:seg + 1]

    def mod_ts(l, which=None):
        if which is None:
            return MODRT[l % 2] + MODAT[l % 2]
        return [MODRT[l % 2][which], MODAT[l % 2][which]]

    class NormBufs:
        def __init__(self, ph, nmax, nsq=2, sq_eng="act"):
            self.nsq = nsq
            self.sq_eng = sq_eng
            self.SQ = ph.ring(nsq, [nmax], BF16, "SQ")
            self.RT = ph.buf([nmax], F32, "RT")
            self.NT = ph.ring(2, [nmax], F32, "NT")
            self.ctr = 0
            self.sqctr = 0

    def rstd_psum(nb, src_fn, src_ts_fn, n):
        b = bank()
        pap = psum_t[:, b, 0:n]
        sqs = []
        for c in range(8):
            sq = nb.SQ[nb.sqctr % nb.nsq]
            nb.sqctr += 1
            sqs.append(sq)
            if nb.sq_eng == "pool":
                pg.op("pool", lambda e, sq=sq, c=c: e.tensor_tensor(out=sq.ap[:, 0:n], in0=src_fn(c), in1=src_fn(c),
                                                                 op=ALU.mult), r=src_ts_fn(c), w=[sq.t()])
            else:
                act(sq.ap[:, 0:n], src_fn(c), AF.Square, r=src_ts_fn(c), w=[sq.t()])
            if nb.nsq <= 2:
                mm(pap, ps_t[b], ONES.ap, sq.ap[:, 0:n], [ONES.t(), sq.t()], c == 0, c == 7)
            elif c >= nb.nsq - 1:
                cc = c - (nb.nsq - 1)
                mm(pap, ps_t[b], ONES.ap, sqs[cc].ap[:, 0:n], [ONES.t(), sqs[cc].t()], cc == 0, cc == 7)
        if nb.nsq > 2:
            for cc in range(8 - (nb.nsq - 1), 8):
                mm(pap, ps_t[b], ONES.ap, sqs[cc].ap[:, 0:n], [ONES.t(), sqs[cc].t()], cc == 0, cc == 7)
        act(nb.RT.ap[:, 0:n], pap, AF.Ln, r=[ps_t[b]], w=[nb.RT.t()], bias=EPS, scale=1.0 / D)
        act(pap, nb.RT.ap[:, 0:n], AF.Exp, r=[nb.RT.t()], w=[ps_t[b]], scale=-0.5)
        return b, pap

    def norm_mod(nb, l, which, seg, src_fn, src_ts_fn, n, dst_fn, dst_ts_fn):
        b, pap = rstd_psum(nb, src_fn, src_ts_fn, n)
        for c in range(8):
            nt = nb.NT[nb.ctr % 2]
            nb.ctr += 1
            pg.op("dve", lambda e, c=c, nt=nt, pap=pap: e.scalar_tensor_tensor(
                out=nt.ap[:, 0:n], in0=src_fn(c), scalar=m_a(l, which, c, seg), in1=pap,
                op0=ALU.mult, op1=ALU.mult),
                r=src_ts_fn(c) + [ps_t[b]] + mod_ts(l, which), w=[nt.t()])
            act(dst_fn(c), nt.ap[:, 0:n], AF.Identity, r=[nt.t()] + mod_ts(l, which), w=dst_ts_fn(c),
                bias=m_sh(l, which, c, seg))

    def xr_update(l, which, seg, dc, t0, n, pap, pb):
        pg.op("dve", lambda e, dc=dc, t0=t0, n=n, pap=pap: e.scalar_tensor_tensor(
            out=XR.ap[:, dc, t0:t0 + n], in0=pap, scalar=m_g(l, which, dc, seg), in1=XR.ap[:, dc, t0:t0 + n],
            op0=ALU.mult, op1=ALU.add),
            r=[ps_t[pb]] + xts(dc, t0, n) + mod_ts(l, which), w=xts(dc, t0, n))

    pending_gens = []
    first_layer_part1 = [None]

    def emit_ffn(l, next_mod, fuse_final=False):
        cfg = LAYER_CFG[l]
        ph = Phase(AR, PH0, PHLIM)
        HTA = ph.buf([8, NXR], BF16, "HTA")
        _fl = AR.split(HTA.t(), [f"hta{c}_{j}" for c in range(8) for j in range(NXR // 128)])
        hta_t = [[_fl[c * (NXR // 128) + j] for j in range(NXR // 128)] for c in range(8)]
        W1G = ph.ring(2, [8, 512], BF16, "W1G")
        W2G = ph.ring(2, [4, 1024], BF16, "W2G")
        H1R = ph.ring(2, [512], BF16, "H1R")
        H1 = ph.ring(2, [4, 512], BF16, "H1", nt=4)
        nb = NormBufs(ph, 512)
        blocks = []
        for (a, bnd, seg) in cfg["ffn"]:
            for (t0, n) in split_blocks(a, bnd, 512):
                blocks.append((t0, n, seg))
        def do_norm(bi):
            t0, n, seg = blocks[bi]
            norm_mod(nb, l, 1, seg,
                     lambda c, t0=t0, n=n: XR.ap[:, c, t0:t0 + n],
                     lambda c, t0=t0, n=n: xts(c, t0, n), n,
                     lambda c, t0=t0, n=n: HTA.ap[:, c, t0:t0 + n],
                     lambda c, t0=t0, n=n: [hta_t[c][j] for j in range(t0 // 128, (t0 + n + 127) // 128)])

        gens = []
        OBF = None
        if fuse_final:
            OBF = ph.buf([8, 512], F32, "OBF", nt=8)
            yv = yout.rearrange("(c p) t -> p c t", p=128)

        def final_block(bi):
            t0, n, seg = blocks[bi]
            b, pap = rstd_psum(nb, lambda c, t0=t0, n=n: XR.ap[:, c, t0:t0 + n],
                               lambda c, t0=t0, n=n: xts(c, t0, n), n)
            for c in range(8):
                pg.op("dve", lambda e, c=c, pap=pap, t0=t0, n=n: e.scalar_tensor_tensor(
                    out=OBF.ap[:, c, 0:n], in0=XR.ap[:, c, t0:t0 + n], scalar=FINALG.ap[:, c:c + 1], in1=pap,
                    op0=ALU.mult, op1=ALU.mult),
                    r=xts(c, t0, n) + [ps_t[b], FINALG.t()], w=[OBF.t(c)])
            pg.dma("sp", lambda e, t0=t0, n=n: e.dma_start(out=yv[:, :, t0:t0 + n], in_=OBF.ap[:, :, 0:n]),
                   r=OBF.ts, w=[])

        if next_mod is not None:
            WM = ph.ring(2, [8, 512], BF16, "WM")
            gens = [mod_part_gen(next_mod, 0, WM, 512), mod_part_gen(next_mod, 1, WM, 512)]
        for extra_gen in pending_gens:
            gens.insert(0, extra_gen)
        del pending_gens[:]

        def step_gens():
            while gens:
                if run_gen(gens[0], 1):
                    gens.pop(0)
                    continue
                return
        w1v = dr["wff1"][l].rearrange("(kc p) f -> p kc f", p=128)
        w2v = dr["wff2"][l].rearrange("(fc p) d -> p fc d", p=128)
        h1ctr = [0]

        groups = [(4 * k, 4) for k in range(8)]
        NG = len(groups)

        def load_group(g):
            f0, nfc = groups[g]
            dma_w(W1G[g % 2].ap[:, :, 0:nfc * 128], w1v[:, :, f0 * 128:(f0 + nfc) * 128], [W1G[g % 2].t()])
            dma_w(W2G[g % 2].ap[:, 0:nfc, :], w2v[:, f0:f0 + nfc, :], [W2G[g % 2].t()])

        load_group(0)
        for g in range(NG):
            w1 = W1G[g % 2]
            w2 = W2G[g % 2]
            nfc = groups[g][1]
            if g + 1 < NG:
                load_group(g + 1)

            def stage_a(bi):
                t0, n, seg = blocks[bi]
                hs = H1[h1ctr[0] % 2]
                for fc in range(nfc):
                    b = bank()
                    pap = psum_t[:, b, 0:n]
                    for kc in range(8):
                        mm(pap, ps_t[b], w1.ap[:, kc, fc * 128:(fc + 1) * 128], HTA.ap[:, kc, t0:t0 + n],
                           [w1.t()] + [hta_t[kc][j] for j in range(t0 // 128, (t0 + n + 127) // 128)],
                           kc == 0, kc == 7)
                    hr = H1R[(h1ctr[0] * 4 + fc) % 2]
                    act(hr.ap[:, 0:n], pap, AF.Relu, r=[ps_t[b]], w=[hr.t()])
                    act(hs.ap[:, fc, 0:n], hr.ap[:, 0:n], AF.Square, r=[hr.t()], w=[hs.t(fc)])
                h1ctr[0] += 1
                return hs

            def stage_b(bi, hs):
                t0, n, seg = blocks[bi]
                for dc in range(8):
                    b = bank()
                    pap = psum_t[:, b, 0:n]
                    for fc in range(nfc):
                        mm(pap, ps_t[b], w2.ap[:, fc, dc * 128:(dc + 1) * 128], hs.ap[:, fc, 0:n],
                           [w2.t(), hs.t(fc)], fc == 0, fc == nfc - 1)
                    xr_update(l, 1, seg, dc, t0, n, pap, b)

            prev = None
            if g == 0:
                do_norm(0)
                if len(blocks) > 1:
                    do_norm(1)
            for bi in range(len(blocks)):
                if g == 0 and bi + 2 < len(blocks):
                    do_norm(bi + 2)
                hs = stage_a(bi)
                if prev is not None:
                    stage_b(prev[0], prev[1])
                    if fuse_final and g == NG - 1:
                        final_block(prev[0])
                    if g >= 1:
                        step_gens()
                prev = (bi, hs)
            stage_b(prev[0], prev[1])
            if fuse_final and g == NG - 1:
                final_block(prev[0])
        while gens:
            step_gens()

    def emit_even(l):
        cfg = LAYER_CFG[l]
        i = l // 2
        NQ, NKV = cfg["nq"], cfg["nkv"]
        ph = Phase(AR, PH0, PHLIM)
        WIN = ph.buf([8, 2560], BF16, "WIN")
        win_t = AR.split(WIN.t(), [f"win{b}" for b in range(5)])
        WOUT = ph.buf([8, 1024], BF16, "WOUT")
        TINT = ph.buf([8, 640], BF16, "TINT", nt=8)
        tbs_off = ph.off
        TBS = ph.buf([512], F32, "TBS")
        TBE = ph.ring(2, [512], BF16, "TBE")
        KR = ph.ring(6, [4, 128], BF16, "KR")
        VR = ph.ring(6, [512], BF16, "VR")
        KCX = ph.ring(2, [4, 128], BF16, "KCX")
        VCX = ph.ring(2, [512], BF16, "VCX")
        HT = ph.ring(2, [8, 128], BF16, "HT", nt=8)
        nb = NormBufs(ph, 128, nsq=4, sq_eng="pool")
        QR = ph.ring(4, [4, 128], BF16, "QR")
        ATR = ph.ring(4, [4, 128], BF16, "ATR")
        UTR = ph.ring(2, [4, 128], BF16, "UT")
        GGB = ph.buf([512], F32, "GGB")
        G2B = ph.buf([512], BF16, "G2B")
        GN = ph.buf([512], BF16, "GN")
        ST = ph.buf([8, 4], F32, "ST", nt=8)
        EE = ph.ring(2, [7, 128], BF16, "EE")
        RC = ph.ring(2, [128], F32, "RC")
        BT = ph.buf([4, 128], BF16, "BT", nt=4)
        even_gens = []
        if first_layer_part1[0] == l:
            WMS = ph.buf([8, 128], BF16, "WMS")
            even_gens.append(mod_part_gen(l, 1, [WMS], 128))
            first_layer_part1[0] = None

        wv = dr["win"][i].rearrange("(kc p) f -> p kc f", p=128)
        for blk in (3, 4, 2, 0, 1):
            dma_w(WIN.ap[:, :, blk * 512:(blk + 1) * 512], wv[:, :, blk * 512:(blk + 1) * 512], [win_t[blk]])
        dma_w(WOUT.ap, dr["wout"][i].rearrange("(kc p) f -> p kc f", p=128), [WOUT.t()])
        dma_w(LNVG.ap, dr["lnvg"][i], [LNVG.t()], eng="sp")
        dma_w(BSPB.ap.rearrange("p a b -> p (a b)"), dr["bspb"][i], [BSPB.t()], eng="sp")
        dma_w(WSP.ap.rearrange("p a b -> p (a b)"), dr["wspT"][i], [WSP.t()])
        for h in range(8):
            for hf in range(2):
                dma_w(TBS.ap[:, 0:320], dr["tint"][i][:, h * 640 + hf * 320:h * 640 + (hf + 1) * 320],
                      [TBS.t()], eng="sp")
                act(TINT.ap[:, h, hf * 320:(hf + 1) * 320], TBS.ap[:, 0:320], AF.Exp, r=[TBS.t()], w=[TINT.t(h)])

        GG = GGB.ap
        G2 = G2B.ap

        class _GGT:
            @staticmethod
            def t(k):
                return GGB.t() if k == 0 else G2B.t()
        GGT = _GGT
        s1, s2, mean, msq, var, sd, rstd, nmr = [ST.ap[:, k, :] for k in range(8)]

        class WT:
            pass

        def table_int(h):
            return TINT.ap[:, h, :], [TINT.t(h)]

        def make_table_bnd(j):
            def fn(h):
                tb = TBE[h % 2]
                dma_w(TBS.ap[:, 0:512], dr["tbnd"][i][j][:, h * 512:(h + 1) * 512], [TBS.t()], eng="sp")
                act(tb.ap, TBS.ap[:, 0:512], AF.Exp, r=[TBS.t()], w=[tb.t()])
                return tb.ap, [tb.t()]
            return fn

        ckeys = [(KCX[0], VCX[0]), (KCX[1], VCX[1])]
        W = []
        if cfg["ctx"] in ("full", "kv"):
            for ct in range(2):
                w = WT()
                w.kind, w.idx, w.seg = "ctx", ct, 1
                w.t0 = CTXO + ct * 128
                w.xe = False
                w.kdst, w.vdst = KCX[ct], VCX[ct]
                w.has_q = cfg["ctx"] == "full"
                w.keys, w.nloc, w.table_fn = ckeys, 0, None
                W.append(w)
        for t in range(NKV):
            w = WT()
            w.kind, w.idx, w.seg = "main", t, 0
            w.t0 = t * 128
            w.xe = t >= 19
            w.kdst, w.vdst = KR[t % 6], VR[t % 6]
            w.has_q = t < NQ
            if w.has_q:
                if t < 2:
                    kl = [0, 1, 2, 3]
                    w.table_fn = make_table_bnd(t)
                else:
                    kl = [t - 2, t - 1, t, t + 1, t + 2]
                    w.table_fn = table_int
                w.keys = [(KR[k % 6], VR[k % 6]) for k in kl] + ckeys
                w.nloc = len(kl)
            W.append(w)
        for wi, w in enumerate(W):
            w.wi = wi
            w.ht = HT[wi % 2]
            w.qdst = QR[wi % 4] if w.has_q else None
            w.adst = ATR[wi % 4] if w.has_q else None
            w.ut = UTR[wi % 2]
        XEb = [None]

        def stage_A(w):
            if w.xe:
                if XEb[0] is None:
                    XEb[0] = AR.buf(tbs_off, [8, 128], F32, "XE", nt=8)
                XE = XEb[0]
                xsrc = dr["xin"].rearrange("(c p) t -> p c t", p=128)
                dma_w(XE.ap, xsrc[:, :, w.t0:w.t0 + 128], XE.ts, eng="sp")
                src_fn = lambda c: XE.ap[:, c, :]
                src_ts = lambda c: [XE.t(c)]
            else:
                src_fn = lambda c, t0=w.t0: XR.ap[:, c, t0:t0 + 128]
                src_ts = lambda c, t0=w.t0: xts(c, t0, 128)
            norm_mod(nb, l, 0, w.seg, src_fn, src_ts, 128,
                     lambda c: w.ht.ap[:, c, :], lambda c: [w.ht.t(c)])

        def proj_fm(ht, blk, dst, func, scale=None):
            b = bank()
            for hc in range(4):
                for kc in range(8):
                    mm(psum_t[:, b, hc * 128:(hc + 1) * 128], ps_t[b],
                       WIN.ap[:, kc, blk * 512 + hc * 128: blk * 512 + (hc + 1) * 128], ht.ap[:, kc, :],
                       [win_t[blk], ht.t(kc)], kc == 0, kc == 7)
            act(dst.ap.rearrange("p a b -> p (a b)"), psum_t[:, b, :], func, r=[ps_t[b]], w=[dst.t()], scale=scale)

        def proj_tm(ht, blk):
            b = bank()
            for kc in range(8):
                mm(psum_t[:, b, :], ps_t[b], ht.ap[:, kc, :], WIN.ap[:, kc, blk * 512:(blk + 1) * 512],
                   [win_t[blk], ht.t(kc)], kc == 0, kc == 7)
            return b

        def B_k(w):
            proj_fm(w.ht, 3, w.kdst, AF.Identity)

        def B_v(w):
            b = proj_tm(w.ht, 4)
            pg.op("dve", lambda e, b=b, vdst=w.vdst: e.tensor_copy(out=vdst.ap, in_=psum_t[:, b, :]),
                  r=[ps_t[b]], w=[w.vdst.t()])

        def B_q(w):
            if w.has_q:
                proj_fm(w.ht, 2, w.qdst, AF.Identity, scale=0.125)

        def B_u(w):
            if w.has_q:
                proj_fm(w.ht, 0, w.ut, AF.Gelu_apprx_tanh)

        def B_g(w):
            if w.has_q:
                b = proj_tm(w.ht, 1)
                act(GG, psum_t[:, b, :], AF.Gelu_apprx_tanh, r=[ps_t[b]], w=[GGT.t(0)])

        def C1(w):
            if not w.has_q:
                return
            pg.op("pool", lambda e: e.tensor_tensor(out=G2, in0=GG, in1=GG, op=ALU.mult),
                  r=[GGT.t(0)], w=[GGT.t(1)])
            pg.op("dve", lambda e: e.tensor_reduce(out=s1, in_=GG.rearrange("p (a b) -> p a b", a=4), axis=AX.X, op=ALU.add),
                  r=[GGT.t(0)], w=[ST.t(0)])
            pg.op("dve", lambda e: e.tensor_reduce(out=s2, in_=G2.rearrange("p (a b) -> p a b", a=4), axis=AX.X, op=ALU.add),
                  r=[GGT.t(1)], w=[ST.t(1)])
            pg.op("dve", lambda e: e.tensor_scalar(out=mean, in0=s1, scalar1=1.0 / 128, scalar2=None, op0=ALU.mult),
                  r=[ST.t(0)], w=[ST.t(2)])
            pg.op("dve", lambda e: e.tensor_tensor(out=msq, in0=mean, in1=mean, op=ALU.mult),
                  r=[ST.t(2)], w=[ST.t(3)])
            pg.op("dve", lambda e: e.scalar_tensor_tensor(out=var, in0=s2, scalar=1.0 / 128, in1=msq,
                                                         op0=ALU.mult, op1=ALU.subtract),
                  r=[ST.t(1), ST.t(3)], w=[ST.t(4)])
            act(sd, var, AF.Ln, r=[ST.t(4)], w=[ST.t(5)], bias=EPS)
            act(rstd, sd, AF.Exp, r=[ST.t(5)], w=[ST.t(6)], scale=-0.5)
            pg.op("dve", lambda e: e.scalar_tensor_tensor(out=nmr, in0=mean, scalar=-1.0, in1=rstd,
                                                         op0=ALU.mult, op1=ALU.mult),
                  r=[ST.t(2), ST.t(6)], w=[ST.t(7)])
            for g in range(4):
                pg.op("pool", lambda e, g=g: e.tensor_scalar(
                    out=GN.ap[:, g * 128:(g + 1) * 128], in0=GG[:, g * 128:(g + 1) * 128],
                    scalar1=rstd[:, g:g + 1], scalar2=nmr[:, g:g + 1], op0=ALU.mult, op1=ALU.add),
                    r=[GGT.t(0), ST.t(6), ST.t(7)], w=[GN.t()])

        def C2(w):
            if not w.has_q:
                return
            b = bank()
            for g in range(4):
                mm(psum_t[:, b, g * 128:(g + 1) * 128], ps_t[b], GN.ap[:, g * 128:(g + 1) * 128], WSP.ap[:, g, :],
                   [GN.t(), WSP.t()], True, True)
            for g in range(4):
                pg.op("dve", lambda e, g=g, b=b: e.scalar_tensor_tensor(
                    out=G2[:, g * 128:(g + 1) * 128], in0=psum_t[:, b, g * 128:(g + 1) * 128],
                    scalar=LNVG.ap[:, g:g + 1], in1=BSPB.ap[:, g, :], op0=ALU.mult, op1=ALU.add),
                    r=[ps_t[b], LNVG.t(), BSPB.t()], w=[GGT.t(1)])
            pg.op("dve", lambda e, w=w: e.tensor_tensor(out=w.adst.ap.rearrange("p a b -> p (a b)"), in0=G2,
                                                       in1=w.ut.ap.rearrange("p a b -> p (a b)"), op=ALU.mult),
                  r=[GGT.t(1), w.ut.t()], w=[w.adst.t()])

        ectr = [0]

        def D_scores(w, h):
            hp, pbase = h // 2, (h % 2) * 64
            nk = len(w.keys)
            bs = [bank(), bank()]
            for ki, (kb, vb) in enumerate(w.keys):
                b = bs[ki // 4]
                mm(psum_t[:, b, (ki % 4) * 128:(ki % 4 + 1) * 128], ps_t[b],
                   kb.ap[pbase:pbase + 64, hp, :], w.qdst.ap[pbase:pbase + 64, hp, :],
                   [kb.t(), w.qdst.t()], True, True)
            ee = EE[ectr[0] % 2]
            ectr[0] += 1
            n0 = min(nk, 4)
            act(ee.ap[:, 0:n0, :].rearrange("p a b -> p (a b)"), psum_t[:, bs[0], 0:n0 * 128], AF.Exp,
                r=[ps_t[bs[0]]], w=[ee.t()])
            if nk > 4:
                act(ee.ap[:, 4:nk, :].rearrange("p a b -> p (a b)"), psum_t[:, bs[1], 0:(nk - 4) * 128], AF.Exp,
                    r=[ps_t[bs[1]]], w=[ee.t()])
            if w.nloc > 0:
                tap, tts = w.table_fn(h)
                nloc = w.nloc
                pg.op("dve", lambda e, ee=ee, tap=tap, nloc=nloc: e.tensor_tensor(
                    out=ee.ap[:, 0:nloc, :].rearrange("p a b -> p (a b)"),
                    in0=ee.ap[:, 0:nloc, :].rearrange("p a b -> p (a b)"), in1=tap, op=ALU.mult),
                    r=[ee.t()] + tts, w=[ee.t()])
            w.ee[h] = ee

        def D_pv(w, h):
            hp, pbase = h // 2, (h % 2) * 64
            nk = len(w.keys)
            if h % 2 == 0:
                w.pvb = bank()
            b = w.pvb
            ee = w.ee[h]
            for ki, (kb, vb) in enumerate(w.keys):
                mm(psum_t[pbase:pbase + 64, b, 0:128], ps_t[b], vb.ap[:, h * 64:(h + 1) * 64], ee.ap[:, ki, :],
                   [vb.t(), ee.t()], ki == 0, ki == nk - 1)
            for ki, (kb, vb) in enumerate(w.keys):
                mm(psum_t[pbase:pbase + 64, b, 128:256], ps_t[b], ONES.ap[:, 0:64], ee.ap[:, ki, :],
                   [ONES.t(), ee.t()], ki == 0, ki == nk - 1)
            if h % 2 == 1:
                rc = RC[hp % 2]
                act(rc.ap, psum_t[:, b, 128:256], AF.Ln, r=[ps_t[b]], w=[rc.t()])
                act(rc.ap, rc.ap, AF.Exp, r=[rc.t()], w=[rc.t()], scale=-1.0)
                pg.op("dve", lambda e, b=b, rc=rc, hp=hp: e.tensor_tensor(
                    out=BT.ap[:, hp, :], in0=psum_t[:, b, 0:128], in1=rc.ap, op=ALU.mult),
                    r=[ps_t[b], rc.t()], w=[BT.t(hp)])

        def D_out(w):
            for half in range(2):
                b = bank()
                for dl in range(4):
                    dc = half * 4 + dl
                    pap = psum_t[:, b, dl * 128:(dl + 1) * 128]
                    for kc in range(8):
                        if kc < 4:
                            rhs, rt = w.adst.ap[:, kc, :], w.adst.t()
                        else:
                            rhs, rt = BT.ap[:, kc - 4, :], BT.t(kc - 4)
                        mm(pap, ps_t[b], WOUT.ap[:, kc, dc * 128:(dc + 1) * 128], rhs, [WOUT.t(), rt], kc == 0, kc == 7)
                for dl in range(4):
                    dc = half * 4 + dl
                    xr_update(l, 0, w.seg, dc, w.t0, 128, psum_t[:, b, dl * 128:(dl + 1) * 128], b)

        NW = len(W)
        LAG = 3
        pend_out = []
        stage_A(W[0])
        for s in range(NW + LAG):
            wB = W[s] if s < NW else None
            wA = W[s + 1] if s + 1 < NW else None
            wC = W[s - 1] if 1 <= s <= NW else None
            wD = W[s - LAG] if 0 <= s - LAG < NW and W[s - LAG].has_q else None
            if wD is not None:
                wD.ee = {}
            serial = False
            if wD is not None and wB is not None and wB.kind == "main" and wD.kind == "main":
                need = 3 if wD.idx < 2 else wD.idx + 2
                if need >= wB.idx:
                    serial = True
            if wA is not None:
                stage_A(wA)
            if even_gens and s >= 1:
                if run_gen(even_gens[0], 1):
                    even_gens.pop(0)
            if wC is not None:
                C1(wC)
            bgroups = []
            if wB is not None:
                bgroups = [lambda: B_k(wB), lambda: B_v(wB), lambda: B_q(wB), lambda: B_u(wB), lambda: B_g(wB)]
            if serial:
                for f in bgroups:
                    f()
                bgroups = []
            extra = list(bgroups)
            if wC is not None:
                extra.append(lambda: C2(wC))
            if wD is None:
                for f in extra:
                    f()
                if pend_out:
                    D_out(pend_out.pop())
                continue
            D_scores(wD, 0)
            if pend_out:
                D_out(pend_out.pop())
            for h in range(8):
                if extra:
                    extra.pop(0)()
                if h + 1 < 8:
                    D_scores(wD, h + 1)
                D_pv(wD, h)
            for f in extra:
                f()
            pend_out.append(wD)
        if pend_out:
            D_out(pend_out.pop())
        while even_gens:
            if run_gen(even_gens[0], None):
                even_gens.pop(0)

    def emit_odd(l):
        cfg = LAYER_CFG[l]
        i = l // 2
        ny, nout = cfg["ny"], cfg["nout"]
        ph = Phase(AR, PH0, PHLIM)
        YW = 16 + NMAIN + 16
        YM = ph.buf([8, YW], BF16, "YM")
        YC = ph.buf([8, 288], BF16, "YC")
        _fl = AR.split(YM.t(), [f"ym{c}_{j}" for c in range(8) for j in range(YW // 128 + 1)])
        ym_t = [[_fl[c * (YW // 128 + 1) + j] for j in range(YW // 128 + 1)] for c in range(8)]
        yc_t = AR.split(YC.t(), [f"yc{c}" for c in range(8)])
        base2 = ph.off
        WPW1 = ph.buf([8, 2048], BF16, "WPW1")
        w1_t = AR.split(WPW1.t(), [f"wpw1_{b}" for b in range(4)])
        HT = ph.ring(2, [8, 512], BF16, "HT", nt=8)
        nb = NormBufs(ph, 512)
        SG = ph.ring(2, [512], F32, "SG")

        wv = dr["wpw1"][i].rearrange("(kc p) f -> p kc f", p=128)
        for blk in (0, 2, 1, 3):
            dma_w(WPW1.ap[:, :, blk * 512:(blk + 1) * 512], wv[:, :, blk * 512:(blk + 1) * 512], [w1_t[blk]])
        dma_w(BPW1.ap, dr["bpw1"][i], [BPW1.t()], eng="sp")
        dma_w(BDW.ap, dr["bdw"][i], [BDW.t()], eng="sp")
        dma_w(LNCG.ap, dr["lncg"][i], [LNCG.t()], eng="sp")
        dma_w(LNCB.ap, dr["lncb"][i], [LNCB.t()], eng="sp")
        dma_w(WDW.ap.rearrange("p a b -> p (a b)"), dr["wdw"][i], [WDW.t()], eng="sp")
        dma_w(BPW2.ap[0:1, :], dr["bpw2"][i], [BPW2.t()])
        pg.op("pool", lambda e: e.memset(YM.ap, 0.0), w=[ym_t[c][j] for c in range(8) for j in range(YW // 128 + 1)])
        pg.op("pool", lambda e: e.memset(YC.ap, 0.0), w=yc_t)

        def ymts(c, col0, n):
            return [ym_t[c][j] for j in range(col0 // 128, (col0 + n + 127) // 128)]

        segs = [(0, ny, 0, YM, ymts)]
        if cfg["ctx"]:
            segs.append((CTXO, CTXO + 256, 1, YC, lambda c, col0, n: [yc_t[c]]))
        p1blocks = []
        for (a, bnd, seg, Y, ytf) in segs:
            for (t0, n) in split_blocks(a, bnd, 512):
                p1blocks.append((a, seg, Y, ytf, t0, n))

        def p1_norm(bi):
            a, seg, Y, ytf, t0, n = p1blocks[bi]
            ht = HT[bi % 2]
            norm_mod(nb, l, 0, seg,
                     lambda c, t0=t0, n=n: XR.ap[:, c, t0:t0 + n],
                     lambda c, t0=t0, n=n: xts(c, t0, n), n,
                     lambda c, ht=ht, n=n: ht.ap[:, c, 0:n], lambda c, ht=ht: [ht.t(c)])

        p1_norm(0)
        for bi in range(len(p1blocks)):
            if True:
                a, seg, Y, ytf, t0, n = p1blocks[bi]
                ht = HT[bi % 2]
                if bi + 1 < len(p1blocks):
                    p1_norm(bi + 1)
                ycol = 16 + (t0 - a)
                for c in range(8):
                    ba = bank()
                    bg = bank()
                    for kc in range(8):
                        mm(psum_t[:, ba, 0:n], ps_t[ba], WPW1.ap[:, kc, c * 128:(c + 1) * 128], ht.ap[:, kc, 0:n],
                           [w1_t[c // 4], ht.t(kc)], kc == 0, kc == 7)
                    for kc in range(8):
                        mm(psum_t[:, bg, 0:n], ps_t[bg], WPW1.ap[:, kc, 1024 + c * 128:1024 + (c + 1) * 128],
                           ht.ap[:, kc, 0:n], [w1_t[2 + c // 4], ht.t(kc)], kc == 0, kc == 7)
                    sg = SG[c % 2]
                    act(sg.ap[:, 0:n], psum_t[:, bg, 0:n], AF.Sigmoid, r=[ps_t[bg], BPW1.t()], w=[sg.t()],
                        bias=BPW1.ap[:, 8 + c:9 + c])
                    pg.op("dve", lambda e, c=c, ba=ba, sg=sg, n=n, Y=Y, ycol=ycol: e.scalar_tensor_tensor(
                        out=Y.ap[:, c, ycol:ycol + n], in0=psum_t[:, ba, 0:n], scalar=BPW1.ap[:, c:c + 1],
                        in1=sg.ap[:, 0:n], op0=ALU.add, op1=ALU.mult),
                        r=[ps_t[ba], sg.t(), BPW1.t()], w=ytf(c, ycol, n))

        ph2 = Phase(AR, base2, PHLIM)
        NDVE = 8
        NPE = 31 - NDVE
        WPW2 = ph2.buf([8, 1024], BF16, "WPW2")
        DG = ph2.ring(4, [NPE, 128], BF16, "DG", nt=NPE)
        CVB = ph2.buf([8, 512], BF16, "CVB", nt=8)
        SQB = ph2.buf([8, 512], BF16, "SQB", nt=8)
        MS = ph2.buf([512], F32, "MS")
        DT = ph2.ring(2, [512], F32, "DT")
        HN = ph2.buf([8, 512], BF16, "HN", nt=8)
        TMPF = ph2.buf([512], F32, "TMPF")
        dma_w(WPW2.ap, dr["wpw2"][i].rearrange("(kc p) f -> p kc f", p=128), [WPW2.t()])
        tiles2 = []
        for (t0, n) in split_blocks(0, nout, 512):
            tiles2.append((0, t0, n, YM, ymts, 16 + t0))
        if cfg["ctx"]:
            tiles2.append((1, CTXO, 256, YC, lambda c, col0, n: [yc_t[c]], 16))
        dgmap = {}
        dctr = [0]

        def build(ti, cp):
            for c in (2 * cp, 2 * cp + 1):
                dg = DG[dctr[0] % 4]
                dctr[0] += 1
                dgmap[(ti, c)] = dg
                for k in range(NDVE, 31):
                    kk = k - NDVE
                    if k % 2 == 0:
                        pg.op("pool", lambda e, dg=dg, k=k, kk=kk, c=c: e.tensor_scalar(
                            out=dg.ap[:, kk, :], in0=IDENT.ap, scalar1=WDW.ap[:, c, k:k + 1], scalar2=0.0,
                            op0=ALU.mult, op1=ALU.add), r=[IDENT.t(), WDW.t()], w=[dg.t(kk)])
                    else:
                        act(dg.ap[:, kk, :], IDENT.ap, AF.Identity, r=[IDENT.t(), WDW.t()], w=[dg.t(kk)],
                            scale=WDW.ap[:, c, k:k + 1])

        def conv_rest(ti, cp):
            seg, t0, n, Y, ytf, ycol = tiles2[ti]
            cs = (2 * cp, 2 * cp + 1)
            bds = {c: bank() for c in cs}
            for k in range(NDVE):
                for c in cs:
                    col = ycol + k - 15
                    accD = psum_t[:, bds[c], 0:n]
                    if k == 0:
                        pg.op("dve", lambda e, c=c, col=col, accD=accD, Y=Y, n=n: e.tensor_scalar(
                            out=accD, in0=Y.ap[:, c, col:col + n], scalar1=WDW.ap[:, c, 0:1], scalar2=None,
                            op0=ALU.mult), r=ytf(c, col, n) + [WDW.t()], w=[ps_t[bds[c]]])
                    else:
                        pg.op("dve", lambda e, c=c, col=col, accD=accD, Y=Y, n=n, k=k: e.scalar_tensor_tensor(
                            out=accD, in0=Y.ap[:, c, col:col + n], scalar=WDW.ap[:, c, k:k + 1], in1=accD,
                            op0=ALU.mult, op1=ALU.add),
                            r=ytf(c, col, n) + [WDW.t(), ps_t[bds[c]]], w=[ps_t[bds[c]]])
            for c in cs:
                dg = dgmap[(ti, c)]
                b = bank()
                for k in range(NDVE, 31):
                    col = ycol + k - 15
                    mm(psum_t[:, b, 0:n], ps_t[b], dg.ap[:, k - NDVE, :], Y.ap[:, c, col:col + n],
                       [dg.t(k - NDVE)] + ytf(c, col, n), k == NDVE, k == 30)
                act(TMPF.ap[:, 0:n], psum_t[:, b, 0:n], AF.Identity, r=[ps_t[b], BDW.t()], w=[TMPF.t()],
                    bias=BDW.ap[:, c:c + 1])
                pg.op("dve", lambda e, c=c, n=n, bd=bds[c]: e.tensor_tensor(
                    out=CVB.ap[:, c, 0:n], in0=TMPF.ap[:, 0:n], in1=psum_t[:, bd, 0:n], op=ALU.add),
                    r=[TMPF.t(), ps_t[bds[c]]], w=[CVB.t(c)])
                act(SQB.ap[:, c, 0:n], CVB.ap[:, c, 0:n], AF.Square, r=[CVB.t(c)], w=[SQB.t(c)])

        def finish_chain(ti):
            seg, t0, n, Y, ytf, ycol = tiles2[ti]
            b1 = bank()
            b2 = bank()
            for c in range(8):
                mm(psum_t[:, b1, 0:n], ps_t[b1], ONES.ap, CVB.ap[:, c, 0:n], [ONES.t(), CVB.t(c)], c == 0, c == 7)
            for c in range(8):
                mm(psum_t[:, b2, 0:n], ps_t[b2], ONES.ap, SQB.ap[:, c, 0:n], [ONES.t(), SQB.t(c)], c == 0, c == 7)
            mean_p = psum_t[:, b1, 0:n]
            rstd_p = psum_t[:, b2, 0:n]
            ms = MS.ap[:, 0:n]
            pg.op("dve", lambda e, mean_p=mean_p: e.tensor_scalar(out=mean_p, in0=mean_p, scalar1=1.0 / D,
                                                                scalar2=None, op0=ALU.mult),
                  r=[ps_t[b1]], w=[ps_t[b1]])
            act(ms, mean_p, AF.Square, r=[ps_t[b1]], w=[MS.t()])
            pg.op("dve", lambda e, rstd_p=rstd_p, ms=ms: e.scalar_tensor_tensor(
                out=ms, in0=rstd_p, scalar=1.0 / D, in1=ms, op0=ALU.mult, op1=ALU.subtract),
                r=[ps_t[b2], MS.t()], w=[MS.t()])
            act(ms, ms, AF.Ln, r=[MS.t()], w=[MS.t()], bias=EPS)
            act(rstd_p, ms, AF.Exp, r=[MS.t()], w=[ps_t[b2]], scale=-0.5)
            for c in range(8):
                dt_ = DT[c % 2]
                pg.op("dve", lambda e, c=c, dt_=dt_, mean_p=mean_p, n=n: e.tensor_tensor(
                    out=dt_.ap[:, 0:n], in0=CVB.ap[:, c, 0:n], in1=mean_p, op=ALU.subtract),
                    r=[CVB.t(c), ps_t[b1]], w=[dt_.t()])
                pg.op("dve", lambda e, dt_=dt_, rstd_p=rstd_p, n=n: e.tensor_tensor(
                    out=dt_.ap[:, 0:n], in0=dt_.ap[:, 0:n], in1=rstd_p, op=ALU.mult),
                    r=[dt_.t(), ps_t[b2]], w=[dt_.t()])
                act(HN.ap[:, c, 0:n], dt_.ap[:, 0:n], AF.Silu, r=[dt_.t(), LNCG.t(), LNCB.t()], w=[HN.t(c)],
                    bias=LNCB.ap[:, c:c + 1], scale=LNCG.ap[:, c:c + 1])

        def finish_pe(ti):
            seg, t0, n, Y, ytf, ycol = tiles2[ti]
            for dc in range(8):
                b = bank()
                pap = psum_t[:, b, 0:n]
                for kc in range(8):
                    mm(pap, ps_t[b], WPW2.ap[:, kc, dc * 128:(dc + 1) * 128], HN.ap[:, kc, 0:n],
                       [WPW2.t(), HN.t(kc)], kc == 0, False)
                mm(pap, ps_t[b], BPW2.ap[0:1, dc * 128:(dc + 1) * 128], ONESROW.ap[0:1, 0:n],
                   [BPW2.t(), ONESROW.t()], False, True)
                xr_update(l, 0, seg, dc, t0, n, pap, b)

        PL = [(ti, cp) for ti in range(len(tiles2)) for cp in range(4)]
        build(*PL[0])
        for idx, (ti, cp) in enumerate(PL):
            if idx + 1 < len(PL):
                build(*PL[idx + 1])
            if cp == 0 and ti > 0:
                finish_chain(ti - 1)
            conv_rest(ti, cp)
            if cp == 0 and ti > 0:
                finish_pe(ti - 1)
        finish_chain(len(tiles2) - 1)
        finish_pe(len(tiles2) - 1)

    def emit_final():
        ph = Phase(AR, PH0, PHLIM)
        nb = NormBufs(ph, 512)
        OB = ph.ring(2, [8, 512], F32, "OB", nt=8)
        yv = yout.rearrange("(c p) t -> p c t", p=128)
        for bi, (t0, n) in enumerate(split_blocks(0, 2048, 512)):
            b, pap = rstd_psum(nb, lambda c, t0=t0, n=n: XR.ap[:, c, t0:t0 + n],
                               lambda c, t0=t0, n=n: xts(c, t0, n), n)
            ob = OB[bi % 2]
            for c in range(8):
                pg.op("dve", lambda e, c=c, ob=ob, pap=pap, t0=t0, n=n: e.scalar_tensor_tensor(
                    out=ob.ap[:, c, 0:n], in0=XR.ap[:, c, t0:t0 + n], scalar=FINALG.ap[:, c:c + 1], in1=pap,
                    op0=ALU.mult, op1=ALU.mult),
                    r=xts(c, t0, n) + [ps_t[b], FINALG.t()], w=[ob.t(c)])
            pg.dma("sp", lambda e, ob=ob, t0=t0, n=n: e.dma_start(out=yv[:, :, t0:t0 + n], in_=ob.ap[:, :, 0:n]),
                   r=ob.ts, w=[])

    layers = list(layers)
    ph_mod = Phase(AR, PH0, PHLIM)
    l0 = layers[0]
    WM0 = ph_mod.ring(6, [8, 512], BF16, "WM0")
    run_gen(mod_part_gen(l0, 0, WM0, 512), None)
    if not load_state:
        src = dr["xin"].rearrange("(c p) t -> p c t", p=128)
        for (t0, n) in late_x:
            dma_w(XR.ap[:, :, t0:t0 + n], src[:, :, t0:t0 + n],
                  [xr_t[c][j] for c in range(8) for j in range(t0 // 128, (t0 + n) // 128)], eng="sp",
                  r=[MODRT[l0 % 2][0]])
    if LAYER_CFG[l0]["kind"] == "even":
        first_layer_part1[0] = l0
    else:
        run_gen(mod_part_gen(l0, 1, WM0, 512), None)
    for li, l in enumerate(layers):
        nxt = layers[li + 1] if li + 1 < len(layers) else None
        if LAYER_CFG[l]["kind"] == "even":
            emit_even(l)
        else:
            emit_odd(l)
        fuse = do_final and l == 3 and nxt is None
        emit_ffn(l, nxt, fuse_final=fuse)
    if do_final and not (layers[-1] == 3):
        emit_final()
    if store_state:
        dst = xst_out.rearrange("(c p) t -> p c t", p=128)
        for (t0, n) in split_blocks(0, NXR, 512):
            pg.dma("sp", lambda e, t0=t0, n=n: e.dma_start(out=dst[:, :, t0:t0 + n], in_=XR.ap[:, :, t0:t0 + n]),
                   r=[xr_t[c][j] for c in range(8) for j in range(t0 // 128, (t0 + n) // 128)], w=[])
    out_dmas = [ins for ins in pg.q["sp"] if ins.dma]
    tail = out_dmas[-DMA_RING:]
    fin = Inst()
    fin.eng = "sp"
    fin.fn = None
    fin.dma = False
    fin.sig = False
    fin.key = None
    fin.val = 0
    fin.deps = list(tail)
    pg.q["sp"].append(fin)

    pg.finalize()

    keys = set()
    for e, lst in pg.q.items():
        for ins in lst:
            if ins.key is not None and (ins.dma or ins.sig):
                keys.add(ins.key)
            for k, v in ins.waits:
                keys.add(k)
    sems = {}
    for k in sorted(keys, key=str):
        sems[k] = es.enter_context(nc.semaphore("s_" + "_".join(str(x) for x in k)))

    def emit(engname, e):
        for ins in pg.q[engname]:
            for key, val in ins.waits:
                e.wait_ge(sems[key], val)
            if ins.fn is None:
                continue
            r = ins.fn(e)
            if ins.dma:
                r.then_inc(sems[ins.key], 16)
            elif ins.sig:
                r.then_inc(sems[ins.key], 1)

    with nc.Block() as block:
        @block.tensor
        def _(e):
            emit("pe", e)

        @block.scalar
        def _(e):
            emit("act", e)

        @block.vector
        def _(e):
            emit("dve", e)

        @block.gpsimd
        def _(e):
            emit("pool", e)

        @block.sync
        def _(e):
            emit("sp", e)
    es.close()
    return nc


NEG_FILL = -30000.0


def _bias_tables(rpb_l, flipped):
    def table(jq, ktiles):
        out = np.full((128, 8, len(ktiles), 128), NEG_FILL, np.float32)
        kk = np.arange(128)
        qq = np.arange(128)
        for ki, kt in enumerate(ktiles):
            kr_l = 2 * kt + kk // 64
            kc_l = kk % 64
            qr_l = 2 * jq + qq // 64
            qc_l = qq % 64
            if flipped:
                kr, kc_, qr, qc = 63 - kr_l, 63 - kc_l, 63 - qr_l, 63 - qc_l
            else:
                kr, kc_, qr, qc = kr_l, kc_l, qr_l, qc_l
            rs = np.clip(qr - 4, 0, 56)
            cs = np.clip(qc - 8, 0, 48)
            KR_, QR_ = np.meshgrid(kr, qr, indexing="ij")
            KC_, QC_ = np.meshgrid(kc_, qc, indexing="ij")
            RS_ = np.broadcast_to(rs[None, :], KR_.shape)
            CS_ = np.broadcast_to(cs[None, :], KR_.shape)
            valid = (KR_ >= RS_) & (KR_ < RS_ + 8) & (KC_ >= CS_) & (KC_ < CS_ + 16) & (KR_ >= 0) & (KR_ < 64)
            drr = np.clip(KR_ - QR_ + 7, 0, 14)
            dcc = np.clip(KC_ - QC_ + 15, 0, 30)
            vals = rpb_l[:, drr, dcc]
            blk = np.where(valid[None], vals, np.float32(NEG_FILL)).astype(np.float32)
            out[:, :, ki, :] = np.transpose(blk, (1, 0, 2))
        return out

    tint = table(10, [8, 9, 10, 11, 12]).reshape(128, 8 * 5 * 128)
    tb = np.stack([table(0, [0, 1, 2, 3]).reshape(128, 8 * 4 * 128),
                   table(1, [0, 1, 2, 3]).reshape(128, 8 * 4 * 128)], axis=0)
    return tint, tb


def _pc(v, nchunk):
    return np.ascontiguousarray(v.reshape(nchunk, 128).T)


def prep_inputs(inp):
    f = lambda a: np.ascontiguousarray(np.asarray(a, dtype=np.float32))
    x = f(inp["x"])
    c = f(inp["c"])
    ctx = f(inp["ctx"])
    c_ctx = f(inp["c_ctx"])
    shared = {
        "wmod": f(inp["w_mod"]),
        "bmod": np.stack([_pc(f(inp["b_mod"])[l], 48) for l in range(4)]),
        "normg": np.stack([np.concatenate([_pc(f(inp["norm_g"])[l, 0], 8), _pc(f(inp["norm_g"])[l, 1], 8)], axis=1)
                           for l in range(4)]),
        "finalg": _pc(f(inp["final_g"]), 8),
        "win": f(inp["w_in"]),
        "wout": f(inp["w_out"]),
        "lnvg": np.stack([_pc(f(inp["ln_v_g"])[i], 4) for i in range(2)]),
        "wpw1": f(inp["w_pw1"]),
        "bpw1": np.stack([_pc(f(inp["b_pw1"])[i], 16) for i in range(2)]),
        "bdw": np.stack([_pc(f(inp["b_dw"])[i], 8) for i in range(2)]),
        "lncg": np.stack([_pc(f(inp["ln_c_g"])[i], 8) for i in range(2)]),
        "lncb": np.stack([_pc(f(inp["ln_c_b"])[i], 8) for i in range(2)]),
        "wpw2": f(inp["w_pw2"]),
        "bpw2": f(inp["b_pw2"]).reshape(2, 1, D),
        "wff1": f(inp["w_ff1"]),
        "wff2": f(inp["w_ff2"]),
        "ident": np.eye(128, dtype=np.float32),
    }
    w_sp = f(inp["w_sp"])
    b_sp = f(inp["b_sp"])
    w_dw = f(inp["w_dw"])
    rpb = f(inp["rpb"])
    per_half = []
    for flipped in (False, True):
        wsp = w_sp[:, :, ::-1, ::-1] if flipped else w_sp
        bsp = b_sp[:, :, ::-1] if flipped else b_sp
        wdw = w_dw[:, ::-1, :] if flipped else w_dw
        wspT = np.ascontiguousarray(np.transpose(wsp, (0, 3, 1, 2))).reshape(2, 128, 512)
        bspb = np.ascontiguousarray(np.broadcast_to(bsp.reshape(2, 1, 512), (2, 128, 512)))
        wdwl = np.ascontiguousarray(np.transpose(wdw.reshape(2, 31, 8, 128), (0, 3, 2, 1))).reshape(2, 128, 248)
        tints, tbnds = [], []
        for i in range(2):
            ti, tb = _bias_tables(rpb[i], flipped)
            tints.append(ti)
            tbnds.append(tb)
        per_half.append({"wspT": wspT, "bspb": bspb, "wdw": wdwl,
                         "tint": np.stack(tints), "tbnd": np.stack(tbnds)})
    cores = []
    for core in range(8):
        b, half = core // 2, core % 2
        if half == 0:
            xs = x[b, 0:NWIN]
            cs = ctx[b]
        else:
            xs = x[b, ::-1][0:NWIN]
            cs = ctx[b, ::-1]
        xin = np.ascontiguousarray(np.concatenate([xs, cs], axis=0).T)
        cvec = np.ascontiguousarray(np.stack([_pc(c[b], 8), _pc(c_ctx, 8)], axis=2).reshape(128, 16))
        m = dict(shared)
        m.update(per_half[half])
        m["xin"] = xin
        m["cvec"] = cvec
        cores.append(m)
    return cores


_NC_CACHE = {}


def _get_nc(key, **kw):
    if key not in _NC_CACHE:
        _NC_CACHE[key] = build_program(**kw)
    return _NC_CACHE[key]


def assemble_output(youts):
    out = np.empty((4, 4096, D), np.float32)
    for core in range(8):
        b, half = core // 2, core % 2
        y = np.asarray(youts[core]).T
        if half == 0:
            out[b, 0:2048] = y
        else:
            out[b, 2048:4096] = y[::-1]
    return out


def kernel(**inputs):
    cores = prep_inputs(inputs)
    nc = _get_nc("full", layers=(0, 1, 2, 3), load_state=False, store_state=False, do_final=True)
    res = run_bass_kernel_spmd(nc, cores, core_ids=list(range(8)))
    return assemble_output([r["yout"] for r in res.results])
```

```python
import numpy as np
import concourse.bass as bass
import concourse.mybir as mybir
from concourse.bass_utils import run_bass_kernel_spmd
from contextlib import ExitStack

F32 = mybir.dt.float32
BF16 = mybir.dt.bfloat16
AF = mybir.ActivationFunctionType
ALU = mybir.AluOpType
AX = mybir.AxisListType

D = 1024
KC = 8
NMAIN = 2432
CTXO = 2432
NXR = 2688
NWIN = 2688
EPS = 1e-6
SELF_SYNC = True
DMA_RING = 8


class Inst:
    __slots__ = ("eng", "fn", "deps", "idx", "sig", "key", "val", "dma", "waits")


class T:
    __slots__ = ("name", "w", "rs", "init")

    def __init__(self, name, init=()):
        self.name = name
        self.w = None
        self.rs = []
        self.init = list(init)


class Prog:
    ENGS = ("pe", "act", "dve", "pool", "sp")

    def __init__(self):
        self.q = {e: [] for e in self.ENGS}
        self.ndma = {e: 0 for e in self.ENGS}
        self.dhist = {e: [] for e in self.ENGS}

    def _add(self, eng, fn, r, w, dma):
        ins = Inst()
        ins.eng = eng
        ins.fn = fn
        ins.dma = dma
        ins.sig = False
        ins.key = None
        ins.val = 0
        deps = []
        for t in r:
            if t.w is not None:
                deps.append(t.w)
            elif t.init:
                deps.extend(t.init)
        for t in w:
            if t.w is not None:
                deps.append(t.w)
            if t.init:
                deps.extend(t.init)
                t.init = []
            deps.extend(t.rs)
        for t in w:
            t.w = ins
            t.rs = []
        for t in r:
            if t.w is not ins:
                t.rs.append(ins)
        if dma:
            m = self.ndma[eng]
            self.ndma[eng] = m + 1
            ins.key = ("d", eng, m % DMA_RING)
            ins.val = 16 * (m // DMA_RING + 1)
            if m >= DMA_RING:
                deps.append(self.dhist[eng][m - DMA_RING])
            self.dhist[eng].append(ins)
        ins.deps = deps
        self.q[eng].append(ins)
        return ins

    def op(self, eng, fn, r=(), w=()):
        return self._add(eng, fn, r, w, False)

    def dma(self, eng, fn, r=(), w=()):
        return self._add(eng, fn, r, w, True)

    def finalize(self):
        for e, lst in self.q.items():
            for i, ins in enumerate(lst):
                ins.idx = i
        for e, lst in self.q.items():
            for ins in lst:
                best = {}
                dm = {}
                for d in ins.deps:
                    if d is ins:
                        continue
                    if d.dma:
                        dm[id(d)] = d
                    else:
                        if d.eng == e and (e == "pe" or not SELF_SYNC):
                            continue
                        b = best.get(d.eng)
                        if b is None or b.idx < d.idx:
                            best[d.eng] = d
                ins.deps = list(best.values()) + list(dm.values())
                for d in best.values():
                    d.sig = True
        for e in self.ENGS:
            cnt = 0
            for ins in self.q[e]:
                if ins.dma:
                    continue
                if ins.sig:
                    cnt += 1
                    ins.key = ("c", e)
                    ins.val = cnt
        for e, lst in self.q.items():
            known = {}
            for ins in lst:
                waits = []
                for d in ins.deps:
                    if known.get(d.key, 0) >= d.val:
                        continue
                    known[d.key] = d.val
                    waits.append((d.key, d.val))
                ins.waits = waits

    def emit_engine(self, eng, e, sems):
        for ins in self.q[eng]:
            for key, val in ins.waits:
                e.wait_ge(sems[key], val)
            r = ins.fn(e)
            if ins.dma:
                r.then_inc(sems[ins.key], 16)
            elif ins.sig:
                r.then_inc(sems[ins.key], 1)


class Buf:
    def __init__(self, ap, ts):
        self.ap = ap
        self.ts = ts

    def t(self, i=0):
        return self.ts[i]


class Arena:
    def __init__(self, tensor, nbytes):
        self.tensor = tensor
        self.nbytes = nbytes
        self.live = []

    def _track(self, off, nbytes, name):
        t = T(name)
        init = []
        keep = []
        end = off + nbytes
        for (o, e, old) in self.live:
            if o < end and off < e:
                if old.w is not None:
                    init.append(old.w)
                init.extend(old.rs)
                init.extend(old.init)
                if not (off <= o and e <= end):
                    keep.append((o, e, old))
            else:
                keep.append((o, e, old))
        t.init = init
        keep.append((off, end, t))
        self.live = keep
        return t

    def split(self, parent, names):
        out = []
        keep = []
        rng = None
        for (o, e, t) in self.live:
            if t is parent:
                rng = (o, e)
            else:
                keep.append((o, e, t))
        assert rng is not None
        for nm in names:
            t = T(nm, init=parent.init)
            keep.append((rng[0], rng[1], t))
            out.append(t)
        self.live = keep
        return out

    def buf(self, off, shape, dtype, name, nt=1):
        esz = 4 if dtype == F32 else 2
        n = int(np.prod(shape))
        nbytes = n * esz
        assert off % 4 == 0 and off + nbytes <= self.nbytes, (name, off, nbytes, self.nbytes)
        ap = self.tensor[:, off // 2: off // 2 + nbytes // 2]
        if dtype == F32:
            ap = ap.bitcast(F32)
        if len(shape) == 2:
            ap = ap.rearrange("p (a b) -> p a b", a=shape[0])
        elif len(shape) == 3:
            ap = ap.rearrange("p (a b c) -> p a b c", a=shape[0], b=shape[1])
        if nt == 1:
            ts = [self._track(off, nbytes, name)]
        else:
            assert shape[0] == nt
            sub = nbytes // nt
            ts = [self._track(off + i * sub, sub, f"{name}{i}") for i in range(nt)]
        return Buf(ap, ts)


class Phase:
    def __init__(self, arena, base, limit):
        self.arena = arena
        self.off = base
        self.limit = limit

    def buf(self, shape, dtype, name, nt=1):
        esz = 4 if dtype == F32 else 2
        nbytes = int(np.prod(shape)) * esz
        nbytes_al = (nbytes + 31) // 32 * 32
        off = self.off
        self.off += nbytes_al
        assert self.off <= self.limit, ("phase overflow", name, self.off, self.limit)
        return self.arena.buf(off, shape, dtype, name, nt)

    def ring(self, n, shape, dtype, name, nt=1):
        return [self.buf(shape, dtype, f"{name}_{i}", nt) for i in range(n)]


LAYER_CFG = {
    0: dict(kind="even", nq=19, nkv=21, ctx="full", ffn=[(0, 2384, 0), (2432, 2688, 1)]),
    1: dict(kind="odd", ny=2384, nout=2368, ctx=True, ffn=[(0, 2368, 0), (2432, 2688, 1)]),
    2: dict(kind="even", nq=17, nkv=19, ctx="kv", ffn=[(0, 2064, 0)]),
    3: dict(kind="odd", ny=2064, nout=2048, ctx=False, ffn=[(0, 2048, 0)]),
}


def split_blocks(t0, t1, step):
    out = []
    t = t0
    while t < t1:
        n = min(step, t1 - t)
        out.append((t, n))
        t += n
    return out


def build_program(layers=(0, 1, 2, 3), load_state=False, store_state=False, do_final=True):
    nc = bass.Bass("TRN2", target_bir_lowering=False)
    pg = Prog()
    dr = {}

    def din(name, shape):
        dr[name] = nc.dram_tensor(name, list(shape), F32, kind="ExternalInput").ap()
        return dr[name]

    if load_state:
        din("xst", [D, NXR])
    else:
        din("xin", [D, NWIN + 256])
    din("cvec", [128, 16])
    din("wmod", [4, D, 6 * D])
    din("bmod", [4, 128, 48])
    din("normg", [4, 128, 16])
    din("finalg", [128, 8])
    din("win", [2, D, 2560])
    din("wout", [2, D, D])
    din("lnvg", [2, 128, 4])
    din("wspT", [2, 128, 512])
    din("bspb", [2, 128, 512])
    din("tint", [2, 128, 8 * 640])
    din("tbnd", [2, 2, 128, 8 * 512])
    din("wpw1", [2, D, 2048])
    din("bpw1", [2, 128, 16])
    din("wdw", [2, 128, 248])
    din("bdw", [2, 128, 8])
    din("lncg", [2, 128, 8])
    din("lncb", [2, 128, 8])
    din("wpw2", [2, D, D])
    din("bpw2", [2, 1, D])
    din("wff1", [4, D, 4096])
    din("wff2", [4, 4096, D])
    din("ident", [128, 128])
    if do_final:
        yout = nc.dram_tensor("yout", [D, 2048], F32, kind="ExternalOutput").ap()
    if store_state:
        xst_out = nc.dram_tensor("xst_out", [D, NXR], F32, kind="ExternalOutput").ap()

    es = ExitStack()
    TOTAL = 212800
    arena_t = es.enter_context(nc.sbuf_tensor("arena", [128, TOTAL // 2], BF16))
    psum_t = es.enter_context(nc.psum_tensor("ps", [128, 8, 512], F32))
    AR = Arena(arena_t, TOTAL)

    pers = Phase(AR, 0, TOTAL)
    XR = pers.buf([8, NXR], F32, "XR")
    xr_t = [[T(f"xr{c}_{j}") for j in range(NXR // 128)] for c in range(8)]
    ONES = pers.buf([128], BF16, "ONES")
    IDENT = pers.buf([128], BF16, "IDENT")
    ONESROW = pers.buf([512], BF16, "ONESROW")
    SCT = pers.buf([8, 2], BF16, "SCT")
    CVEC = pers.buf([16], F32, "CVEC")
    FINALG = pers.buf([8], F32, "FINALG")
    NORMG = pers.buf([4, 16], F32, "NORMG")
    MODR = [pers.buf([48, 2], F32, f"MODR{i}") for i in range(2)]
    MODA = [pers.buf([2, 8, 2], F32, f"MODA{i}") for i in range(2)]
    BMOD = [pers.buf([48], F32, f"BMOD{i}") for i in range(2)]
    MODRT = [[T(f"modr{i}_{p}") for p in range(2)] for i in range(2)]
    MODAT = [[T(f"moda{i}_{p}") for p in range(2)] for i in range(2)]
    LNVG = pers.buf([4], F32, "LNVG")
    BSPB = pers.buf([4, 128], F32, "BSPB")
    WSP = pers.buf([4, 128], BF16, "WSP")
    BPW1 = pers.buf([16], F32, "BPW1")
    BDW = pers.buf([8], F32, "BDW")
    LNCG = pers.buf([8], F32, "LNCG")
    LNCB = pers.buf([8], F32, "LNCB")
    WDW = pers.buf([8, 31], F32, "WDW")
    BPW2 = pers.buf([1024], BF16, "BPW2")
    PH0 = (pers.off + 63) // 64 * 64
    PHLIM = TOTAL

    ps_t = [T(f"psb{b}") for b in range(8)]
    bank_ctr = [0]

    reserved_banks = set()

    def bank():
        while True:
            b = bank_ctr[0] % 8
            bank_ctr[0] += 1
            if b not in reserved_banks:
                return b

    def xts(c, t0, n):
        return [xr_t[c][j] for j in range(t0 // 128, (t0 + n + 127) // 128)]

    def mm(out_ap, out_t, lhsT, rhs, reads, start, stop):
        pg.op("pe", lambda e, o=out_ap, l=lhsT, r=rhs, s=start, p=stop: e.matmul(o, l, r, start=s, stop=p),
              r=reads, w=[out_t])

    def act(out_ap, in_ap, func, r, w, bias=None, scale=None):
        def fn(e, o=out_ap, i=in_ap, f=func, b=bias, s=scale):
            kw = {}
            if b is not None:
                kw["bias"] = b
            if s is not None:
                kw["scale"] = s
            return e.activation(out=o, in_=i, func=f, **kw)
        pg.op("act", fn, r=r, w=w)

    def dma_w(out_ap, in_ap, w, eng="pool", r=()):
        pg.dma(eng, lambda e, o=out_ap, i=in_ap: e.dma_start(out=o, in_=i), r=r, w=w)

    dma_w(IDENT.ap, dr["ident"], [IDENT.t()])
    pg.op("dve", lambda e: e.memset(ONES.ap, 1.0), w=[ONES.t()])
    pg.op("dve", lambda e: e.memset(ONESROW.ap, 1.0), w=[ONESROW.t()])
    dma_w(CVEC.ap, dr["cvec"], [CVEC.t()], eng="sp")
    dma_w(FINALG.ap, dr["finalg"], [FINALG.t()], eng="sp")
    dma_w(NORMG.ap, dr["normg"].rearrange("l p f -> p l f"), [NORMG.t()], eng="sp")
    act(SCT.ap.rearrange("p a b -> p (a b)"), CVEC.ap, AF.Silu, r=[CVEC.t()], w=[SCT.t()])

    if load_state:
        src = dr["xst"].rearrange("(c p) t -> p c t", p=128)
        for (t0, n) in split_blocks(0, NXR, 512):
            dma_w(XR.ap[:, :, t0:t0 + n], src[:, :, t0:t0 + n],
                  [xr_t[c][j] for c in range(8) for j in range(t0 // 128, (t0 + n) // 128)], eng="sp")
    else:
        src = dr["xin"].rearrange("(c p) t -> p c t", p=128)
        dma_w(XR.ap[:, :, CTXO:CTXO + 256], src[:, :, NWIN:NWIN + 256],
              [xr_t[c][j] for c in range(8) for j in (19, 20)], eng="sp")
        late_x = []
        for (t0, n) in [(0, 256), (256, 256)] + split_blocks(512, NMAIN, 512):
            if t0 >= 512:
                late_x.append((t0, n))
                continue
            dma_w(XR.ap[:, :, t0:t0 + n], src[:, :, t0:t0 + n],
                  [xr_t[c][j] for c in range(8) for j in range(t0 // 128, (t0 + n) // 128)], eng="sp")

    def mod_part_gen(l, part, bufs, pcols):
        slot = l % 2
        wv = dr["wmod"][l].rearrange("(kc p) f -> p kc f", p=128)
        c0 = part * 24
        npc = pcols // 128
        npieces = 24 // npc
        b = bank()
        reserved_banks.add(b)
        R = len(bufs)

        def issue(k):
            wm = bufs[k % R]
            col = (c0 + k * npc) * 128
            dma_w(wm.ap, wv[:, :, col:col + pcols], [wm.t()])

        def matmuls(k):
            wm = bufs[k % R]
            for fl in range(npc):
                fc = k * npc + fl
                for kc in range(8):
                    mm(psum_t[:, b, fc * 2:fc * 2 + 2], ps_t[b], wm.ap[:, kc, fl * 128:(fl + 1) * 128],
                       SCT.ap[:, kc, :], [wm.t(), SCT.t()], kc == 0, kc == 7)

        if part == 0:
            dma_w(BMOD[slot].ap, dr["bmod"][l], [BMOD[slot].t()], eng="sp")
        for k in range(min(R - 1, npieces)):
            issue(k)
        if R > 1:
            yield
        for k in range(npieces):
            if R == 1:
                issue(k) if k == 0 else None
            elif k + R - 1 < npieces:
                issue(k + R - 1)
            matmuls(k)
            if R == 1 and k + 1 < npieces:
                issue(k + 1)
            yield
        pv = psum_t[:, b, 0:48].rearrange("p (a b) -> p a b", b=2)
        mt = MODRT[slot][part]
        for sgi in range(2):
            pg.op("dve", lambda e, sgi=sgi, pv=pv, slot=slot: e.tensor_tensor(
                out=MODR[slot].ap[:, c0:c0 + 24, sgi], in0=pv[:, :, sgi], in1=BMOD[slot].ap[:, c0:c0 + 24], op=ALU.add),
                r=[ps_t[b], BMOD[slot].t()], w=[mt])
        reserved_banks.discard(b)
        which = part
        sb = 8 if which == 0 else 32
        for sgi in range(2):
            pg.op("dve", lambda e, which=which, sgi=sgi, sb=sb, slot=slot, l=l: e.scalar_tensor_tensor(
                out=MODA[slot].ap[:, which, :, sgi], in0=MODR[slot].ap[:, sb:sb + 8, sgi], scalar=1.0,
                in1=NORMG.ap[:, l, which * 8:which * 8 + 8], op0=ALU.add, op1=ALU.mult),
                r=[mt, NORMG.t()], w=[MODAT[slot][which]])

    def run_gen(g, n=None):
        k = 0
        while n is None or k < n:
            try:
                next(g)
            except StopIteration:
                return True
            k += 1
        return False

    def m_sh(l, which, c, seg):
        return MODR[l % 2].ap[:, (0 if which == 0 else 24) + c, seg:seg + 1]

    def m_g(l, which, c, seg):
        return MODR[l % 2].ap[:, (16 if which == 0 else 40) + c, seg:seg + 1]

    def m_a(l, which, c, seg):
        return MODA[l % 2].ap[:, which, c, seg:seg + 1]

    def mod_ts(l, which=None):
        if which is None:
            return MODRT[l % 2] + MODAT[l % 2]
        return [MODRT[l % 2][which], MODAT[l % 2][which]]

    class NormBufs:
        def __init__(self, ph, nmax, nsq=2, sq_eng="act"):
            self.nsq = nsq
            self.sq_eng = sq_eng
            self.SQ = ph.ring(nsq, [nmax], BF16, "SQ")
            self.RT = ph.buf([nmax], F32, "RT")
            self.NT = ph.ring(2, [nmax], F32, "NT")
            self.ctr = 0
            self.sqctr = 0

    def rstd_psum(nb, src_fn, src_ts_fn, n):
        b = bank()
        pap = psum_t[:, b, 0:n]
        sqs = []
        for c in range(8):
            sq = nb.SQ[nb.sqctr % nb.nsq]
            nb.sqctr += 1
            sqs.append(sq)
            if nb.sq_eng == "pool":
                pg.op("pool", lambda e, sq=sq, c=c: e.tensor_tensor(out=sq.ap[:, 0:n], in0=src_fn(c), in1=src_fn(c),
                                                                 op=ALU.mult), r=src_ts_fn(c), w=[sq.t()])
            else:
                act(sq.ap[:, 0:n], src_fn(c), AF.Square, r=src_ts_fn(c), w=[sq.t()])
            if nb.nsq <= 2:
                mm(pap, ps_t[b], ONES.ap, sq.ap[:, 0:n], [ONES.t(), sq.t()], c == 0, c == 7)
            elif c >= nb.nsq - 1:
                cc = c - (nb.nsq - 1)
                mm(pap, ps_t[b], ONES.ap, sqs[cc].ap[:, 0:n], [ONES.t(), sqs[cc].t()], cc == 0, cc == 7)
        if nb.nsq > 2:
            for cc in range(8 - (nb.nsq - 1), 8):
                mm(pap, ps_t[b], ONES.ap, sqs[cc].ap[:, 0:n], [ONES.t(), sqs[cc].t()], cc == 0, cc == 7)
        act(nb.RT.ap[:, 0:n], pap, AF.Ln, r=[ps_t[b]], w=[nb.RT.t()], bias=EPS, scale=1.0 / D)
        act(pap, nb.RT.ap[:, 0:n], AF.Exp, r=[nb.RT.t()], w=[ps_t[b]], scale=-0.5)
        return b, pap

    def norm_mod(nb, l, which, seg, src_fn, src_ts_fn, n, dst_fn, dst_ts_fn):
        b, pap = rstd_psum(nb, src_fn, src_ts_fn, n)
        for c in range(8):
            nt = nb.NT[nb.ctr % 2]
            nb.ctr += 1
            pg.op("dve", lambda e, c=c, nt=nt, pap=pap: e.scalar_tensor_tensor(
                out=nt.ap[:, 0:n], in0=src_fn(c), scalar=m_a(l, which, c, seg), in1=pap,
                op0=ALU.mult, op1=ALU.mult),
                r=src_ts_fn(c) + [ps_t[b]] + mod_ts(l, which), w=[nt.t()])
            act(dst_fn(c), nt.ap[:, 0:n], AF.Identity, r=[nt.t()] + mod_ts(l, which), w=dst_ts_fn(c),
                bias=m_sh(l, which, c, seg))

    def xr_update(l, which, seg, dc, t0, n, pap, pb):
        pg.op("dve", lambda e, dc=dc, t0=t0, n=n, pap=pap: e.scalar_tensor_tensor(
            out=XR.ap[:, dc, t0:t0 + n], in0=pap, scalar=m_g(l, which, dc, seg), in1=XR.ap[:, dc, t0:t0 + n],
            op0=ALU.mult, op1=ALU.add),
            r=[ps_t[pb]] + xts(dc, t0, n) + mod_ts(l, which), w=xts(dc, t0, n))

    pending_gens = []
    first_layer_part1 = [None]

    def emit_ffn(l, next_mod, fuse_final=False):
        cfg = LAYER_CFG[l]
        ph = Phase(AR, PH0, PHLIM)
        HTA = ph.buf([8, NXR], BF16, "HTA")
        _fl = AR.split(HTA.t(), [f"hta{c}_{j}" for c in range(8) for j in range(NXR // 128)])
        hta_t = [[_fl[c * (NXR // 128) + j] for j in range(NXR // 128)] for c in range(8)]
        W1G = ph.ring(2, [8, 512], BF16, "W1G")
        W2G = ph.ring(2, [4, 1024], BF16, "W2G")
        H1R = ph.ring(2, [512], BF16, "H1R")
        H1 = ph.ring(2, [4, 512], BF16, "H1", nt=4)
        nb = NormBufs(ph, 512)
        blocks = []
        for (a, bnd, seg) in cfg["ffn"]:
            for (t0, n) in split_blocks(a, bnd, 512):
                blocks.append((t0, n, seg))
        def do_norm(bi):
            t0, n, seg = blocks[bi]
            norm_mod(nb, l, 1, seg,
                     lambda c, t0=t0, n=n: XR.ap[:, c, t0:t0 + n],
                     lambda c, t0=t0, n=n: xts(c, t0, n), n,
                     lambda c, t0=t0, n=n: HTA.ap[:, c, t0:t0 + n],
                     lambda c, t0=t0, n=n: [hta_t[c][j] for j in range(t0 // 128, (t0 + n + 127) // 128)])

        gens = []
        OBF = None
        if fuse_final:
            OBF = ph.buf([8, 512], F32, "OBF", nt=8)
            yv = yout.rearrange("(c p) t -> p c t", p=128)

        def final_block(bi):
            t0, n, seg = blocks[bi]
            b, pap = rstd_psum(nb, lambda c, t0=t0, n=n: XR.ap[:, c, t0:t0 + n],
                               lambda c, t0=t0, n=n: xts(c, t0, n), n)
            for c in range(8):
                pg.op("dve", lambda e, c=c, pap=pap, t0=t0, n=n: e.scalar_tensor_tensor(
                    out=OBF.ap[:, c, 0:n], in0=XR.ap[:, c, t0:t0 + n], scalar=FINALG.ap[:, c:c + 1], in1=pap,
                    op0=ALU.mult, op1=ALU.mult),
                    r=xts(c, t0, n) + [ps_t[b], FINALG.t()], w=[OBF.t(c)])
            pg.dma("sp", lambda e, t0=t0, n=n: e.dma_start(out=yv[:, :, t0:t0 + n], in_=OBF.ap[:, :, 0:n]),
                   r=OBF.ts, w=[])

        if next_mod is not None:
            WM = ph.ring(2, [8, 512], BF16, "WM")
            gens = [mod_part_gen(next_mod, 0, WM, 512), mod_part_gen(next_mod, 1, WM, 512)]
        for extra_gen in pending_gens:
            gens.insert(0, extra_gen)
        del pending_gens[:]

        def step_gens():
            while gens:
                if run_gen(gens[0], 1):
                    gens.pop(0)
                    continue
                return
        w1v = dr["wff1"][l].rearrange("(kc p) f -> p kc f", p=128)
        w2v = dr["wff2"][l].rearrange("(fc p) d -> p fc d", p=128)
        h1ctr = [0]

        groups = [(4 * k, 4) for k in range(8)]
        NG = len(groups)

        def load_group(g):
            f0, nfc = groups[g]
            dma_w(W1G[g % 2].ap[:, :, 0:nfc * 128], w1v[:, :, f0 * 128:(f0 + nfc) * 128], [W1G[g % 2].t()])
            dma_w(W2G[g % 2].ap[:, 0:nfc, :], w2v[:, f0:f0 + nfc, :], [W2G[g % 2].t()])

        load_group(0)
        for g in range(NG):
            w1 = W1G[g % 2]
            w2 = W2G[g % 2]
            nfc = groups[g][1]
            if g + 1 < NG:
                load_group(g + 1)

            def stage_a(bi):
                t0, n, seg = blocks[bi]
                hs = H1[h1ctr[0] % 2]
                for fc in range(nfc):
                    b = bank()
                    pap = psum_t[:, b, 0:n]
                    for kc in range(8):
                        mm(pap, ps_t[b], w1.ap[:, kc, fc * 128:(fc + 1) * 128], HTA.ap[:, kc, t0:t0 + n],
                           [w1.t()] + [hta_t[kc][j] for j in range(t0 // 128, (t0 + n + 127) // 128)],
                           kc == 0, kc == 7)
                    hr = H1R[(h1ctr[0] * 4 + fc) % 2]
                    act(hr.ap[:, 0:n], pap, AF.Relu, r=[ps_t[b]], w=[hr.t()])
                    act(hs.ap[:, fc, 0:n], hr.ap[:, 0:n], AF.Square, r=[hr.t()], w=[hs.t(fc)])
                h1ctr[0] += 1
                return hs

            def stage_b(bi, hs):
                t0, n, seg = blocks[bi]
                for dc in range(8):
                    b = bank()
                    pap = psum_t[:, b, 0:n]
                    for fc in range(nfc):
                        mm(pap, ps_t[b], w2.ap[:, fc, dc * 128:(dc + 1) * 128], hs.ap[:, fc, 0:n],
                           [w2.t(), hs.t(fc)], fc == 0, fc == nfc - 1)
                    xr_update(l, 1, seg, dc, t0, n, pap, b)

            prev = None
            if g == 0:
                do_norm(0)
                if len(blocks) > 1:
                    do_norm(1)
            for bi in range(len(blocks)):
                if g == 0 and bi + 2 < len(blocks):
                    do_norm(bi + 2)
                hs = stage_a(bi)
                if prev is not None:
                    stage_b(prev[0], prev[1])
                    if fuse_final and g == NG - 1:
                        final_block(prev[0])
                    if g >= 1:
                        step_gens()
                prev = (bi, hs)
            stage_b(prev[0], prev[1])
            if fuse_final and g == NG - 1:
                final_block(prev[0])
        while gens:
            step_gens()

    def emit_even(l):
        cfg = LAYER_CFG[l]
        i = l // 2
        NQ, NKV = cfg["nq"], cfg["nkv"]
        ph = Phase(AR, PH0, PHLIM)
        WIN = ph.buf([8, 2560], BF16, "WIN")
        win_t = AR.split(WIN.t(), [f"win{b}" for b in range(5)])
        WOUT = ph.buf([8, 1024], BF16, "WOUT")
        TINT = ph.buf([8, 640], BF16, "TINT", nt=8)
        tbs_off = ph.off
        TBS = ph.buf([512], F32, "TBS")
        TBE = ph.ring(2, [512], BF16, "TBE")
        KR = ph.ring(6, [4, 128], BF16, "KR")
        VR = ph.ring(6, [512], BF16, "VR")
        KCX = ph.ring(2, [4, 128], BF16, "KCX")
        VCX = ph.ring(2, [512], BF16, "VCX")
        HT = ph.ring(2, [8, 128], BF16, "HT", nt=8)
        nb = NormBufs(ph, 128, nsq=4, sq_eng="pool")
        QR = ph.ring(4, [4, 128], BF16, "QR")
        ATR = ph.ring(4, [4, 128], BF16, "ATR")
        UTR = ph.ring(2, [4, 128], BF16, "UT")
        GGB = ph.buf([512], F32, "GGB")
        G2B = ph.buf([512], BF16, "G2B")
        GN = ph.buf([512], BF16, "GN")
        ST = ph.buf([8, 4], F32, "ST", nt=8)
        EE = ph.ring(2, [7, 128], BF16, "EE")
        RC = ph.ring(2, [128], F32, "RC")
        BT = ph.buf([4, 128], BF16, "BT", nt=4)
        even_gens = []
        if first_layer_part1[0] == l:
            WMS = ph.buf([8, 128], BF16, "WMS")
            even_gens.append(mod_part_gen(l, 1, [WMS], 128))
            first_layer_part1[0] = None

        wv = dr["win"][i].rearrange("(kc p) f -> p kc f", p=128)
        for blk in (3, 4, 2, 0, 1):
            dma_w(WIN.ap[:, :, blk * 512:(blk + 1) * 512], wv[:, :, blk * 512:(blk + 1) * 512], [win_t[blk]])
        dma_w(WOUT.ap, dr["wout"][i].rearrange("(kc p) f -> p kc f", p=128), [WOUT.t()])
        dma_w(LNVG.ap, dr["lnvg"][i], [LNVG.t()], eng="sp")
        dma_w(BSPB.ap.rearrange("p a b -> p (a b)"), dr["bspb"][i], [BSPB.t()], eng="sp")
        dma_w(WSP.ap.rearrange("p a b -> p (a b)"), dr["wspT"][i], [WSP.t()])
        for h in range(8):
            for hf in range(2):
                dma_w(TBS.ap[:, 0:320], dr["tint"][i][:, h * 640 + hf * 320:h * 640 + (hf + 1) * 320],
                      [TBS.t()], eng="sp")
                act(TINT.ap[:, h, hf * 320:(hf + 1) * 320], TBS.ap[:, 0:320], AF.Exp, r=[TBS.t()], w=[TINT.t(h)])

        GG = GGB.ap
        G2 = G2B.ap

        class _GGT:
            @staticmethod
            def t(k):
                return GGB.t() if k == 0 else G2B.t()
        GGT = _GGT
        s1, s2, mean, msq, var, sd, rstd, nmr = [ST.ap[:, k, :] for k in range(8)]

        class WT:
            pass

        def table_int(h):
            return TINT.ap[:, h, :], [TINT.t(h)]

        def make_table_bnd(j):
            def fn(h):
                tb = TBE[h % 2]
                dma_w(TBS.ap[:, 0:512], dr["tbnd"][i][j][:, h * 512:(h + 1) * 512], [TBS.t()], eng="sp")
                act(tb.ap, TBS.ap[:, 0:512], AF.Exp, r=[TBS.t()], w=[tb.t()])
                return tb.ap, [tb.t()]
            return fn

        ckeys = [(KCX[0], VCX[0]), (KCX[1], VCX[1])]
        W = []
        if cfg["ctx"] in ("full", "kv"):
            for ct in range(2):
                w = WT()
                w.kind, w.idx, w.seg = "ctx", ct, 1
                w.t0 = CTXO + ct * 128
                w.xe = False
                w.kdst, w.vdst = KCX[ct], VCX[ct]
                w.has_q = cfg["ctx"] == "full"
                w.keys, w.nloc, w.table_fn = ckeys, 0, None
                W.append(w)
        for t in range(NKV):
            w = WT()
            w.kind, w.idx, w.seg = "main", t, 0
            w.t0 = t * 128
            w.xe = t >= 19
            w.kdst, w.vdst = KR[t % 6], VR[t % 6]
            w.has_q = t < NQ
            if w.has_q:
                if t < 2:
                    kl = [0, 1, 2, 3]
                    w.table_fn = make_table_bnd(t)
                else:
                    kl = [t - 2, t - 1, t, t + 1, t + 2]
                    w.table_fn = table_int
                w.keys = [(KR[k % 6], VR[k % 6]) for k in kl] + ckeys
                w.nloc = len(kl)
            W.append(w)
        for wi, w in enumerate(W):
            w.wi = wi
            w.ht = HT[wi % 2]
            w.qdst = QR[wi % 4] if w.has_q else None
            w.adst = ATR[wi % 4] if w.has_q else None
            w.ut = UTR[wi % 2]
        XEb = [None]

        def stage_A(w):
            if w.xe:
                if XEb[0] is None:
                    XEb[0] = AR.buf(tbs_off, [8, 128], F32, "XE", nt=8)
                XE = XEb[0]
                xsrc = dr["xin"].rearrange("(c p) t -> p c t", p=128)
                dma_w(XE.ap, xsrc[:, :, w.t0:w.t0 + 128], XE.ts, eng="sp")
                src_fn = lambda c: XE.ap[:, c, :]
                src_ts = lambda c: [XE.t(c)]
            else:
                src_fn = lambda c, t0=w.t0: XR.ap[:, c, t0:t0 + 128]
                src_ts = lambda c, t0=w.t0: xts(c, t0, 128)
            norm_mod(nb, l, 0, w.seg, src_fn, src_ts, 128,
                     lambda c: w.ht.ap[:, c, :], lambda c: [w.ht.t(c)])

        def proj_fm(ht, blk, dst, func, scale=None):
            b = bank()
            for hc in range(4):
                for kc in range(8):
                    mm(psum_t[:, b, hc * 128:(hc + 1) * 128], ps_t[b],
                       WIN.ap[:, kc, blk * 512 + hc * 128: blk * 512 + (hc + 1) * 128], ht.ap[:, kc, :],
                       [win_t[blk], ht.t(kc)], kc == 0, kc == 7)
            act(dst.ap.rearrange("p a b -> p (a b)"), psum_t[:, b, :], func, r=[ps_t[b]], w=[dst.t()], scale=scale)

        def proj_tm(ht, blk):
            b = bank()
            for kc in range(8):
                mm(psum_t[:, b, :], ps_t[b], ht.ap[:, kc, :], WIN.ap[:, kc, blk * 512:(blk + 1) * 512],
                   [win_t[blk], ht.t(kc)], kc == 0, kc == 7)
            return b

        def B_k(w):
            proj_fm(w.ht, 3, w.kdst, AF.Identity)

        def B_v(w):
            b = proj_tm(w.ht, 4)
            pg.op("dve", lambda e, b=b, vdst=w.vdst: e.tensor_copy(out=vdst.ap, in_=psum_t[:, b, :]),
                  r=[ps_t[b]], w=[w.vdst.t()])

        def B_q(w):
            if w.has_q:
                proj_fm(w.ht, 2, w.qdst, AF.Identity, scale=0.125)

        def B_u(w):
            if w.has_q:
                proj_fm(w.ht, 0, w.ut, AF.Gelu_apprx_tanh)

        def B_g(w):
            if w.has_q:
                b = proj_tm(w.ht, 1)
                act(GG, psum_t[:, b, :], AF.Gelu_apprx_tanh, r=[ps_t[b]], w=[GGT.t(0)])

        def C1(w):
            if not w.has_q:
                return
            pg.op("pool", lambda e: e.tensor_tensor(out=G2, in0=GG, in1=GG, op=ALU.mult),
                  r=[GGT.t(0)], w=[GGT.t(1)])
            pg.op("dve", lambda e: e.tensor_reduce(out=s1, in_=GG.rearrange("p (a b) -> p a b", a=4), axis=AX.X, op=ALU.add),
                  r=[GGT.t(0)], w=[ST.t(0)])
            pg.op("dve", lambda e: e.tensor_reduce(out=s2, in_=G2.rearrange("p (a b) -> p a b", a=4), axis=AX.X, op=ALU.add),
                  r=[GGT.t(1)], w=[ST.t(1)])
            pg.op("dve", lambda e: e.tensor_scalar(out=mean, in0=s1, scalar1=1.0 / 128, scalar2=None, op0=ALU.mult),
                  r=[ST.t(0)], w=[ST.t(2)])
            pg.op("dve", lambda e: e.tensor_tensor(out=msq, in0=mean, in1=mean, op=ALU.mult),
                  r=[ST.t(2)], w=[ST.t(3)])
            pg.op("dve", lambda e: e.scalar_tensor_tensor(out=var, in0=s2, scalar=1.0 / 128, in1=msq,
                                                         op0=ALU.mult, op1=ALU.subtract),
                  r=[ST.t(1), ST.t(3)], w=[ST.t(4)])
            act(sd, var, AF.Ln, r=[ST.t(4)], w=[ST.t(5)], bias=EPS)
            act(rstd, sd, AF.Exp, r=[ST.t(5)], w=[ST.t(6)], scale=-0.5)
            pg.op("dve", lambda e: e.scalar_tensor_tensor(out=nmr, in0=mean, scalar=-1.0, in1=rstd,
                                                         op0=ALU.mult, op1=ALU.mult),
                  r=[ST.t(2), ST.t(6)], w=[ST.t(7)])
            for g in range(4):
                pg.op("pool", lambda e, g=g: e.tensor_scalar(
                    out=GN.ap[:, g * 128:(g + 1) * 128], in0=GG[:, g * 128:(g + 1) * 128],
                    scalar1=rstd[:, g:g + 1], scalar2=nmr[:, g:g + 1], op0=ALU.mult, op1=ALU.add),
                    r=[GGT.t(0), ST.t(6), ST.t(7)], w=[GN.t()])

        def C2(w):
            if not w.has_q:
                return
            b = bank()
            for g in range(4):
                mm(psum_t[:, b, g * 128:(g + 1) * 128], ps_t[b], GN.ap[:, g * 128:(g + 1) * 128], WSP.ap[:, g, :],
                   [GN.t(), WSP.t()], True, True)
            for g in range(4):
                pg.op("dve", lambda e, g=g, b=b: e.scalar_tensor_tensor(
                    out=G2[:, g * 128:(g + 1) * 128], in0=psum_t[:, b, g * 128:(g + 1) * 128],
                    scalar=LNVG.ap[:, g:g + 1], in1=BSPB.ap[:, g, :], op0=ALU.mult, op1=ALU.add),
                    r=[ps_t[b], LNVG.t(), BSPB.t()], w=[GGT.t(1)])
            pg.op("dve", lambda e, w=w: e.tensor_tensor(out=w.adst.ap.rearrange("p a b -> p (a b)"), in0=G2,
                                                       in1=w.ut.ap.rearrange("p a b -> p (a b)"), op=ALU.mult),
                  r=[GGT.t(1), w.ut.t()], w=[w.adst.t()])

        ectr = [0]

        def D_scores(w, h):
            hp, pbase = h // 2, (h % 2) * 64
            nk = len(w.keys)
            bs = [bank(), bank()]
            for ki, (kb, vb) in enumerate(w.keys):
                b = bs[ki // 4]
                mm(psum_t[:, b, (ki % 4) * 128:(ki % 4 + 1) * 128], ps_t[b],
                   kb.ap[pbase:pbase + 64, hp, :], w.qdst.ap[pbase:pbase + 64, hp, :],
                   [kb.t(), w.qdst.t()], True, True)
            ee = EE[ectr[0] % 2]
            ectr[0] += 1
            n0 = min(nk, 4)
            act(ee.ap[:, 0:n0, :].rearrange("p a b -> p (a b)"), psum_t[:, bs[0], 0:n0 * 128], AF.Exp,
                r=[ps_t[bs[0]]], w=[ee.t()])
            if nk > 4:
                act(ee.ap[:, 4:nk, :].rearrange("p a b -> p (a b)"), psum_t[:, bs[1], 0:(nk - 4) * 128], AF.Exp,
                    r=[ps_t[bs[1]]], w=[ee.t()])
            if w.nloc > 0:
                tap, tts = w.table_fn(h)
                nloc = w.nloc
                pg.op("dve", lambda e, ee=ee, tap=tap, nloc=nloc: e.tensor_tensor(
                    out=ee.ap[:, 0:nloc, :].rearrange("p a b -> p (a b)"),
                    in0=ee.ap[:, 0:nloc, :].rearrange("p a b -> p (a b)"), in1=tap, op=ALU.mult),
                    r=[ee.t()] + tts, w=[ee.t()])
            w.ee[h] = ee

        def D_pv(w, h):
            hp, pbase = h // 2, (h % 2) * 64
            nk = len(w.keys)
            if h % 2 == 0:
                w.pvb = bank()
            b = w.pvb
            ee = w.ee[h]
            for ki, (kb, vb) in enumerate(w.keys):
                mm(psum_t[pbase:pbase + 64, b, 0:128], ps_t[b], vb.ap[:, h * 64:(h + 1) * 64], ee.ap[:, ki, :],
                   [vb.t(), ee.t()], ki == 0, ki == nk - 1)
            for ki, (kb, vb) in enumerate(w.keys):
                mm(psum_t[pbase:pbase + 64, b, 128:256], ps_t[b], ONES.ap[:, 0:64], ee.ap[:, ki, :],
                   [ONES.t(), ee.t()], ki == 0, ki == nk - 1)
            if h % 2 == 1:
                rc = RC[hp % 2]
                act(rc.ap, psum_t[:, b, 128:256], AF.Ln, r=[ps_t[b]], w=[rc.t()])
                act(rc.ap, rc.ap, AF.Exp, r=[rc.t()], w=[rc.t()], scale=-1.0)
                pg.op("dve", lambda e, b=b, rc=rc, hp=hp: e.tensor_tensor(
                    out=BT.ap[:, hp, :], in0=psum_t[:, b, 0:128], in1=rc.ap, op=ALU.mult),
                    r=[ps_t[b], rc.t()], w=[BT.t(hp)])

        def D_out(w):
            for half in range(2):
                b = bank()
                for dl in range(4):
                    dc = half * 4 + dl
                    pap = psum_t[:, b, dl * 128:(dl + 1) * 128]
                    for kc in range(8):
                        if kc < 4:
                            rhs, rt = w.adst.ap[:, kc, :], w.adst.t()
                        else:
                            rhs, rt = BT.ap[:, kc - 4, :], BT.t(kc - 4)
                        mm(pap, ps_t[b], WOUT.ap[:, kc, dc * 128:(dc + 1) * 128], rhs, [WOUT.t(), rt], kc == 0, kc == 7)
                for dl in range(4):
                    dc = half * 4 + dl
                    xr_update(l, 0, w.seg, dc, w.t0, 128, psum_t[:, b, dl * 128:(dl + 1) * 128], b)

        NW = len(W)
        LAG = 3
        pend_out = []
        stage_A(W[0])
        for s in range(NW + LAG):
            wB = W[s] if s < NW else None
            wA = W[s + 1] if s + 1 < NW else None
            wC = W[s - 1] if 1 <= s <= NW else None
            wD = W[s - LAG] if 0 <= s - LAG < NW and W[s - LAG].has_q else None
            if wD is not None:
                wD.ee = {}
            serial = False
            if wD is not None and wB is not None and wB.kind == "main" and wD.kind == "main":
                need = 3 if wD.idx < 2 else wD.idx + 2
                if need >= wB.idx:
                    serial = True
            if wA is not None:
                stage_A(wA)
            if even_gens and s >= 1:
                if run_gen(even_gens[0], 1):
                    even_gens.pop(0)
            if wC is not None:
                C1(wC)
            bgroups = []
            if wB is not None:
                bgroups = [lambda: B_k(wB), lambda: B_v(wB), lambda: B_q(wB), lambda: B_u(wB), lambda: B_g(wB)]
            if serial:
                for f in bgroups:
                    f()
                bgroups = []
            extra = list(bgroups)
            if wC is not None:
                extra.append(lambda: C2(wC))
            if wD is None:
                for f in extra:
                    f()
                if pend_out:
                    D_out(pend_out.pop())
                continue
            D_scores(wD, 0)
            if pend_out:
                D_out(pend_out.pop())
            for h in range(8):
                if extra:
                    extra.pop(0)()
                if h + 1 < 8:
                    D_scores(wD, h + 1)
                D_pv(wD, h)
            for f in extra:
                f()
            pend_out.append(wD)
        if pend_out:
            D_out(pend_out.pop())
        while even_gens:
            if run_gen(even_gens[0], None):
                even_gens.pop(0)

    def emit_odd(l):
        cfg = LAYER_CFG[l]
        i = l // 2
        ny, nout = cfg["ny"], cfg["nout"]
        ph = Phase(AR, PH0, PHLIM)
        YW = 16 + NMAIN + 16
        YM = ph.buf([8, YW], BF16, "YM")
        YC = ph.buf([8, 288], BF16, "YC")
        _fl = AR.split(YM.t(), [f"ym{c}_{j}" for c in range(8) for j in range(YW // 128 + 1)])
        ym_t = [[_fl[c * (YW // 128 + 1) + j] for j in range(YW // 128 + 1)] for c in range(8)]
        yc_t = AR.split(YC.t(), [f"yc{c}" for c in range(8)])
        base2 = ph.off
        WPW1 = ph.buf([8, 2048], BF16, "WPW1")
        w1_t = AR.split(WPW1.t(), [f"wpw1_{b}" for b in range(4)])
        HT = ph.ring(2, [8, 512], BF16, "HT", nt=8)
        nb = NormBufs(ph, 512)
        SG = ph.ring(2, [512], F32, "SG")

        wv = dr["wpw1"][i].rearrange("(kc p) f -> p kc f", p=128)
        for blk in (0, 2, 1, 3):
            dma_w(WPW1.ap[:, :, blk * 512:(blk + 1) * 512], wv[:, :, blk * 512:(blk + 1) * 512], [w1_t[blk]])
        dma_w(BPW1.ap, dr["bpw1"][i], [BPW1.t()], eng="sp")
        dma_w(BDW.ap, dr["bdw"][i], [BDW.t()], eng="sp")
        dma_w(LNCG.ap, dr["lncg"][i], [LNCG.t()], eng="sp")
        dma_w(LNCB.ap, dr["lncb"][i], [LNCB.t()], eng="sp")
        dma_w(WDW.ap.rearrange("p a b -> p (a b)"), dr["wdw"][i], [WDW.t()], eng="sp")
        dma_w(BPW2.ap[0:1, :], dr["bpw2"][i], [BPW2.t()])
        pg.op("pool", lambda e: e.memset(YM.ap, 0.0), w=[ym_t[c][j] for c in range(8) for j in range(YW // 128 + 1)])
        pg.op("pool", lambda e: e.memset(YC.ap, 0.0), w=yc_t)

        def ymts(c, col0, n):
            return [ym_t[c][j] for j in range(col0 // 128, (col0 + n + 127) // 128)]

        segs = [(0, ny, 0, YM, ymts)]
        if cfg["ctx"]:
            segs.append((CTXO, CTXO + 256, 1, YC, lambda c, col0, n: [yc_t[c]]))
        p1blocks = []
        for (a, bnd, seg, Y, ytf) in segs:
            for (t0, n) in split_blocks(a, bnd, 512):
                p1blocks.append((a, seg, Y, ytf, t0, n))

        def p1_norm(bi):
            a, seg, Y, ytf, t0, n = p1blocks[bi]
            ht = HT[bi % 2]
            norm_mod(nb, l, 0, seg,
                     lambda c, t0=t0, n=n: XR.ap[:, c, t0:t0 + n],
                     lambda c, t0=t0, n=n: xts(c, t0, n), n,
                     lambda c, ht=ht, n=n: ht.ap[:, c, 0:n], lambda c, ht=ht: [ht.t(c)])

        p1_norm(0)
        for bi in range(len(p1blocks)):
            if True:
                a, seg, Y, ytf, t0, n = p1blocks[bi]
                ht = HT[bi % 2]
                if bi + 1 < len(p1blocks):
                    p1_norm(bi + 1)
                ycol = 16 + (t0 - a)
                for c in range(8):
                    ba = bank()
                    bg = bank()
                    for kc in range(8):
                        mm(psum_t[:, ba, 0:n], ps_t[ba], WPW1.ap[:, kc, c * 128:(c + 1) * 128], ht.ap[:, kc, 0:n],
                           [w1_t[c // 4], ht.t(kc)], kc == 0, kc == 7)
                    for kc in range(8):
                        mm(psum_t[:, bg, 0:n], ps_t[bg], WPW1.ap[:, kc, 1024 + c * 128:1024 + (c + 1) * 128],
                           ht.ap[:, kc, 0:n], [w1_t[2 + c // 4], ht.t(kc)], kc == 0, kc == 7)
                    sg = SG[c % 2]
                    act(sg.ap[:, 0:n], psum_t[:, bg, 0:n], AF.Sigmoid, r=[ps_t[bg], BPW1.t()], w=[sg.t()],
                        bias=BPW1.ap[:, 8 + c:9 + c])
                    pg.op("dve", lambda e, c=c, ba=ba, sg=sg, n=n, Y=Y, ycol=ycol: e.scalar_tensor_tensor(
                        out=Y.ap[:, c, ycol:ycol + n], in0=psum_t[:, ba, 0:n], scalar=BPW1.ap[:, c:c + 1],
                        in1=sg.ap[:, 0:n], op0=ALU.add, op1=ALU.mult),
                        r=[ps_t[ba], sg.t(), BPW1.t()], w=ytf(c, ycol, n))

        ph2 = Phase(AR, base2, PHLIM)
        NDVE = 8
        NPE = 31 - NDVE
        WPW2 = ph2.buf([8, 1024], BF16, "WPW2")
        DG = ph2.ring(4, [NPE, 128], BF16, "DG", nt=NPE)
        CVB = ph2.buf([8, 512], BF16, "CVB", nt=8)
        SQB = ph2.buf([8, 512], BF16, "SQB", nt=8)
        MS = ph2.buf([512], F32, "MS")
        DT = ph2.ring(2, [512], F32, "DT")
        HN = ph2.buf([8, 512], BF16, "HN", nt=8)
        TMPF = ph2.buf([512], F32, "TMPF")
        dma_w(WPW2.ap, dr["wpw2"][i].rearrange("(kc p) f -> p kc f", p=128), [WPW2.t()])
        tiles2 = []
        for (t0, n) in split_blocks(0, nout, 512):
            tiles2.append((0, t0, n, YM, ymts, 16 + t0))
        if cfg["ctx"]:
            tiles2.append((1, CTXO, 256, YC, lambda c, col0, n: [yc_t[c]], 16))
        dgmap = {}
        dctr = [0]

        def build(ti, cp):
            for c in (2 * cp, 2 * cp + 1):
                dg = DG[dctr[0] % 4]
                dctr[0] += 1
                dgmap[(ti, c)] = dg
                for k in range(NDVE, 31):
                    kk = k - NDVE
                    if k % 2 == 0:
                        pg.op("pool", lambda e, dg=dg, k=k, kk=kk, c=c: e.tensor_scalar(
                            out=dg.ap[:, kk, :], in0=IDENT.ap, scalar1=WDW.ap[:, c, k:k + 1], scalar2=0.0,
                            op0=ALU.mult, op1=ALU.add), r=[IDENT.t(), WDW.t()], w=[dg.t(kk)])
                    else:
                        act(dg.ap[:, kk, :], IDENT.ap, AF.Identity, r=[IDENT.t(), WDW.t()], w=[dg.t(kk)],
                            scale=WDW.ap[:, c, k:k + 1])

        def conv_rest(ti, cp):
            seg, t0, n, Y, ytf, ycol = tiles2[ti]
            cs = (2 * cp, 2 * cp + 1)
            bds = {c: bank() for c in cs}
            for k in range(NDVE):
                for c in cs:
                    col = ycol + k - 15
                    accD = psum_t[:, bds[c], 0:n]
                    if k == 0:
                        pg.op("dve", lambda e, c=c, col=col, accD=accD, Y=Y, n=n: e.tensor_scalar(
                            out=accD, in0=Y.ap[:, c, col:col + n], scalar1=WDW.ap[:, c, 0:1], scalar2=None,
                            op0=ALU.mult), r=ytf(c, col, n) + [WDW.t()], w=[ps_t[bds[c]]])
                    else:
                        pg.op("dve", lambda e, c=c, col=col, accD=accD, Y=Y, n=n, k=k: e.scalar_tensor_tensor(
                            out=accD, in0=Y.ap[:, c, col:col + n], scalar=WDW.ap[:, c, k:k + 1], in1=accD,
                            op0=ALU.mult, op1=ALU.add),
                            r=ytf(c, col, n) + [WDW.t(), ps_t[bds[c]]], w=[ps_t[bds[c]]])
            for c in cs:
                dg = dgmap[(ti, c)]
                b = bank()
                for k in range(NDVE, 31):
                    col = ycol + k - 15
                    mm(psum_t[:, b, 0:n], ps_t[b], dg.ap[:, k - NDVE, :], Y.ap[:, c, col:col + n],
                       [dg.t(k - NDVE)] + ytf(c, col, n), k == NDVE, k == 30)
                act(TMPF.ap[:, 0:n], psum_t[:, b, 0:n], AF.Identity, r=[ps_t[b], BDW.t()], w=[TMPF.t()],
                    bias=BDW.ap[:, c:c + 1])
                pg.op("dve", lambda e, c=c, n=n, bd=bds[c]: e.tensor_tensor(
                    out=CVB.ap[:, c, 0:n], in0=TMPF.ap[:, 0:n], in1=psum_t[:, bd, 0:n], op=ALU.add),
                    r=[TMPF.t(), ps_t[bds[c]]], w=[CVB.t(c)])
                act(SQB.ap[:, c, 0:n], CVB.ap[:, c, 0:n], AF.Square, r=[CVB.t(c)], w=[SQB.t(c)])

        def finish_chain(ti):
            seg, t0, n, Y, ytf, ycol = tiles2[ti]
            b1 = bank()
            b2 = bank()
            for c in range(8):
                mm(psum_t[:, b1, 0:n], ps_t[b1], ONES.ap, CVB.ap[:, c, 0:n], [ONES.t(), CVB.t(c)], c == 0, c == 7)
            for c in range(8):
                mm(psum_t[:, b2, 0:n], ps_t[b2], ONES.ap, SQB.ap[:, c, 0:n], [ONES.t(), SQB.t(c)], c == 0, c == 7)
            mean_p = psum_t[:, b1, 0:n]
            rstd_p = psum_t[:, b2, 0:n]
            ms = MS.ap[:, 0:n]
            pg.op("dve", lambda e, mean_p=mean_p: e.tensor_scalar(out=mean_p, in0=mean_p, scalar1=1.0 / D,
                                                                scalar2=None, op0=ALU.mult),
                  r=[ps_t[b1]], w=[ps_t[b1]])
            act(ms, mean_p, AF.Square, r=[ps_t[b1]], w=[MS.t()])
            pg.op("dve", lambda e, rstd_p=rstd_p, ms=ms: e.scalar_tensor_tensor(
                out=ms, in0=rstd_p, scalar=1.0 / D, in1=ms, op0=ALU.mult, op1=ALU.subtract),
                r=[ps_t[b2], MS.t()], w=[MS.t()])
            act(ms, ms, AF.Ln, r=[MS.t()], w=[MS.t()], bias=EPS)
            act(rstd_p, ms, AF.Exp, r=[MS.t()], w=[ps_t[b2]], scale=-0.5)
            for c in range(8):
                dt_ = DT[c % 2]
                pg.op("dve", lambda e, c=c, dt_=dt_, mean_p=mean_p, n=n: e.tensor_tensor(
                    out=dt_.ap[:, 0:n], in0=CVB.ap[:, c, 0:n], in1=mean_p, op=ALU.subtract),
                    r=[CVB.t(c), ps_t[b1]], w=[dt_.t()])
                pg.op("dve", lambda e, dt_=dt_, rstd_p=rstd_p, n=n: e.tensor_tensor(
                    out=dt_.ap[:, 0:n], in0=dt_.ap[:, 0:n], in1=rstd_p, op=ALU.mult),
                    r=[dt_.t(), ps_t[b2]], w=[dt_.t()])
                act(HN.ap[:, c, 0:n], dt_.ap[:, 0:n], AF.Silu, r=[dt_.t(), LNCG.t(), LNCB.t()], w=[HN.t(c)],
                    bias=LNCB.ap[:, c:c + 1], scale=LNCG.ap[:, c:c + 1])

        def finish_pe(ti):
            seg, t0, n, Y, ytf, ycol = tiles2[ti]
            for dc in range(8):
                b = bank()
                pap = psum_t[:, b, 0:n]
                for kc in range(8):
                    mm(pap, ps_t[b], WPW2.ap[:, kc, dc * 128:(dc + 1) * 128], HN.ap[:, kc, 0:n],
                       [WPW2.t(), HN.t(kc)], kc == 0, False)
                mm(pap, ps_t[b], BPW2.ap[0:1, dc * 128:(dc + 1) * 128], ONESROW.ap[0:1, 0:n],
                   [BPW2.t(), ONESROW.t()], False, True)
                xr_update(l, 0, seg, dc, t0, n, pap, b)

        PL = [(ti, cp) for ti in range(len(tiles2)) for cp in range(4)]
        build(*PL[0])
        for idx, (ti, cp) in enumerate(PL):
            if idx + 1 < len(PL):
                build(*PL[idx + 1])
            if cp == 0 and ti > 0:
                finish_chain(ti - 1)
            conv_rest(ti, cp)
            if cp == 0 and ti > 0:
                finish_pe(ti - 1)
        finish_chain(len(tiles2) - 1)
        finish_pe(len(tiles2) - 1)

    def emit_final():
        ph = Phase(AR, PH0, PHLIM)
        nb = NormBufs(ph, 512)
        OB = ph.ring(2, [8, 512], F32, "OB", nt=8)
        yv = yout.rearrange("(c p) t -> p c t", p=128)
        for bi, (t0, n) in enumerate(split_blocks(0, 2048, 512)):
            b, pap = rstd_psum(nb, lambda c, t0=t0, n=n: XR.ap[:, c, t0:t0 + n],
                               lambda c, t0=t0, n=n: xts(c, t0, n), n)
            ob = OB[bi % 2]
            for c in range(8):
                pg.op("dve", lambda e, c=c, ob=ob, pap=pap, t0=t0, n=n: e.scalar_tensor_tensor(
                    out=ob.ap[:, c, 0:n], in0=XR.ap[:, c, t0:t0 + n], scalar=FINALG.ap[:, c:c + 1], in1=pap,
                    op0=ALU.mult, op1=ALU.mult),
                    r=xts(c, t0, n) + [ps_t[b], FINALG.t()], w=[ob.t(c)])
            pg.dma("sp", lambda e, ob=ob, t0=t0, n=n: e.dma_start(out=yv[:, :, t0:t0 + n], in_=ob.ap[:, :, 0:n]),
                   r=ob.ts, w=[])

    layers = list(layers)
    ph_mod = Phase(AR, PH0, PHLIM)
    l0 = layers[0]
    WM0 = ph_mod.ring(6, [8, 512], BF16, "WM0")
    run_gen(mod_part_gen(l0, 0, WM0, 512), None)
    if not load_state:
        src = dr["xin"].rearrange("(c p) t -> p c t", p=128)
        for (t0, n) in late_x:
            dma_w(XR.ap[:, :, t0:t0 + n], src[:, :, t0:t0 + n],
                  [xr_t[c][j] for c in range(8) for j in range(t0 // 128, (t0 + n) // 128)], eng="sp",
                  r=[MODRT[l0 % 2][0]])
    if LAYER_CFG[l0]["kind"] == "even":
        first_layer_part1[0] = l0
    else:
        run_gen(mod_part_gen(l0, 1, WM0, 512), None)
    for li, l in enumerate(layers):
        nxt = layers[li + 1] if li + 1 < len(layers) else None
        if LAYER_CFG[l]["kind"] == "even":
            emit_even(l)
        else:
            emit_odd(l)
        fuse = do_final and l == 3 and nxt is None
        emit_ffn(l, nxt, fuse_final=fuse)
    if do_final and not (layers[-1] == 3):
        emit_final()
    if store_state:
        dst = xst_out.rearrange("(c p) t -> p c t", p=128)
        for (t0, n) in split_blocks(0, NXR, 512):
            pg.dma("sp", lambda e, t0=t0, n=n: e.dma_start(out=dst[:, :, t0:t0 + n], in_=XR.ap[:, :, t0:t0 + n]),
                   r=[xr_t[c][j] for c in range(8) for j in range(t0 // 128, (t0 + n) // 128)], w=[])
    out_dmas = [ins for ins in pg.q["sp"] if ins.dma]
    tail = out_dmas[-DMA_RING:]
    fin = Inst()
    fin.eng = "sp"
    fin.fn = None
    fin.dma = False
    fin.sig = False
    fin.key = None
    fin.val = 0
    fin.deps = list(tail)
    pg.q["sp"].append(fin)

    pg.finalize()

    keys = set()
    for e, lst in pg.q.items():
        for ins in lst:
            if ins.key is not None and (ins.dma or ins.sig):
                keys.add(ins.key)
            for k, v in ins.waits:
                keys.add(k)
    sems = {}
    for k in sorted(keys, key=str):
        sems[k] = es.enter_context(nc.semaphore("s_" + "_".join(str(x) for x in k)))

    def emit(engname, e):
        for ins in pg.q[engname]:
            for key, val in ins.waits:
                e.wait_ge(sems[key], val)
            if ins.fn is None:
                continue
            r = ins.fn(e)
            if ins.dma:
                r.then_inc(sems[ins.key], 16)
            elif ins.sig:
                r.then_inc(sems[ins.key], 1)

    with nc.Block() as block:
        @block.tensor
        def _(e):
            emit("pe", e)

        @block.scalar
        def _(e):
            emit("act", e)

        @block.vector
        def _(e):
            emit("dve", e)

        @block.gpsimd
        def _(e):
            emit("pool", e)

        @block.sync
        def _(e):
            emit("sp", e)
    es.close()
    return nc


NEG_FILL = -30000.0


def _bias_tables(rpb_l, flipped):
    def table(jq, ktiles):
        out = np.full((128, 8, len(ktiles), 128), NEG_FILL, np.float32)
        kk = np.arange(128)
        qq = np.arange(128)
        for ki, kt in enumerate(ktiles):
            kr_l = 2 * kt + kk // 64
            kc_l = kk % 64
            qr_l = 2 * jq + qq // 64
            qc_l = qq % 64
            if flipped:
                kr, kc_, qr, qc = 63 - kr_l, 63 - kc_l, 63 - qr_l, 63 - qc_l
            else:
                kr, kc_, qr, qc = kr_l, kc_l, qr_l, qc_l
            rs = np.clip(qr - 4, 0, 56)
            cs = np.clip(qc - 8, 0, 48)
            KR_, QR_ = np.meshgrid(kr, qr, indexing="ij")
            KC_, QC_ = np.meshgrid(kc_, qc, indexing="ij")
            RS_ = np.broadcast_to(rs[None, :], KR_.shape)
            CS_ = np.broadcast_to(cs[None, :], KR_.shape)
            valid = (KR_ >= RS_) & (KR_ < RS_ + 8) & (KC_ >= CS_) & (KC_ < CS_ + 16) & (KR_ >= 0) & (KR_ < 64)
            drr = np.clip(KR_ - QR_ + 7, 0, 14)
            dcc = np.clip(KC_ - QC_ + 15, 0, 30)
            vals = rpb_l[:, drr, dcc]
            blk = np.where(valid[None], vals, np.float32(NEG_FILL)).astype(np.float32)
            out[:, :, ki, :] = np.transpose(blk, (1, 0, 2))
        return out

    tint = table(10, [8, 9, 10, 11, 12]).reshape(128, 8 * 5 * 128)
    tb = np.stack([table(0, [0, 1, 2, 3]).reshape(128, 8 * 4 * 128),
                   table(1, [0, 1, 2, 3]).reshape(128, 8 * 4 * 128)], axis=0)
    return tint, tb


def _pc(v, nchunk):
    return np.ascontiguousarray(v.reshape(nchunk, 128).T)


def prep_inputs(inp):
    f = lambda a: np.ascontiguousarray(np.asarray(a, dtype=np.float32))
    x = f(inp["x"])
    c = f(inp["c"])
    ctx = f(inp["ctx"])
    c_ctx = f(inp["c_ctx"])
    shared = {
        "wmod": f(inp["w_mod"]),
        "bmod": np.stack([_pc(f(inp["b_mod"])[l], 48) for l in range(4)]),
        "normg": np.stack([np.concatenate([_pc(f(inp["norm_g"])[l, 0], 8), _pc(f(inp["norm_g"])[l, 1], 8)], axis=1)
                           for l in range(4)]),
        "finalg": _pc(f(inp["final_g"]), 8),
        "win": f(inp["w_in"]),
        "wout": f(inp["w_out"]),
        "lnvg": np.stack([_pc(f(inp["ln_v_g"])[i], 4) for i in range(2)]),
        "wpw1": f(inp["w_pw1"]),
        "bpw1": np.stack([_pc(f(inp["b_pw1"])[i], 16) for i in range(2)]),
        "bdw": np.stack([_pc(f(inp["b_dw"])[i], 8) for i in range(2)]),
        "lncg": np.stack([_pc(f(inp["ln_c_g"])[i], 8) for i in range(2)]),
        "lncb": np.stack([_pc(f(inp["ln_c_b"])[i], 8) for i in range(2)]),
        "wpw2": f(inp["w_pw2"]),
        "bpw2": f(inp["b_pw2"]).reshape(2, 1, D),
        "wff1": f(inp["w_ff1"]),
        "wff2": f(inp["w_ff2"]),
        "ident": np.eye(128, dtype=np.float32),
    }
    w_sp = f(inp["w_sp"])
    b_sp = f(inp["b_sp"])
    w_dw = f(inp["w_dw"])
    rpb = f(inp["rpb"])
    per_half = []
    for flipped in (False, True):
        wsp = w_sp[:, :, ::-1, ::-1] if flipped else w_sp
        bsp = b_sp[:, :, ::-1] if flipped else b_sp
        wdw = w_dw[:, ::-1, :] if flipped else w_dw
        wspT = np.ascontiguousarray(np.transpose(wsp, (0, 3, 1, 2))).reshape(2, 128, 512)
        bspb = np.ascontiguousarray(np.broadcast_to(bsp.reshape(2, 1, 512), (2, 128, 512)))
        wdwl = np.ascontiguousarray(np.transpose(wdw.reshape(2, 31, 8, 128), (0, 3, 2, 1))).reshape(2, 128, 248)
        tints, tbnds = [], []
        for i in range(2):
            ti, tb = _bias_tables(rpb[i], flipped)
            tints.append(ti)
            tbnds.append(tb)
        per_half.append({"wspT": wspT, "bspb": bspb, "wdw": wdwl,
                         "tint": np.stack(tints), "tbnd": np.stack(tbnds)})
    cores = []
    for core in range(8):
        b, half = core // 2, core % 2
        if half == 0:
            xs = x[b, 0:NWIN]
            cs = ctx[b]
        else:
            xs = x[b, ::-1][0:NWIN]
            cs = ctx[b, ::-1]
        xin = np.ascontiguousarray(np.concatenate([xs, cs], axis=0).T)
        cvec = np.ascontiguousarray(np.stack([_pc(c[b], 8), _pc(c_ctx, 8)], axis=2).reshape(128, 16))
        m = dict(shared)
        m.update(per_half[half])
        m["xin"] = xin
        m["cvec"] = cvec
        cores.append(m)
    return cores


_NC_CACHE = {}


def _get_nc(key, **kw):
    if key not in _NC_CACHE:
        _NC_CACHE[key] = build_program(**kw)
    return _NC_CACHE[key]


def assemble_output(youts):
    out = np.empty((4, 4096, D), np.float32)
    for core in range(8):
        b, half = core // 2, core % 2
        y = np.asarray(youts[core]).T
        if half == 0:
            out[b, 0:2048] = y
        else:
            out[b, 2048:4096] = y[::-1]
    return out


def kernel(**inputs):
    cores = prep_inputs(inputs)
    nc = _get_nc("full", layers=(0, 1, 2, 3), load_state=False, store_state=False, do_final=True)
    res = run_bass_kernel_spmd(nc, cores, core_ids=list(range(8)))
    return assemble_output([r["yout"] for r in res.results])
```
